# Optimizing a Trainium2 kernel written in Bass

```python
import jax, jax.numpy as jnp
from jax import lax
import numpy as np

D_MODEL = 2048
BATCH = 4
SEQ = 2048
DEPTH = 2
DEC_BATCH = 128
DEC_SEQ = 1
PAST_LEN = 16384
PAGE_SIZE = 128

MIX_WIDTH = D_MODEL
N_MIXERS = 4
GROUP_WIDTH = MIX_WIDTH // N_MIXERS
POOL_WINDOWS = (2, 4, 8, 16)
POOL_GROUPS = len(POOL_WINDOWS)
POOL_CH = GROUP_WIDTH // POOL_GROUPS
POOL_BUF = max(POOL_WINDOWS) - 1
CONV_WIDTH = 3
RET_HEADS = 4
RET_HEAD_DIM = GROUP_WIDTH // RET_HEADS
RET_CHUNK = 128
ROPE_BASE = 10000.0
SGU_HEADS = 4
SGU_CH = GROUP_WIDTH // SGU_HEADS
SGU_CHUNK = 128
D_FF = ((8 * D_MODEL + 3 * 256 - 1) // (3 * 256)) * 256
IN_WIDTH = 10 * GROUP_WIDTH
NORM_EPS = 1e-6

kernel_name = 'hybrid_pool_conv_retention_sgu_step'


def rmsnorm(x, g):
    xf = x.astype(jnp.float32)
    y = xf * lax.rsqrt(jnp.mean(xf * xf, axis=-1, keepdims=True) + NORM_EPS)
    return (y * g.astype(jnp.float32)).astype(x.dtype)


def layernorm_f32(x):
    xf = x.astype(jnp.float32)
    mu = jnp.mean(xf, axis=-1, keepdims=True)
    xc = xf - mu
    return xc * lax.rsqrt(jnp.mean(xc * xc, axis=-1, keepdims=True) + NORM_EPS)


def pool_mixer(a, prev, start, w, scale):
    B, L, _ = a.shape
    ext = jnp.concatenate([prev.astype(a.dtype), a], axis=1).astype(jnp.float32)
    cs = jnp.pad(jnp.cumsum(ext, axis=1), ((0, 0), (1, 0), (0, 0)))
    end = cs[:, POOL_BUF + 1:]
    pos = start + jnp.arange(L)
    af = a.astype(jnp.float32)
    outs = []
    for gi, win in enumerate(POOL_WINDOWS):
        sl = slice(gi * POOL_CH, (gi + 1) * POOL_CH)
        lo = POOL_BUF + 1 - win
        wsum = end[..., sl] - cs[:, lo:lo + L, sl]
        cnt = jnp.minimum(pos + 1, win).astype(jnp.float32)[None, :, None]
        d = (wsum / cnt - af[..., sl]).astype(a.dtype)
        outs.append(jnp.einsum('blc,ce->ble', d, w[gi]))
    y = jnp.concatenate(outs, axis=-1) * scale
    return y, ext[:, -POOL_BUF:].astype(a.dtype)


def conv_mixer(bg, cg, hh, prev, w):
    z = cg * hh
    L = z.shape[1]
    ext = jnp.concatenate([prev.astype(z.dtype), z], axis=1)
    acc = ext[:, 0:L] * w[0]
    for kk in range(1, CONV_WIDTH):
        acc = acc + ext[:, kk:kk + L] * w[kk]
    return bg * acc, ext[:, -(CONV_WIDTH - 1):]


def rotary(x, pos):
    half = x.shape[-1] // 2
    inv = ROPE_BASE ** (-jnp.arange(half, dtype=jnp.float32) / half)
    ang = pos.astype(jnp.float32)[:, None] * inv[None, :]
    cos = jnp.cos(ang)[None, :, None, :]
    sin = jnp.sin(ang)[None, :, None, :]
    x1, x2 = x[..., :half], x[..., half:]
    return jnp.concatenate([x1 * cos - x2 * sin, x2 * cos + x1 * sin], axis=-1)


def retention(q, k, v, S0, start):
    B, L, _ = q.shape
    H, d = RET_HEADS, RET_HEAD_DIM
    pos = start + jnp.arange(L)
    f = lambda t: t.astype(jnp.float32).reshape(B, L, H, d)
    q = rotary(f(q), pos)
    k = rotary(f(k), pos) * (d ** -0.5)
    v = f(v)
    c = RET_CHUNK if L % RET_CHUNK == 0 else L
    n = L // c
    chunks = lambda t: t.reshape(B, n, c, H, d).transpose(1, 0, 3, 2, 4)
    lg = jnp.log1p(-(2.0 ** (-5.0 - jnp.arange(H, dtype=jnp.float32))))
    idx = jnp.arange(c, dtype=jnp.float32)
    diff = idx[:, None] - idx[None, :]
    decay_mask = jnp.where(diff >= 0, jnp.exp(jnp.maximum(diff, 0.0)[None] * lg[:, None, None]), 0.0)
    q_decay = jnp.exp((idx + 1.0)[None, :] * lg[:, None])[:, :, None]
    k_decay = jnp.exp((c - 1.0 - idx)[None, :] * lg[:, None])[:, :, None]
    chunk_decay = jnp.exp(c * lg)[:, None, None]

    def step(S, qkv):
        qc, kc, vc = qkv
        scores = jnp.einsum('bhid,bhjd->bhij', qc, kc) * decay_mask
        o = jnp.einsum('bhij,bhje->bhie', scores, vc) + jnp.einsum('bhid,bhde->bhie', qc * q_decay, S)
        S = chunk_decay * S + jnp.einsum('bhjd,bhje->bhde', kc * k_decay, vc)
        return S, o

    S, o = lax.scan(step, S0.astype(jnp.float32), (chunks(q), chunks(k), chunks(v)))
    o = o.transpose(1, 0, 3, 2, 4).reshape(B, L, H, d)
    return o, S


def sgu_mixer(u, v, w_s, b_s, norm_g):
    B, L, _ = u.shape
    vn = layernorm_f32(v) * norm_g.astype(jnp.float32)
    c = L if L <= SGU_CHUNK else SGU_CHUNK
    Lp = -(-L // c) * c
    vp = jnp.pad(vn, ((0, 0), (0, Lp - L), (0, 0))).reshape(B, Lp // c, c, SGU_HEADS, SGU_CH)
    tri = jnp.tril(jnp.ones((c, c), dtype=bool))
    w = jnp.where(tri[None], w_s[:, :c, :c].astype(jnp.float32), 0.0)
    bias = b_s[:, :c].astype(jnp.float32).T[None, None, :, :, None]
    mixed = jnp.einsum('hts,bnshe->bnthe', w, vp) + bias
    mixed = mixed.reshape(B, Lp, GROUP_WIDTH)[:, :L]
    y = (u.astype(jnp.float32) * mixed).astype(u.dtype)
    n_open = (L - 1) % SGU_CHUNK + 1
    return y, vn[:, L - n_open:].astype(u.dtype)


def decoder_layer(h, start, pool_prev, conv_prev, ret_prev,
                  norm1_g, w_in, pool_w, pool_scale, conv_w, ret_norm_g,
                  sgu_norm_g, sgu_w, sgu_b, w_out, norm2_g, w_gate_up, w_down):
    B, L, _ = h.shape
    xn = rmsnorm(h, norm1_g)
    p = jnp.einsum('bld,de->ble', xn, w_in)
    a, bg, cg, hh, q, k, v, g, u, vv = jnp.split(p, 10, axis=-1)
    y_a, pool_new = pool_mixer(a, pool_prev, start, pool_w, pool_scale)
    y_b, conv_new = conv_mixer(bg, cg, hh, conv_prev, conv_w)
    o, ret_new = retention(q, k, v, ret_prev, start)
    on = layernorm_f32(o).reshape(B, L, GROUP_WIDTH) * ret_norm_g.astype(jnp.float32)
    y_c = (jax.nn.silu(g.astype(jnp.float32)) * on).astype(h.dtype)
    y_d, sgu_rows = sgu_mixer(u, vv, sgu_w, sgu_b, sgu_norm_g)
    mix = jnp.concatenate([y_a, y_b, y_c, y_d], axis=-1)
    h = h + jnp.einsum('blm,md->bld', mix, w_out)
    hn = rmsnorm(h, norm2_g)
    gt, up = jnp.split(jnp.einsum('bld,df->blf', hn, w_gate_up), 2, axis=-1)
    h = h + jnp.einsum('blf,fd->bld', jax.nn.silu(gt) * up, w_down)
    return h, pool_new, conv_new, ret_new, sgu_rows


def setup_inputs(seed: int = 0) -> dict:
    key = jax.random.key(seed)
    ks = jax.random.split(key, 20)
    f32 = jnp.float32
    nrm = lambda k, shape, s: jax.random.normal(k, shape, f32) * s
    return {
        'x_prompt': nrm(ks[0], (BATCH, SEQ, D_MODEL), 1.0),
        'x_sample': nrm(ks[1], (DEC_BATCH, DEC_SEQ, D_MODEL), 1.0),
        'state_pool': nrm(ks[2], (DEPTH, DEC_BATCH, POOL_BUF, GROUP_WIDTH), 1.0),
        'state_conv': nrm(ks[3], (DEPTH, DEC_BATCH, CONV_WIDTH - 1, GROUP_WIDTH), 1.0),
        'state_ret': nrm(ks[4], (DEPTH, DEC_BATCH, RET_HEADS, RET_HEAD_DIM, RET_HEAD_DIM), 0.5),
        'norm1_g': 1.0 + nrm(ks[5], (DEPTH, D_MODEL), 0.02),
        'w_in': nrm(ks[6], (DEPTH, D_MODEL, IN_WIDTH), D_MODEL ** -0.5),
        'pool_w': nrm(ks[7], (DEPTH, POOL_GROUPS, POOL_CH, POOL_CH), POOL_CH ** -0.5),
        'pool_scale': 1.0 + nrm(ks[8], (DEPTH, GROUP_WIDTH), 0.02),
        'conv_w': nrm(ks[9], (DEPTH, CONV_WIDTH, GROUP_WIDTH), CONV_WIDTH ** -0.5),
        'ret_norm_g': 1.0 + nrm(ks[10], (DEPTH, GROUP_WIDTH), 0.02),
        'sgu_norm_g': 1.0 + nrm(ks[11], (DEPTH, GROUP_WIDTH), 0.02),
        'sgu_w': nrm(ks[12], (DEPTH, SGU_HEADS, SGU_CHUNK, SGU_CHUNK), SGU_CHUNK ** -0.5),
        'sgu_b': 1.0 + nrm(ks[13], (DEPTH, SGU_HEADS, SGU_CHUNK), 0.02),
        'w_out': nrm(ks[14], (DEPTH, MIX_WIDTH, D_MODEL), MIX_WIDTH ** -0.5),
        'norm2_g': 1.0 + nrm(ks[15], (DEPTH, D_MODEL), 0.02),
        'w_gate_up': nrm(ks[16], (DEPTH, D_MODEL, 2 * D_FF), D_MODEL ** -0.5),
        'w_down': nrm(ks[17], (DEPTH, D_FF, D_MODEL), D_FF ** -0.5),
        'final_norm_g': 1.0 + nrm(ks[18], (D_MODEL,), 0.02),
    }


def reference(x_prompt, x_sample, state_pool, state_conv, state_ret,
              norm1_g, w_in, pool_w, pool_scale, conv_w, ret_norm_g,
              sgu_norm_g, sgu_w, sgu_b, w_out, norm2_g, w_gate_up, w_down, final_norm_g):
    hp, hs = x_prompt, x_sample
    Bp = x_prompt.shape[0]
    pool_p, pool_s, conv_p, conv_s, ret_p, ret_s, sgu_p, sgu_s = [], [], [], [], [], [], [], []
    for l in range(DEPTH):
        lp = (norm1_g[l], w_in[l], pool_w[l], pool_scale[l], conv_w[l], ret_norm_g[l],
              sgu_norm_g[l], sgu_w[l], sgu_b[l], w_out[l], norm2_g[l], w_gate_up[l], w_down[l])
        hp, a1, b1, c1, d1 = decoder_layer(
            hp, 0,
            jnp.zeros((Bp, POOL_BUF, GROUP_WIDTH), x_prompt.dtype),
            jnp.zeros((Bp, CONV_WIDTH - 1, GROUP_WIDTH), x_prompt.dtype),
            jnp.zeros((Bp, RET_HEADS, RET_HEAD_DIM, RET_HEAD_DIM), jnp.float32),
            *lp)
        hs, a2, b2, c2, d2 = decoder_layer(hs, PAST_LEN, state_pool[l], state_conv[l], state_ret[l], *lp)
        pool_p.append(a1); conv_p.append(b1); ret_p.append(c1); sgu_p.append(d1)
        pool_s.append(a2); conv_s.append(b2); ret_s.append(c2); sgu_s.append(d2)
    y_prompt = rmsnorm(hp, final_norm_g)
    y_sample = rmsnorm(hs, final_norm_g)
    return (y_prompt, y_sample,
            jnp.stack(pool_p), jnp.stack(pool_s),
            jnp.stack(conv_p), jnp.stack(conv_s),
            jnp.stack(ret_p), jnp.stack(ret_s),
            jnp.stack(sgu_p), jnp.stack(sgu_s))
```

```python
import numpy as np
from contextlib import ExitStack
import concourse.bass as bass
import concourse.mybir as mybir
from concourse.bass_utils import run_bass_kernel_spmd

F32 = mybir.dt.float32
BF16 = mybir.dt.bfloat16
ALU = mybir.AluOpType
AF = mybir.ActivationFunctionType
AX = mybir.AxisListType

D = 2048
GW = 512
NL = 2
FF = 5632
NF = FF // 128
SEQ = 2048
G = 512
NG = SEQ // G
NS = 32
GT = G + NS
WORK = [0, 2, 4, 6]
EPS = 1e-6
PAST = 16384
GAM = [1.0 - 2.0 ** (-5.0 - h) for h in range(4)]
WINS = (2, 4, 8, 16)


class Res:
    __slots__ = ("name", "w", "r", "excl")

    def __init__(self, name, excl=False):
        self.name = name
        self.w = None
        self.r = {}
        self.excl = excl


class DSem:
    def __init__(self, nc, es, name):
        self.sem = es.enter_context(nc.semaphore(name))
        self.cnt = 0


class Eng:
    def __init__(self, nc, es, h, name, self_sync=True):
        self.h = h
        self.name = name
        self.sem = es.enter_context(nc.semaphore("sem_" + name))
        self.cnt = 0
        self.seen = {}
        self.self_sync = self_sync
        self.pend = []

    def wait(self, tok):
        if tok is None:
            return
        sem, val = tok
        if sem is self.sem and not self.self_sync:
            return
        k = id(sem)
        if self.seen.get(k, 0) >= val:
            return
        self.h.wait_ge(sem, val)
        self.seen[k] = val

    def _deps(self, reads, writes):
        ws = list(writes) + [r for r in reads if r.excl]
        rs = [r for r in reads if not r.excl]
        for r in rs:
            self.wait(r.w)
        for w in ws:
            self.wait(w.w)
            for tok in list(w.r.values()):
                self.wait(tok)
        return rs, ws

    @staticmethod
    def _apply(tok, rs, ws):
        for r in rs:
            r.r[id(tok[0])] = tok
        for w in ws:
            w.w = tok
            w.r = {}

    def op(self, fn, reads=(), writes=(), inc=True):
        rs, ws = self._deps(reads, writes)
        ins = fn(self.h)
        if inc:
            self.cnt += 1
            ins.then_inc(self.sem, 1)
            tok = (self.sem, self.cnt)
            for (prs, pws) in self.pend:
                self._apply(tok, prs, pws)
            self.pend = []
            self._apply(tok, rs, ws)
        else:
            self.pend.append((rs, ws))

    def dma(self, ds, out, in_, reads=(), writes=(), **kw):
        rs, ws = self._deps(reads, writes)
        ins = self.h.dma_start(out=out, in_=in_, **kw)
        ds.cnt += 16
        ins.then_inc(ds.sem, 16)
        tok = (ds.sem, ds.cnt)
        self._apply(tok, rs, ws)
        return rs, ws


def dma_batch(eng, ds, items, **kw):
    allr, allw = [], []
    for (o, i, rs, ws) in items:
        r2, w2 = eng.dma(ds, o, i, rs, ws, **kw)
        allr += r2
        allw += w2
    tok = (ds.sem, ds.cnt)
    Eng._apply(tok, allr, allw)


def build_program():
    nc = bass.Bass("TRN2", target_bir_lowering=False)

    def din(name, shape):
        return nc.dram_tensor(name, list(shape), F32, kind="ExternalInput").ap()

    def dout(name, shape):
        return nc.dram_tensor(name, list(shape), F32, kind="ExternalOutput").ap()

    xp = din("xp", [SEQ, D])
    xs = din("xs", [NS, D])
    st_pool = din("st_pool", [NL, NS, 15, GW])
    st_conv = din("st_conv", [NL, NS, 2, GW])
    st_ret = din("st_ret", [NL, NS, 4, 128, 128])
    w_in = din("w_in", [NL, D, 10 * GW])
    w_out = din("w_out", [NL, D, D])
    w_gu = din("w_gu", [NL, D, 2 * FF])
    w_dn = din("w_dn", [NL, FF, D])
    pool_w = din("pool_w", [NL, 4, 128, 128])
    gcols_d = din("gcols", [128, 5, 16])
    pscale_d = din("pscale", [128, NL, 4])
    convw_d = din("convw", [128, NL, 3, 4])
    convwb_d = din("convwb", [NS, NL, 3, GW])
    retg_d = din("retg", [128, NL, GW])
    sgug_d = din("sgug", [128, NL, GW])
    sguwT_d = din("sguwT", [128, NL, 4, 128])
    sgub_d = din("sgub", [128, NL, 4, 128])
    sgu00_d = din("sgu00", [NS, NL, 2, 4])
    cos_d = din("cos_t", [128, 16, 64])
    sin_d = din("sin_t", [128, 16, 64])
    coss_d = din("cos_s", [NS, 64])
    sins_d = din("sin_s", [NS, 64])
    maskT_d = din("maskT", [128, 4, 128])
    qdec_d = din("qdec", [128, 4, 128])
    kdec_d = din("kdec", [128, 4])
    tri_d = din("tri", [128, 128])
    ident_d = din("ident", [128, 128])
    invc0_d = din("invc0", [128, 4, 16])
    eye16_d = din("eye16", [NS, NS])

    y_p = dout("y_p", [SEQ, D])
    y_s = dout("y_s", [NS, D])
    pool_p = dout("pool_p", [NL, 15, GW])
    pool_s = dout("pool_s", [NL, NS, 15, GW])
    conv_p = dout("conv_p", [NL, 2, GW])
    conv_s = dout("conv_s", [NL, NS, 2, GW])
    ret_p = dout("ret_p", [NL, 4, 128, 128])
    ret_s = dout("ret_s", [NL, NS, 4, 128, 128])
    sgu_p = dout("sgu_p", [NL, 128, GW])
    sgu_s = dout("sgu_s", [NL, NS, GW])

    with ExitStack() as es:
        def sb(name, shape, dt=F32):
            return es.enter_context(nc.sbuf_tensor("sb_" + name, list(shape), dt))

        def psum(name, shape, dt=F32):
            return es.enter_context(nc.psum_tensor("ps_" + name, list(shape), dt))

        pe = Eng(nc, es, nc.tensor, "pe", self_sync=False)
        dve = Eng(nc, es, nc.vector, "dve")
        act = Eng(nc, es, nc.scalar, "act")
        pool = Eng(nc, es, nc.gpsimd, "pool")
        sp = Eng(nc, es, nc.sync, "sp")
        dsems = []

        def mkds(name):
            d = DSem(nc, es, name)
            dsems.append(d)
            return d

        hT = sb("hT", [128, 16, GT])
        xnT = sb("xnT", [128, 16, GT], BF16)
        mixT = sb("mixT", [128, 16, GT], BF16)
        big = sb("big", [128, NF * GT // 2])
        arena = sb("arena", [128, 8704])

        def carve(base, off, shape, dt=F32):
            nfl = int(np.prod(shape[1:]))
            if dt == BF16:
                v = base[:, off:off + (nfl + 1) // 2].bitcast(BF16)[:, 0:nfl]
                used = (nfl + 1) // 2
            else:
                v = base[:, off:off + nfl]
                used = nfl
            v = v[0:shape[0]]
            if len(shape) == 3:
                v = v.rearrange("p (a b) -> p a b", a=shape[1])
            elif len(shape) == 4:
                v = v.rearrange("p (a b c) -> p a b c", a=shape[1], b=shape[2])
            return v, off + used

        actT = big[:].bitcast(BF16).rearrange("p (j t) -> p j t", t=GT)
        o = 0
        aext, o = carve(big, o, [128, 4, 15 + G])
        zext, o = carve(big, o, [128, 4, 2 + G])
        hh_sb, o = carve(big, o, [128, 4, G])
        u_sb, o = carve(big, o, [128, 4, G], BF16)
        vnb, o = carve(big, o, [128, 4, GW], BF16)
        gs, o = carve(big, o, [128, 4, GW], BF16)
        assert o <= 10688
        pFM, _ = carve(big, 10688, [128, 40, NS])
        sguw_f, _ = carve(big, 0, [128, NL, 4, 128])
        o = 0
        S_half, o = carve(big, o, [128, 8, 4, 128])
        s_cprev, o = carve(big, o, [NS, 2, GW])
        s_convw, o = carve(big, o, [NS, 3, GW])
        s_prev4, o = carve(big, o, [NS, 26, 128])
        assert o <= 10688
        o = 0
        xin, _ = carve(arena, 0, [128, D])
        qrot, o = carve(arena, o, [128, 4, GW], BF16)
        krot, o = carve(arena, o, [128, 4, GW], BF16)
        kd, o = carve(arena, o, [128, 4, GW], BF16)
        vbf, o = carve(arena, o, [128, 4, GW], BF16)
        qT, o = carve(arena, o, [128, 4, 128], BF16)
        qdT, o = carve(arena, o, [128, 4, 128], BF16)
        kT, o = carve(arena, o, [128, 4, 128], BF16)
        scT, o = carve(arena, o, [128, 4, 128], BF16)
        ycb, o = carve(arena, o, [128, GW], BF16)
        tmp4, o = carve(arena, o, [128, 4, 128])
        pt0, o = carve(arena, o, [128, 15 + G])
        pt1, o = carve(arena, o, [128, 15 + G])
        ptmp = [pt0, pt1]
        Sst, o = carve(arena, o, [128, NL, 4, 128])
        Sbf, o = carve(arena, o, [128, NL, 4, 128], BF16)
        carry_a, o = carve(arena, o, [128, NL, 4, 15])
        carry_z, o = carve(arena, o, [128, NL, 4, 2])
        assert o <= 8704, o
        o = 0
        s_a, o = carve(arena, o, [NS, GW])
        s_d, o = carve(arena, o, [NS, GW])
        s_hh, o = carve(arena, o, [NS, GW])
        s_u, o = carve(arena, o, [NS, GW])
        s_q, o = carve(arena, o, [NS, GW])
        s_k, o = carve(arena, o, [NS, GW])
        s_v, o = carve(arena, o, [NS, GW])
        s_t, o = carve(arena, o, [NS, GW])
        s_g, o = carve(arena, o, [NS, GW])
        s_mix, o = carve(arena, o, [NS, 3 * GW])
        s_kdiag0, o = carve(arena, o, [NS, GW])
        s_kdiag = [s_kdiag0, s_kdiag0]
        s_qT, o = carve(arena, o, [128, 4, NS])
        s_oT, o = carve(arena, o, [128, 4, NS])
        s_sc, o = carve(arena, o, [NS, 4])
        assert o <= 6942, o
        NSLOT = 3
        wslot = [sb(f"wslot{i}", [128, 4096], BF16) for i in range(NSLOT)]
        rstd = sb("rstd", [128, G])
        sqb = [sb(f"sqb{i}", [128, G], BF16) for i in range(2)]
        dT = sb("dT", [128, G], BF16)
        dT2 = sb("dT2", [128, G], BF16)
        dTs = [dT, dT2]
        cacc = sb("cacc", [128, G])
        vnf = sb("vnf", [128, GW])
        rt = [sb(f"rt{i}", [128, 4, 64]) for i in range(2)]
        onb = sb("onb", [128, GW])
        stt = sb("stt", [128, 4, 6])
        mv = sb("mv", [128, 4, 2])
        sd = sb("sd", [128, 4])
        nb = sb("nb", [128, 4])
        ident = sb("ident", [128, 128])
        identb = sb("identb", [128, 128], BF16)
        ones_b = sb("ones_b", [128, 128], BF16)
        epsc = sb("epsc", [128, 1])
        gcols = sb("gcols", [128, 5, 16])
        pscale = sb("pscale", [128, NL, 4])
        convw = sb("convw", [128, NL, 3, 4])
        retg = sb("retg", [128, GW])
        sgug = sb("sgug", [128, GW])
        WsT = sb("WsT", [128, NL, 4, 128], BF16)
        sgub = sb("sgub", [128, 4, 128])
        sgu00 = sb("sgu00", [NS, NL, 2, 4])
        poolw = sb("poolw", [128, NL * 4, 128], BF16)
        cos_g = sb("cos_g", [128, 4, 64])
        sin_g = sb("sin_g", [128, 4, 64])
        cos_s = sb("cos_s", [NS, 64])
        sin_s = sb("sin_s", [NS, 64])
        maskT = sb("maskT", [128, 4, 128])
        qdec = sb("qdec", [128, 4, 128])
        kdec = sb("kdec", [128, 4])
        tri = sb("tri", [128, 128])
        invc0 = sb("invc0", [128, 4, 16])
        eye16 = sb("eye16", [NS, NS])

        pm = [psum(f"pm{i}", [128, 512]) for i in range(4)]
        pa = [psum(f"pa{i}", [128, 512]) for i in range(2)]
        pb = [psum(f"pb{i}", [128, 1024], BF16) for i in range(2)]
        R_pm = [Res(f"pm{i}", True) for i in range(4)]
        R_pa = [Res(f"pa{i}", True) for i in range(2)]
        R_pb = [Res(f"pb{i}", True) for i in range(2)]
        rot = {"m": 0, "a": 0, "b": 0}

        mlimit = [4]

        def nxt(kind):
            lst, rl = {"m": (pm, R_pm), "a": (pa, R_pa), "b": (pb, R_pb)}[kind]
            i = rot[kind] % (mlimit[0] if kind == "m" else len(lst))
            rot[kind] += 1
            return lst[i], rl[i]

        R = {}

        def res(name):
            if name not in R:
                R[name] = Res(name)
            return R[name]

        R_h = [res(f"h{k}") for k in range(16)]
        R_xn = [res(f"xn{k}") for k in range(16)]
        R_mix = [res(f"mix{k}") for k in range(16)]
        R_act = [res(f"act{j}") for j in range(NF)]
        R_slot = [res(f"slot{i}") for i in range(NSLOT)]
        slot_ds = [mkds(f"ds_slot{i}") for i in range(NSLOT)]
        slot_rot = [0]

        ds_setup = mkds("ds_setup")
        setup_items = []
        for (t, d) in [(ident, ident_d), (gcols, gcols_d), (pscale, pscale_d), (convw, convw_d),
                       (sguw_f, sguwT_d), (sgu00, sgu00_d), (cos_s, coss_d), (sin_s, sins_d), (maskT, maskT_d), (qdec, qdec_d),
                       (kdec, kdec_d), (tri, tri_d), (invc0, invc0_d), (eye16, eye16_d)]:
            setup_items.append((t if t is sguw_f else t[:], d, [], [res("consts")]))
        dma_batch(sp, ds_setup, setup_items)
        ds_setup2 = mkds("ds_setup2")
        dma_batch(pool, ds_setup2, [
            (identb[:], ident_d, [], [res("consts2")]),
            (poolw[:], pool_w.rearrange("l g c e -> c (l g) e"), [], [res("consts2")]),
        ])
        RC = [res("consts"), res("consts2"), res("consts3")]
        dve.op(lambda e: e.memset(ones_b[:], 1.0), [], [res("consts3")])
        dve.op(lambda e: e.memset(epsc[:], EPS), [], [res("consts3")])
        dve.op(lambda e: e.memset(Sst[:], 0.0), [], [res("S0"), res("S1")])
        dve.op(lambda e: e.memset(Sbf[:], 0.0), [], [res("Sbf0"), res("Sbf1")])
        dve.op(lambda e: e.memset(carry_a[:], 0.0), [], [res("ca0"), res("ca1")])
        dve.op(lambda e: e.memset(carry_z[:], 0.0), [], [res("cz0"), res("cz1")])
        dve.op(lambda e: e.tensor_tensor(WsT[:].rearrange("p l h t -> p (l h) t"),
                                         sguw_f.rearrange("p l h t -> p (l h) t"),
                                         tri[:].unsqueeze(1).to_broadcast([128, NL * 4, 128]), ALU.mult),
               RC, [res("consts3")])

        def load_w(view_shape, src):
            i = slot_rot[0] % NSLOT
            slot_rot[0] += 1
            n = int(np.prod(view_shape[1:]))
            v = wslot[i][:, 0:n]
            if len(view_shape) == 3:
                v = v.rearrange("p (a b) -> p a b", a=view_shape[1])
            elif len(view_shape) == 4:
                v = v.rearrange("p (a b c) -> p a b c", a=view_shape[1], b=view_shape[2])
            pool.dma(slot_ds[i], v, src, [], [R_slot[i]])
            return v, R_slot[i]

        def rmsnorm(n, gidx, out_xn=True, c0=0):
            cs = slice(c0, c0 + n)
            ss, Rss = nxt("a")
            for k in range(16):
                s = sqb[k % 2]
                Rs = res(f"sqb{k % 2}")
                act.op(lambda e, s=s, k=k: e.activation(s[:, :n], hT[:, k, cs], AF.Square), [R_h[k]], [Rs])
                pe.op(lambda e, s=s, k=k: e.matmul(ss[:, :n], ones_b[:], s[:, :n], start=(k == 0), stop=(k == 15)),
                      [Rs, res("consts3")], [Rss], inc=True)
            act.op(lambda e: e.activation(rstd[:, :n], ss[:, :n], AF.Sqrt, bias=epsc[:], scale=1.0 / D),
                   [Rss, res("consts3")], [res("rstd")])
            dve.op(lambda e: e.reciprocal(rstd[:, :n], rstd[:, :n]), [], [res("rstd")])
            for k in range(16):
                if out_xn:
                    dve.op(lambda e, k=k: e.scalar_tensor_tensor(xnT[:, k, cs], hT[:, k, cs], gcols[:, gidx, k:k + 1],
                                                                 rstd[:, :n], ALU.mult, ALU.mult),
                           [R_h[k], res("rstd"), res("consts")], [R_xn[k]])
                else:
                    dve.op(lambda e, k=k: e.scalar_tensor_tensor(hT[:, k, cs], hT[:, k, cs], gcols[:, gidx, k:k + 1],
                                                                 rstd[:, :n], ALU.mult, ALU.mult),
                           [res("rstd"), res("consts")], [R_h[k]])

        def load_ltab(l):
            dma_batch(sp, ds_lt, [(retg[:], retg_d[:, l, :], [], [res("ltab")]),
                                  (sgug[:], sgug_d[:, l, :], [], [res("ltab")]),
                                  (sgub[:], sgub_d[:, l, :, :], [], [res("ltab")])])

        def load_x(n, src_rows):
            for (src, nt, c0) in src_rows:
                sp.dma(ds_x, xin[:nt, :], src, [], [res("xin")])
                for k4 in range(4):
                    pt, Rpt = nxt("a")
                    for kk in range(4):
                        k = k4 * 4 + kk
                        pe.op(lambda e, k=k, kk=kk: e.transpose(pt[:, kk * 128:kk * 128 + nt], xin[:nt, k * 128:(k + 1) * 128],
                                                                ident[:nt, :nt]),
                              [res("xin"), res("consts")], [Rpt], inc=(kk == 3))
                    act.op(lambda e, k4=k4: e.copy(hT[:, k4 * 4:k4 * 4 + 4, c0:c0 + nt],
                                                   pt[:, 0:512].rearrange("p (a b) -> p a b", a=4)[:, :, :nt]),
                           [Rpt], [R_h[k4 * 4 + kk] for kk in range(4)])

        def store_y(n, dst_rows):
            for (dst, nt, c0) in dst_rows:
                for k4 in range(4):
                    pt, Rpt = nxt("a")
                    for kk in range(4):
                        k = k4 * 4 + kk
                        pe.op(lambda e, k=k, kk=kk: e.transpose(pt[:nt, kk * 128:(kk + 1) * 128], hT[:, k, c0:c0 + nt], ident[:]),
                              [R_h[k], res("consts")], [Rpt], inc=(kk == 3))
                    act.op(lambda e, k4=k4: e.copy(xin[:nt, k4 * 512:(k4 + 1) * 512], pt[:nt, :]), [Rpt], [res("xin")])
                sp.dma(ds_x, dst, xin[:nt, :], [res("xin")], [])

        def wout_and_ffn(l, n, ns=0):
            c1 = G
            if ns:
                pQ, RpQ = nxt("a")
            for dp in range(8):
                wv, Rw = load_w([128, 16, 256], w_out[l, :, dp * 256:(dp + 1) * 256].rearrange("(k p) c -> p k c", p=128))
                for dd in range(2):
                    d = dp * 2 + dd
                    ps, Rps = nxt("m")
                    for k in range(16):
                        pe.op(lambda e, k=k, dd=dd: e.matmul(ps[:, :n], wv[:, k, dd * 128:(dd + 1) * 128], mixT[:, k, :n],
                                                             start=(k == 0), stop=(k == 15)),
                              [Rw, R_mix[k]], [Rps], inc=(k == 15))
                    dve.op(lambda e, d=d: e.tensor_tensor(hT[:, d, :n], hT[:, d, :n], ps[:, :n], ALU.add), [Rps], [R_h[d]])
                    if ns:
                        for k in range(16):
                            pe.op(lambda e, k=k, dd=dd, d=d: e.matmul(pQ[:, d * ns:(d + 1) * ns], wv[:, k, dd * 128:(dd + 1) * 128],
                                                                     mixT[:, k, c1:c1 + ns], start=(k == 0), stop=(k == 15)),
                                  [Rw, R_mix[k]], [RpQ], inc=(k == 15))
            if ns:
                dve.op(lambda e: e.tensor_tensor(hT[:, :, c1:c1 + ns], hT[:, :, c1:c1 + ns],
                                                 pQ[:, 0:16 * ns].rearrange("p (a b) -> p a b", a=16), ALU.add), [RpQ], R_h)
            rmsnorm(n, 2 * l + 1)
            if ns:
                rmsnorm(ns, 2 * l + 1, c0=c1)
            for jp in range(NF // 2):
                wg, Rwg = load_w([128, 16, 256], w_gu[l, :, jp * 256:(jp + 1) * 256].rearrange("(k p) c -> p k c", p=128))
                wu, Rwu = load_w([128, 16, 256], w_gu[l, :, FF + jp * 256:FF + (jp + 1) * 256].rearrange("(k p) c -> p k c", p=128))
                pgs = [nxt("m"), nxt("m")]
                for jj in range(2):
                    pg, Rpg = pgs[jj]
                    for k in range(16):
                        pe.op(lambda e, k=k, jj=jj, pg=pg: e.matmul(pg[:, :n], wg[:, k, jj * 128:(jj + 1) * 128], xnT[:, k, :n],
                                                                  start=(k == 0), stop=(k == 15)),
                              [Rwg, R_xn[k]], [Rpg], inc=(k == 15))
                pus = [nxt("m"), nxt("m")]
                for jj in range(2):
                    pu, Rpu = pus[jj]
                    for k in range(16):
                        pe.op(lambda e, k=k, jj=jj, pu=pu: e.matmul(pu[:, :n], wu[:, k, jj * 128:(jj + 1) * 128], xnT[:, k, :n],
                                                                  start=(k == 0), stop=(k == 15)),
                              [Rwu, R_xn[k]], [Rpu], inc=(k == 15))
                for jj in range(2):
                    j = jp * 2 + jj
                    pg, Rpg = pgs[jj]
                    pu, Rpu = pus[jj]
                    sgb, Rsg = (cacc, res("cacc")) if jj == 0 else (rstd, res("rstd"))
                    act.op(lambda e, pg=pg, sgb=sgb: e.activation(sgb[:, :n], pg[:, :n], AF.Silu), [Rpg], [Rsg])
                    dve.op(lambda e, j=j, pu=pu, sgb=sgb: e.tensor_tensor(actT[:, j, :n], pu[:, :n], sgb[:, :n], ALU.mult),
                           [Rpu, Rsg], [R_act[j]])
                if ns:
                    for jj in range(2):
                        j = jp * 2 + jj
                        q16 = j % 16
                        if q16 == 0:
                            (pgS, RpgS), (puS, RpuS) = nxt("a"), nxt("a")
                            sstate["g"] = (pgS, RpgS, puS, RpuS)
                        pgS, RpgS, puS, RpuS = sstate["g"]
                        for k in range(16):
                            pe.op(lambda e, k=k, jj=jj, q16=q16, pgS=pgS: e.matmul(pgS[:, q16 * ns:(q16 + 1) * ns], wg[:, k, jj * 128:(jj + 1) * 128],
                                                                                 xnT[:, k, c1:c1 + ns], start=(k == 0), stop=(k == 15)),
                                  [Rwg, R_xn[k]], [RpgS], inc=(k == 15))
                        for k in range(16):
                            pe.op(lambda e, k=k, jj=jj, q16=q16, puS=puS: e.matmul(puS[:, q16 * ns:(q16 + 1) * ns], wu[:, k, jj * 128:(jj + 1) * 128],
                                                                                 xnT[:, k, c1:c1 + ns], start=(k == 0), stop=(k == 15)),
                                  [Rwu, R_xn[k]], [RpuS], inc=(k == 15))
                        if q16 == 15 or j == NF - 1:
                            cnt = q16 + 1
                            j0 = j - q16
                            act.op(lambda e, pgS=pgS, cnt=cnt: e.activation(vnf[:, :cnt * ns], pgS[:, :cnt * ns], AF.Silu), [RpgS], [res("lnout")])
                            dve.op(lambda e, puS=puS, cnt=cnt, j0=j0: e.tensor_tensor(
                                actT[:, j0:j0 + cnt, c1:c1 + ns], puS[:, :cnt * ns].rearrange("p (a b) -> p a b", a=cnt),
                                vnf[:, :cnt * ns].rearrange("p (a b) -> p a b", a=cnt), ALU.mult),
                                [RpuS, res("lnout")], [R_act[jx] for jx in range(j0, j0 + cnt)])
            fsegs = [(0, 16), (16, 32), (32, 44)]
            if ns:
                pQd = [nxt("a"), nxt("a")]
            for dp in range(8):
                psd = [nxt("m"), nxt("m")]
                for si, (f0, f1) in enumerate(fsegs):
                    nf = f1 - f0
                    src = w_dn[l, f0 * 128:f1 * 128, dp * 256:(dp + 1) * 256].rearrange("(j p) c -> p j c", p=128)
                    wv, Rw = load_w([128, nf, 256], src)
                    for dd in range(2):
                        ps, Rps = psd[dd]
                        for jj in range(nf):
                            j = f0 + jj
                            pe.op(lambda e, jj=jj, j=j, dd=dd, ps=ps: e.matmul(ps[:, :n], wv[:, jj, dd * 128:(dd + 1) * 128],
                                                                             actT[:, j, :n], start=(j == 0), stop=(j == NF - 1)),
                                  [Rw, R_act[j]], [Rps], inc=(jj == nf - 1))
                    if ns:
                        for dd in range(2):
                            pq_, Rpq_ = pQd[dd]
                            for jj in range(nf):
                                j = f0 + jj
                                pe.op(lambda e, jj=jj, j=j, dd=dd, pq_=pq_, dp=dp: e.matmul(
                                    pq_[:, dp * ns:(dp + 1) * ns], wv[:, jj, dd * 128:(dd + 1) * 128], actT[:, j, c1:c1 + ns],
                                    start=(j == 0), stop=(j == NF - 1)),
                                    [Rw, R_act[j]], [Rpq_], inc=(jj == nf - 1))
                for dd in range(2):
                    d = dp * 2 + dd
                    ps, Rps = psd[dd]
                    dve.op(lambda e, d=d, ps=ps: e.tensor_tensor(hT[:, d, :n], hT[:, d, :n], ps[:, :n], ALU.add), [Rps], [R_h[d]])
            if ns:
                for dd in range(2):
                    pq_, Rpq_ = pQd[dd]
                    for dp in range(8):
                        d = dp * 2 + dd
                        dve.op(lambda e, d=d, dp=dp, pq_=pq_: e.tensor_tensor(hT[:, d, c1:c1 + ns], hT[:, d, c1:c1 + ns],
                                                                            pq_[:, dp * ns:(dp + 1) * ns], ALU.add), [Rpq_], [R_h[d]])

        sstate = {}

        def layernorm_heads(src_ps, npart, Rsrc, dst, nheads=4, width=128):
            for h in range(nheads):
                dve.op(lambda e, h=h: e.bn_stats(stt[:npart, h, :], src_ps[:npart, h * width:(h + 1) * width]), [Rsrc], [res("stt")])
            for h in range(nheads):
                dve.op(lambda e, h=h: e.bn_aggr(mv[:npart, h, :], stt[:npart, h, :]), [res("stt")], [res("mv")])
            act.op(lambda e: e.activation(sd[:npart, :nheads], mv[:npart, :nheads, 1], AF.Sqrt, bias=epsc[:npart, :], scale=1.0),
                   [res("mv"), res("consts3")], [res("sd")])
            dve.op(lambda e: e.reciprocal(sd[:npart, :nheads], sd[:npart, :nheads]), [], [res("sd")])
            dve.op(lambda e: e.scalar_tensor_tensor(nb[:npart, :nheads], mv[:npart, :nheads, 0], -1.0, sd[:npart, :nheads],
                                                    ALU.mult, ALU.mult), [res("mv"), res("sd")], [res("nb")])
            for h in range(nheads):
                dve.op(lambda e, h=h: e.tensor_scalar(dst[:npart, h * width:(h + 1) * width], src_ps[:npart, h * width:(h + 1) * width],
                                                      sd[:npart, h:h + 1], nb[:npart, h:h + 1], ALU.mult, ALU.add),
                       [Rsrc, res("sd"), res("nb")], [res("lnout")])

        def rotary(dst, src_ps, npart, cosv, sinv, Rsrc, Rdst, Rtab):
            s4 = src_ps[:npart, :].rearrange("p (h t e) -> p h t e", h=4, t=2)
            d4 = dst.rearrange("p (h t e) -> p h t e", h=4, t=2)
            cb = cosv.unsqueeze(1).to_broadcast([npart, 4, 64])
            sbb = sinv.unsqueeze(1).to_broadcast([npart, 4, 64])
            Rr = res("rt")
            dve.op(lambda e: e.tensor_tensor(rt[0][:npart], s4[:, :, 0, :], cb, ALU.mult), [Rsrc, Rtab], [Rr])
            dve.op(lambda e: e.tensor_tensor(rt[1][:npart], s4[:, :, 1, :], sbb, ALU.mult), [Rsrc], [Rr])
            dve.op(lambda e: e.tensor_tensor(d4[:, :, 0, :], rt[0][:npart], rt[1][:npart], ALU.subtract), [Rr], [Rdst])
            dve.op(lambda e: e.tensor_tensor(rt[0][:npart], s4[:, :, 1, :], cb, ALU.mult), [Rsrc], [Rr])
            dve.op(lambda e: e.tensor_tensor(rt[1][:npart], s4[:, :, 0, :], sbb, ALU.mult), [Rsrc], [Rr])
            dve.op(lambda e: e.tensor_tensor(d4[:, :, 1, :], rt[0][:npart], rt[1][:npart], ALU.add), [Rr], [Rdst])

        def prompt_group(g, ws):
            n = G
            c1 = G
            dma_batch(sp, ds_cs, [(cos_g[:], cos_d[:, g * 4:(g + 1) * 4, :], [], [res("cs")]),
                                  (sin_g[:], sin_d[:, g * 4:(g + 1) * 4, :], [], [res("cs")])])
            load_x(n, [(xp[g * G + c * 128:g * G + (c + 1) * 128, :], 128, c * 128) for c in range(4)])
            if ws:
                load_x(NS, [(xs, NS, c1)])
            for l in range(NL):
                load_ltab(l)
                rmsnorm(n, 2 * l)
                if ws:
                    rmsnorm(NS, 2 * l, c0=c1)
                    mlimit[0] = 3
                RS, RSb = res(f"S{l}"), res(f"Sbf{l}")
                Rca, Rcz = res(f"ca{l}"), res(f"cz{l}")

                def fm_block(cb, consume):
                    if ws:
                        pS, RpS = pm[3], R_pm[3]
                    for half in range(2):
                        src = w_in[l, :, cb * GW + half * 256: cb * GW + (half + 1) * 256].rearrange("(k p) c -> p k c", p=128)
                        wv, Rw = load_w([128, 16, 256], src)
                        for ee in range(2):
                            ci = half * 2 + ee
                            ps, Rps = nxt("m")
                            for k in range(16):
                                pe.op(lambda e, k=k, ee=ee: e.matmul(ps[:, :n], wv[:, k, ee * 128:(ee + 1) * 128], xnT[:, k, :n],
                                                                     start=(k == 0), stop=(k == 15)),
                                      [Rw, R_xn[k]], [Rps], inc=(k == 15))
                            consume(ci, ps, Rps)
                            if ws:
                                for k in range(16):
                                    pe.op(lambda e, k=k, ee=ee, ci=ci: e.matmul(pS[:, ci * NS:(ci + 1) * NS], wv[:, k, ee * 128:(ee + 1) * 128],
                                                                              xnT[:, k, c1:c1 + NS], start=(k == 0), stop=(k == 15)),
                                          [Rw, R_xn[k]], [RpS], inc=(k == 15))
                            after_group()
                    if ws:
                        act.op(lambda e: e.copy(pFM[:, cb * 4:(cb + 1) * 4, :], pS[:, 0:4 * NS].rearrange("p (a b) -> p a b", a=4)),
                               [RpS], [res(f"pfm{cb}")])

                def tm_block(cb, consume):
                    wvs = []
                    for half in range(2):
                        src = w_in[l, half * 1024:(half + 1) * 1024, cb * GW:(cb + 1) * GW].rearrange("(k p) c -> p k c", p=128)
                        wvs.append(load_w([128, 8, GW], src))
                    if not ws:
                        for half in range(2):
                            wv, Rw = wvs[half]
                            for c in range(4):
                                ps, Rps = pm[c], R_pm[c]
                                for kk in range(8):
                                    k = half * 8 + kk
                                    pe.op(lambda e, k=k, kk=kk, c=c, wv=wv, ps=ps: e.matmul(ps[:, :], xnT[:, k, c * 128:(c + 1) * 128], wv[:, kk, :],
                                                                                          start=(k == 0), stop=(k == 15)),
                                          [Rw, R_xn[k]], [Rps], inc=(kk == 7))
                                if half == 1:
                                    consume(c, ps, Rps)
                    else:
                        for c in range(4):
                            ps, Rps = nxt("m")
                            for k in range(16):
                                wv, Rw = wvs[k // 8]
                                pe.op(lambda e, k=k, c=c, wv=wv: e.matmul(ps[:, :], xnT[:, k, c * 128:(c + 1) * 128], wv[:, k % 8, :],
                                                                          start=(k == 0), stop=(k == 15)),
                                      [Rw, R_xn[k]], [Rps], inc=(k == 15))
                            consume(c, ps, Rps)
                    if ws:
                        pS, RpS = pm[3], R_pm[3]
                        for eb in range(4):
                            for k in range(16):
                                wv, Rw = wvs[k // 8]
                                pe.op(lambda e, k=k, eb=eb, wv=wv: e.matmul(pS[:, eb * NS:(eb + 1) * NS], wv[:, k % 8, eb * 128:(eb + 1) * 128],
                                                                          xnT[:, k, c1:c1 + NS], start=(k == 0), stop=(k == 15)),
                                      [Rw, R_xn[k]], [RpS], inc=(k == 15))
                        act.op(lambda e: e.copy(pFM[:, cb * 4:(cb + 1) * 4, :], pS[:, 0:4 * NS].rearrange("p (a b) -> p a b", a=4)),
                               [RpS], [res(f"pfm{cb}")])

                hooks = {"gen": None, "deferred": []}

                def after_group():
                    d = hooks["deferred"]
                    hooks["deferred"] = []
                    for f in d:
                        f()
                    if hooks["gen"] is not None:
                        try:
                            next(hooks["gen"])
                        except StopIteration:
                            hooks["gen"] = None

                def c_q(c, ps, Rps):
                    rotary(qrot[:, c, :], ps, 128, cos_g[:, c, :], sin_g[:, c, :], Rps, res(f"qrot{c}"), res("cs"))

                def c_k(c, ps, Rps):
                    rotary(krot[:, c, :], ps, 128, cos_g[:, c, :], sin_g[:, c, :], Rps, res(f"krot{c}"), res("cs"))
                    dve.op(lambda e: e.tensor_tensor(kd[:, c, :].rearrange("p (h e) -> p h e", h=4),
                                                     krot[:, c, :].rearrange("p (h e) -> p h e", h=4),
                                                     kdec[:].unsqueeze(2).to_broadcast([128, 4, 128]), ALU.mult),
                           [res(f"krot{c}"), res("consts")], [res(f"kd{c}")])

                def c_v(c, ps, Rps):
                    act.op(lambda e: e.copy(vbf[:, c, :], ps[:, :]), [Rps], [res(f"vbf{c}")])

                def c_g(c, ps, Rps):
                    act.op(lambda e: e.activation(gs[:, c, :], ps[:, :], AF.Silu), [Rps], [res(f"gs{c}")])

                def c_vv(c, ps, Rps):
                    layernorm_heads(ps, 128, Rps, vnf, nheads=1, width=GW)
                    dve.op(lambda e: e.tensor_tensor(vnf[:, :], vnf[:, :], sgug[:, :], ALU.mult), [res("ltab")], [res("lnout")])
                    act.op(lambda e: e.copy(vnb[:, c, :], vnf[:, :]), [res("lnout")], [res(f"vnb{c}")])
                    if g == NG - 1 and c == 3:
                        sp.dma(ds_sg, sgu_p[l], vnf[:, :], [res("lnout")], [])

                tm_block(4, c_q)
                tm_block(5, c_k)
                tm_block(6, c_v)
                tm_block(7, c_g)
                tm_block(9, c_vv)

                def mixer_gen():
                    for c in range(4):
                        tsl = slice(c * 128, (c + 1) * 128)
                        pq, Rpq = nxt("b")
                        for h in range(4):
                            pe.op(lambda e, h=h: e.transpose(pq[:, h * 128:(h + 1) * 128], qrot[:, c, h * 128:(h + 1) * 128], identb[:]),
                                  [res(f"qrot{c}"), res("consts2")], [Rpq], inc=(h == 3))
                        dve.op(lambda e: e.tensor_copy(qT[:], pq[:, 0:512].rearrange("p (h t) -> p h t", h=4)), [Rpq], [res("qT")])
                        dve.op(lambda e: e.tensor_tensor(qdT[:], pq[:, 0:512].rearrange("p (h t) -> p h t", h=4), qdec[:], ALU.mult),
                               [Rpq, res("consts")], [res("qdT")])
                        pk, Rpk = nxt("b")
                        for h in range(4):
                            pe.op(lambda e, h=h: e.transpose(pk[:, h * 128:(h + 1) * 128], krot[:, c, h * 128:(h + 1) * 128], identb[:]),
                                  [res(f"krot{c}"), res("consts2")], [Rpk], inc=(h == 3))
                        act.op(lambda e: e.copy(kT[:], pk[:, 0:512].rearrange("p (h t) -> p h t", h=4)), [Rpk], [res("kT")])
                        yield
                        psc, Rpsc = nxt("a")
                        for h in range(4):
                            pe.op(lambda e, h=h: e.matmul(psc[:, h * 128:(h + 1) * 128], kT[:, h, :], qT[:, h, :], start=True, stop=True),
                                  [res("kT"), res("qT")], [Rpsc], inc=(h == 3))
                        dve.op(lambda e: e.tensor_tensor(scT[:], psc[:, :].rearrange("p (h t) -> p h t", h=4), maskT[:], ALU.mult),
                               [Rpsc, res("consts")], [res("scT")])
                        pkv, Rpkv = nxt("m")
                        for h in range(4):
                            hs = slice(h * 128, (h + 1) * 128)
                            pe.op(lambda e, h=h, hs=hs: e.matmul(pkv[:, hs], kd[:, c, hs], vbf[:, c, hs], start=True, stop=True),
                                  [res(f"kd{c}"), res(f"vbf{c}")], [Rpkv], inc=(h == 3))
                        yield
                        po, Rpo = nxt("a")
                        for h in range(4):
                            hs = slice(h * 128, (h + 1) * 128)
                            pe.op(lambda e, h=h, hs=hs: e.matmul(po[:, hs], scT[:, h, :], vbf[:, c, hs], start=True, stop=False),
                                  [res("scT"), res(f"vbf{c}")], [Rpo], inc=False)
                            pe.op(lambda e, h=h, hs=hs: e.matmul(po[:, hs], qdT[:, h, :], Sbf[:, l, h, :], start=False, stop=True),
                                  [res("qdT"), RSb], [Rpo], inc=(h == 3))
                        for h in range(4):
                            dve.op(lambda e, h=h: e.scalar_tensor_tensor(Sst[:, l, h, :], Sst[:, l, h, :], GAM[h] ** 128,
                                                                         pkv[:, h * 128:(h + 1) * 128], ALU.mult, ALU.add), [Rpkv], [RS])
                        act.op(lambda e: e.copy(Sbf[:, l, :, :], Sst[:, l, :, :]), [RS], [RSb])
                        layernorm_heads(po, 128, Rpo, onb)
                        dve.op(lambda e: e.tensor_tensor(onb[:, :], onb[:, :], retg[:, :], ALU.mult), [res("ltab")], [res("lnout")])
                        dve.op(lambda e: e.tensor_tensor(ycb[:, :], onb[:, :], gs[:, c, :], ALU.mult), [res("lnout"), res(f"gs{c}")], [res("ycb")])
                        yield
                        yield
                        pyc, Rpyc = nxt("b")
                        for h in range(4):
                            pe.op(lambda e, h=h: e.transpose(pyc[:, h * 128:(h + 1) * 128], ycb[:, h * 128:(h + 1) * 128], identb[:]),
                                  [res("ycb"), res("consts2")], [Rpyc], inc=(h == 3))
                        act.op(lambda e: e.copy(mixT[:, 8:12, tsl], pyc[:, 0:512].rearrange("p (h t) -> p h t", h=4)),
                               [Rpyc], [R_mix[8 + h] for h in range(4)])
                        pmx, Rpmx = nxt("a")
                        for h in range(4):
                            pe.op(lambda e, h=h: e.matmul(pmx[:, h * 128:(h + 1) * 128], vnb[:, c, h * 128:(h + 1) * 128], WsT[:, l, h, :],
                                                          start=True, stop=True), [res(f"vnb{c}"), res("consts3")], [Rpmx], inc=(h == 3))
                        dve.op(lambda e: e.tensor_tensor(tmp4[:], pmx[:, :].rearrange("p (h t) -> p h t", h=4), sgub[:, :, :], ALU.add),
                               [Rpmx, res("ltab")], [res("tmp4")])
                        dve.op(lambda e: e.tensor_tensor(mixT[:, 12:16, tsl], tmp4[:], u_sb[:, :, tsl], ALU.mult),
                               [res("tmp4"), res("u_sb")], [R_mix[12 + h] for h in range(4)])
                        yield

                hooks["gen"] = mixer_gen()

                def c_u(ci, ps, Rps):
                    act.op(lambda e: e.copy(u_sb[:, ci, :n], ps[:, :n]), [Rps], [res("u_sb")])

                fm_block(8, c_u)

                dve.op(lambda e: e.tensor_copy(aext[:, :, 0:15], carry_a[:, l, :, :]), [Rca], [res("aext")])

                def c_a(gi, ps, Rps):
                    act.op(lambda e: e.copy(aext[:, gi, 15:15 + n], ps[:, :n]), [Rps], [res("aext")])
                    cur = aext[:, gi, :]
                    L = 15 + n
                    sh = 1
                    for step in range(gi + 1):
                        o = ptmp[step % 2]
                        dve.op(lambda e, cur=cur, o=o, sh=sh: e.tensor_tensor(o[:, sh:L], cur[:, sh:L], cur[:, 0:L - sh], ALU.add),
                               [res("aext")] if step == 0 else [res(f"ptmp{(step - 1) % 2}")], [res(f"ptmp{step % 2}")])
                        cur = o
                        sh *= 2
                    Rcur = res(f"ptmp{gi % 2}")
                    dTg, RdT = dTs[gi % 2], res(f"dT{gi % 2}")
                    dve.op(lambda e, cur=cur: e.scalar_tensor_tensor(dTg[:, :n], cur[:, 15:15 + n], 1.0 / WINS[gi], aext[:, gi, 15:15 + n],
                                                                     ALU.mult, ALU.subtract), [Rcur, res("aext")], [RdT])
                    if g == 0:
                        dve.op(lambda e, cur=cur: e.tensor_tensor(ptmp[(gi + 1) % 2][:, 0:16], cur[:, 15:31], invc0[:, gi, :], ALU.mult),
                               [Rcur, res("consts")], [res(f"ptmp{(gi + 1) % 2}")])
                        dve.op(lambda e: e.tensor_tensor(dTg[:, 0:16], ptmp[(gi + 1) % 2][:, 0:16], aext[:, gi, 15:31], ALU.subtract),
                               [res(f"ptmp{(gi + 1) % 2}"), res("aext")], [RdT])

                    def pool_mm():
                        py, Rpy = nxt("m")
                        pe.op(lambda e: e.matmul(py[:, :n], poolw[:, l * 4 + gi, :], dTg[:, :n], start=True, stop=True),
                              [RdT, res("consts2")], [Rpy])
                        dve.op(lambda e: e.tensor_scalar(mixT[:, gi, :n], py[:, :n], pscale[:, l, gi:gi + 1], None, ALU.mult),
                               [Rpy, res("consts")], [R_mix[gi]])

                    hooks["deferred"].append(pool_mm)

                fm_block(0, c_a)
                dve.op(lambda e: e.tensor_copy(carry_a[:, l, :, :], aext[:, :, n:n + 15]), [res("aext")], [Rca])
                if g == NG - 1:
                    with nc.allow_non_contiguous_dma(reason="small state transpose"):
                        for gi in range(4):
                            sp.dma(ds_st, pool_p[l, :, gi * 128:(gi + 1) * 128].rearrange("t c -> c t"), carry_a[:, l, gi, :], [Rca], [],
                                   allow_slow_non_contiguous=True)

                dve.op(lambda e: e.tensor_copy(zext[:, :, 0:2], carry_z[:, l, :, :]), [Rcz], [res("zext")])

                def c_hh(ci, ps, Rps):
                    act.op(lambda e: e.copy(hh_sb[:, ci, :n], ps[:, :n]), [Rps], [res(f"hh{ci}")])

                fm_block(3, c_hh)

                def c_cg(ci, ps, Rps):
                    dve.op(lambda e: e.tensor_tensor(zext[:, ci, 2:2 + n], ps[:, :n], hh_sb[:, ci, :n], ALU.mult),
                           [Rps, res(f"hh{ci}")], [res("zext")])

                fm_block(2, c_cg)
                dve.op(lambda e: e.tensor_copy(carry_z[:, l, :, :], zext[:, :, n:n + 2]), [res("zext")], [Rcz])
                if g == NG - 1:
                    with nc.allow_non_contiguous_dma(reason="small state transpose"):
                        for gi in range(4):
                            sp.dma(ds_st, conv_p[l, :, gi * 128:(gi + 1) * 128].rearrange("t c -> c t"), carry_z[:, l, gi, :], [Rcz], [],
                                   allow_slow_non_contiguous=True)

                def c_bg(ci, ps, Rps):
                    dve.op(lambda e: e.tensor_scalar(cacc[:, :n], zext[:, ci, 0:n], convw[:, l, 0, ci:ci + 1], None, ALU.mult),
                           [res("zext"), res("consts")], [res("cacc")])
                    dve.op(lambda e: e.scalar_tensor_tensor(cacc[:, :n], zext[:, ci, 1:1 + n], convw[:, l, 1, ci:ci + 1], cacc[:, :n],
                                                            ALU.mult, ALU.add), [res("zext")], [res("cacc")])
                    dve.op(lambda e: e.scalar_tensor_tensor(cacc[:, :n], zext[:, ci, 2:2 + n], convw[:, l, 2, ci:ci + 1], cacc[:, :n],
                                                            ALU.mult, ALU.add), [res("zext")], [res("cacc")])
                    dve.op(lambda e: e.tensor_tensor(mixT[:, 4 + ci, :n], ps[:, :n], cacc[:, :n], ALU.mult),
                           [Rps, res("cacc")], [R_mix[4 + ci]])

                fm_block(1, c_bg)
                while hooks["gen"] is not None or hooks["deferred"]:
                    after_group()
                if g == NG - 1:
                    sp.dma(ds_rp, ret_p[l].rearrange("h d e -> d h e"), Sst[:, l, :, :], [RS], [])
                mlimit[0] = 4
                if ws:
                    sample_mixers(l)
                wout_and_ffn(l, n, NS if ws else 0)
            rmsnorm(n, 4, out_xn=False)
            if ws:
                rmsnorm(NS, 4, out_xn=False, c0=c1)
            store_y(n, [(y_p[g * G + c * 128:g * G + (c + 1) * 128, :], 128, c * 128) for c in range(4)])
            if ws:
                store_y(NS, [(y_s, NS, c1)])

        def sample_mixers(l):
            n = NS
            roff = (0, 1, 4, 11)
            R_bigp = ([res("aext"), res("zext"), res("u_sb")] + [res(f"hh{i}") for i in range(4)]
                      + [res(f"vnb{i}") for i in range(4)] + [res(f"gs{i}") for i in range(4)])
            dma_batch(sp, ds_sst, [
                (s_cprev, st_conv[l], [], R_act + R_bigp),
                (s_convw, convwb_d[:, l], [], R_act),
            ] + [(s_prev4[:, roff[gi]:roff[gi] + WINS[gi] - 1, :], st_pool[l, :, 16 - WINS[gi]:15, gi * 128:(gi + 1) * 128], [], R_act)
                 for gi in range(4)])
            dma_batch(sp, ds_cp, [
                (pool_s[l, :, 0:14, :], st_pool[l, :, 1:15, :], [], []),
                (conv_s[l, :, 0:1, :], st_conv[l, :, 1:2, :], [], []),
            ])

            def tm_block(cb, consume):
                ps, Rps = nxt("m")
                for eb in range(4):
                    pe.op(lambda e, eb=eb: e.transpose(ps[:NS, eb * 128:(eb + 1) * 128], pFM[:, cb * 4 + eb, :], ident[:]),
                          [res(f"pfm{cb}"), res("consts")], [Rps], inc=(eb == 3))
                consume(ps, Rps)

            def to_mix(src, Rsrc, col0):
                dve.op(lambda e: e.tensor_copy(s_mix[:, col0:col0 + GW], src), [Rsrc], [res("s_mix")])

            def c_a(ps, Rps):
                act.op(lambda e: e.copy(s_a[:, :], ps[:n, :]), [Rps], [res("s_a")])
                sp.dma(ds_sa, pool_s[l, :, 14, :], s_a[:, :], [res("s_a")], [])
                for gi, w in enumerate(WINS):
                    cs = slice(gi * 128, (gi + 1) * 128)
                    if w > 2:
                        dve.op(lambda e, cs=cs, w=w, gi=gi: e.tensor_reduce(
                            s_d[:, cs], s_prev4[:, roff[gi]:roff[gi] + w - 1, :].rearrange("p t c -> p c t"), AX.X, ALU.add),
                            [R_act[0]], [res("s_d")])
                    else:
                        dve.op(lambda e, cs=cs: e.tensor_copy(s_d[:, cs], s_prev4[:, 0, :]), [R_act[0]], [res("s_d")])
                    dve.op(lambda e, cs=cs: e.tensor_tensor(s_d[:, cs], s_d[:, cs], s_a[:, cs], ALU.add), [res("s_a")], [res("s_d")])
                    dve.op(lambda e, cs=cs, w=w: e.scalar_tensor_tensor(s_d[:, cs], s_d[:, cs], 1.0 / w, s_a[:, cs], ALU.mult, ALU.subtract),
                           [res("s_a")], [res("s_d")])
                pt, Rpt = nxt("a")
                for gi in range(4):
                    pe.op(lambda e, gi=gi: e.transpose(pt[:, gi * NS:(gi + 1) * NS], s_d[:, gi * 128:(gi + 1) * 128], ident[:NS, :NS]),
                          [res("s_d"), res("consts")], [Rpt], inc=(gi == 3))
                act.op(lambda e: e.copy(dT[:, 0:4 * NS], pt[:, 0:4 * NS]), [Rpt], [res("dT0")])
                py, Rpy = nxt("a")
                for gi in range(4):
                    pe.op(lambda e, gi=gi: e.matmul(py[:, gi * NS:(gi + 1) * NS], poolw[:, l * 4 + gi, :], dT[:, gi * NS:(gi + 1) * NS],
                                                    start=True, stop=True), [res("dT0"), res("consts2")], [Rpy], inc=(gi == 3))
                for gi in range(4):
                    dve.op(lambda e, gi=gi: e.tensor_scalar(mixT[:, gi, G:G + NS], py[:, gi * NS:(gi + 1) * NS], pscale[:, l, gi:gi + 1], None, ALU.mult),
                           [Rpy, res("consts")], [R_mix[gi]])

            tm_block(0, c_a)

            def c_hh(ps, Rps):
                act.op(lambda e: e.copy(s_hh[:, :], ps[:n, :]), [Rps], [res("s_hh")])

            tm_block(3, c_hh)

            def c_cg(ps, Rps):
                dve.op(lambda e: e.tensor_tensor(s_hh[:, :], ps[:n, :], s_hh[:, :], ALU.mult), [Rps], [res("s_hh")])
                sp.dma(ds_sz, conv_s[l, :, 1, :], s_hh[:, :], [res("s_hh")], [])

            tm_block(2, c_cg)

            def c_bg(ps, Rps):
                dve.op(lambda e: e.tensor_tensor(s_t[:, :], s_cprev[:, 0, :], s_convw[:, 0, :], ALU.mult),
                       [R_act[0]], [res("s_t")])
                dve.op(lambda e: e.tensor_tensor(s_d[:, :], s_cprev[:, 1, :], s_convw[:, 1, :], ALU.mult),
                       [R_act[0]], [res("s_d")])
                dve.op(lambda e: e.tensor_tensor(s_t[:, :], s_t[:, :], s_d[:, :], ALU.add), [res("s_d")], [res("s_t")])
                dve.op(lambda e: e.tensor_tensor(s_d[:, :], s_hh[:, :], s_convw[:, 2, :], ALU.mult), [res("s_hh"), R_act[0]], [res("s_d")])
                dve.op(lambda e: e.tensor_tensor(s_t[:, :], s_t[:, :], s_d[:, :], ALU.add), [res("s_d")], [res("s_t")])
                dve.op(lambda e: e.tensor_tensor(s_mix[:, 0:GW], ps[:n, :], s_t[:, :], ALU.mult), [Rps, res("s_t")], [res("s_mix")])

            tm_block(1, c_bg)

            def c_u(ps, Rps):
                act.op(lambda e: e.copy(s_u[:, :], ps[:n, :]), [Rps], [res("s_u")])

            tm_block(8, c_u)

            def c_q(ps, Rps):
                rotary(s_q[:, :], ps, NS, cos_s[:, :], sin_s[:, :], Rps, res("s_q"), res("consts"))

            def c_k(ps, Rps):
                rotary(s_k[:, :], ps, NS, cos_s[:, :], sin_s[:, :], Rps, res("s_k"), res("consts"))
                dve.op(lambda e: e.tensor_scalar(s_k[:, :], s_k[:, :], 128.0 ** -0.5, None, ALU.mult), [], [res("s_k")])

            def c_v(ps, Rps):
                act.op(lambda e: e.copy(s_v[:, :], ps[:n, :]), [Rps], [res("s_v")])

            def c_g(ps, Rps):
                act.op(lambda e: e.activation(s_g[:, :], ps[:n, :], AF.Silu), [Rps], [res("s_g")])

            def c_vv(ps, Rps):
                layernorm_heads(ps, NS, Rps, vnf, nheads=1, width=GW)
                dve.op(lambda e: e.tensor_tensor(vnf[:n, :], vnf[:n, :], sgug[:NS, :], ALU.mult), [res("ltab")], [res("lnout")])
                sp.dma(ds_sv, sgu_s[l], vnf[:n, :], [res("lnout")], [])
                for h in range(4):
                    hs = slice(h * 128, (h + 1) * 128)
                    dve.op(lambda e, h=h, hs=hs: e.tensor_scalar(s_t[:, hs], vnf[:n, hs], sgu00[:, l, 0, h:h + 1], sgu00[:, l, 1, h:h + 1],
                                                                 ALU.mult, ALU.add), [res("lnout"), res("consts")], [res("s_t")])
                dve.op(lambda e: e.tensor_tensor(s_mix[:, 2 * GW:3 * GW], s_t[:, :], s_u[:, :], ALU.mult),
                       [res("s_t"), res("s_u")], [res("s_mix")])

            tm_block(4, c_q)
            tm_block(5, c_k)
            tm_block(6, c_v)
            tm_block(7, c_g)
            tm_block(9, c_vv)

            dve.op(lambda e: e.tensor_tensor(s_t[:, :], s_q[:, :], s_k[:, :], ALU.mult), [res("s_q"), res("s_k")], [res("s_t")])
            dve.op(lambda e: e.tensor_reduce(s_sc[:, :], s_t[:, :].rearrange("p (h e) -> p h e", h=4), AX.X, ALU.add),
                   [res("s_t")], [res("s_sc")])
            pt, Rpt = nxt("a")
            for h in range(4):
                pe.op(lambda e, h=h: e.transpose(pt[:, h * NS:(h + 1) * NS], s_q[:, h * 128:(h + 1) * 128], ident[:NS, :NS]),
                      [res("s_q"), res("consts")], [Rpt], inc=(h == 3))
            act.op(lambda e: e.copy(s_qT[:], pt[:, 0:4 * NS].rearrange("p (h b) -> p h b", h=4)), [Rpt], [res("s_qT")])
            poS, RpoS = nxt("a")
            RSh = res("S_half")
            for hb in range(NS // 8):
                b0 = hb * 8
                sp.dma(ds_sS, S_half, st_ret[l, b0:b0 + 8].rearrange("b h d e -> d b h e"), [], [RSh] + R_act + R_bigp)
                for bb in range(8):
                    b = b0 + bb
                    for h in range(4):
                        pe.op(lambda e, b=b, bb=bb, h=h: e.matmul(poS[:, h * NS + b:h * NS + b + 1], S_half[:, bb, h, :],
                                                                  s_qT[:, h, b:b + 1], start=True, stop=True),
                              [res("s_qT"), RSh], [RpoS], inc=(bb == 7 and h == 3))
                for bb in range(8):
                    b = b0 + bb
                    kdg = s_kdiag[0]
                    Rk = res("s_kdiag0")
                    dve.op(lambda e, b=b, kdg=kdg: e.tensor_scalar(kdg[:, :], s_k[:, :], eye16[:, b:b + 1], None, ALU.mult),
                           [res("s_k"), res("consts")], [Rk])
                    pkv, Rpkv = nxt("m")
                    for h in range(4):
                        hs = slice(h * 128, (h + 1) * 128)
                        pe.op(lambda e, hs=hs, kdg=kdg: e.matmul(pkv[:, hs], kdg[:, hs], s_v[:, hs], start=True, stop=True),
                              [Rk, res("s_v")], [Rpkv], inc=(h == 3))
                    for h in range(4):
                        dve.op(lambda e, bb=bb, h=h: e.scalar_tensor_tensor(S_half[:, bb, h, :], S_half[:, bb, h, :], GAM[h],
                                                                            pkv[:, h * 128:(h + 1) * 128], ALU.mult, ALU.add),
                               [Rpkv], [RSh])
                sp.dma(ds_sS, ret_s[l, b0:b0 + 8].rearrange("b h d e -> d b h e"), S_half, [RSh] + R_act, [])
            act.op(lambda e: e.copy(s_oT[:], poS[:, 0:4 * NS].rearrange("p (h b) -> p h b", h=4)), [RpoS], [res("s_oT")])
            po, Rpo = nxt("a")
            for h in range(4):
                pe.op(lambda e, h=h: e.transpose(po[:NS, h * 128:(h + 1) * 128], s_oT[:, h, :], ident[:]),
                      [res("s_oT"), res("consts")], [Rpo], inc=(h == 3))
            for h in range(4):
                hs = slice(h * 128, (h + 1) * 128)
                dve.op(lambda e, h=h, hs=hs: e.tensor_scalar(s_t[:, hs], s_v[:, hs], s_sc[:, h:h + 1], None, ALU.mult),
                       [res("s_v"), res("s_sc")], [res("s_t")])
                dve.op(lambda e, h=h, hs=hs: e.scalar_tensor_tensor(onb[:NS, hs], po[:NS, hs], GAM[h], s_t[:, hs], ALU.mult, ALU.add),
                       [Rpo, res("s_t")], [res("lnout")])
            layernorm_heads(onb, NS, res("lnout"), onb)
            dve.op(lambda e: e.tensor_tensor(onb[:NS, :], onb[:NS, :], retg[:NS, :], ALU.mult), [res("ltab")], [res("lnout")])
            dve.op(lambda e: e.tensor_tensor(s_mix[:, GW:2 * GW], onb[:NS, :], s_g[:, :], ALU.mult),
                   [res("lnout"), res("s_g")], [res("s_mix")])
            for m4 in range(1, 4):
                pt, Rpt = nxt("a")
                for kk in range(4):
                    m = m4 * 4 + kk
                    pe.op(lambda e, m=m, kk=kk: e.transpose(pt[:, kk * NS:(kk + 1) * NS], s_mix[:, (m - 4) * 128:(m - 3) * 128], ident[:NS, :NS]),
                          [res("s_mix"), res("consts")], [Rpt], inc=(kk == 3))
                act.op(lambda e, m4=m4: e.copy(mixT[:, m4 * 4:m4 * 4 + 4, G:G + NS], pt[:, 0:4 * NS].rearrange("p (a b) -> p a b", a=4)),
                       [Rpt], [R_mix[m4 * 4 + kk] for kk in range(4)])

        ds_x = mkds("ds_x")
        ds_lt = mkds("ds_lt")
        ds_cs = mkds("ds_cs")
        ds_st = mkds("ds_st")
        ds_sg = mkds("ds_sg")
        ds_rp = mkds("ds_rp")
        ds_sst = mkds("ds_sst")
        ds_cp = mkds("ds_cp")
        ds_sa = mkds("ds_sa")
        ds_sz = mkds("ds_sz")
        ds_sv = mkds("ds_sv")
        ds_sS = mkds("ds_sS")

        for g in range(NG):
            prompt_group(g, g == NG - 1)

        for d in dsems:
            if d.cnt:
                nc.sync.wait_ge(d.sem, d.cnt)
    return nc


def _tables():
    f32 = np.float32
    half = 64
    inv = (np.float32(10000.0) ** (-np.arange(half, dtype=f32) / f32(half))).astype(f32)
    pos = np.arange(SEQ, dtype=f32)
    ang = pos[:, None] * inv[None, :]
    cos_t = np.cos(ang).astype(f32).reshape(16, 128, 64).transpose(1, 0, 2)
    sin_t = np.sin(ang).astype(f32).reshape(16, 128, 64).transpose(1, 0, 2)
    angs = (np.full((NS,), PAST, dtype=f32)[:, None] * inv[None, :]).astype(f32)
    cos_s = np.cos(angs).astype(f32)
    sin_s = np.sin(angs).astype(f32)
    lg = np.log1p(-(2.0 ** (-5.0 - np.arange(4, dtype=np.float64))))
    idx = np.arange(128, dtype=np.float64)
    diff = idx[None, :] - idx[:, None]
    s = 128.0 ** -0.5
    maskT = np.where(diff[:, None, :] >= 0, np.exp(np.maximum(diff, 0.0)[:, None, :] * lg[None, :, None]), 0.0) * s
    qdec = np.broadcast_to(np.exp((idx + 1.0)[None, None, :] * lg[None, :, None]), (128, 4, 128))
    kdec = np.exp((127.0 - idx)[:, None] * lg[None, :]) * s
    tri = (idx[:, None] <= idx[None, :]).astype(f32)
    invc0 = np.zeros((128, 4, 16), f32)
    for gi, w in enumerate(WINS):
        invc0[:, gi, :] = 1.0 / np.minimum(np.arange(16) + 1, w)
    return dict(cos_t=np.ascontiguousarray(cos_t), sin_t=np.ascontiguousarray(sin_t), cos_s=cos_s, sin_s=sin_s,
                maskT=maskT.astype(f32), qdec=np.ascontiguousarray(qdec).astype(f32), kdec=kdec.astype(f32), tri=tri,
                ident=np.eye(128, dtype=f32), invc0=invc0, eye16=np.eye(NS, dtype=f32))


_NC = None


def kernel(x_prompt, x_sample, state_pool, state_conv, state_ret, norm1_g, w_in, pool_w, pool_scale, conv_w, ret_norm_g,
           sgu_norm_g, sgu_w, sgu_b, w_out, norm2_g, w_gate_up, w_down, final_norm_g):
    global _NC
    f32 = np.float32
    A = lambda a: np.ascontiguousarray(np.asarray(a, dtype=f32))
    x_prompt, x_sample, state_pool, state_conv, state_ret = map(A, (x_prompt, x_sample, state_pool, state_conv, state_ret))
    w_in, w_out, w_gate_up, w_down, pool_w = map(A, (w_in, w_out, w_gate_up, w_down, pool_w))
    norm1_g, norm2_g, final_norm_g, pool_scale, conv_w = map(A, (norm1_g, norm2_g, final_norm_g, pool_scale, conv_w))
    ret_norm_g, sgu_norm_g, sgu_w, sgu_b = map(A, (ret_norm_g, sgu_norm_g, sgu_w, sgu_b))
    if _NC is None:
        _NC = build_program()
    nc = _NC
    gl = np.stack([norm1_g[0], norm2_g[0], norm1_g[1], norm2_g[1], final_norm_g])
    gcols = A(gl.reshape(5, 16, 128).transpose(2, 0, 1))
    pscale = A(pool_scale.reshape(NL, 4, 128).transpose(2, 0, 1))
    convw = A(conv_w.reshape(NL, 3, 4, 128).transpose(3, 0, 1, 2))
    convwb = A(np.broadcast_to(conv_w[None], (NS, NL, 3, GW)))
    retg = A(np.broadcast_to(ret_norm_g[None], (128, NL, GW)))
    sgug = A(np.broadcast_to(sgu_norm_g[None], (128, NL, GW)))
    sguwT = A(sgu_w.transpose(3, 0, 1, 2))
    sgub = A(np.broadcast_to(sgu_b[None], (128, NL, 4, 128)))
    sgu00 = A(np.broadcast_to(np.stack([sgu_w[:, :, 0, 0], sgu_b[:, :, 0]], axis=1)[None], (NS, NL, 2, 4)))
    tabs = _tables()
    shared = dict(w_in=w_in, w_out=w_out, w_gu=w_gate_up, w_dn=w_down, pool_w=pool_w, gcols=gcols, pscale=pscale, convw=convw,
                  convwb=convwb, retg=retg, sgug=sgug, sguwT=sguwT, sgub=sgub, sgu00=sgu00, **tabs)
    in_maps = [None] * 8
    zero_map = None
    for i, c in enumerate(WORK):
        s0 = i * NS
        m = dict(shared)
        m.update(xp=x_prompt[i], xs=A(x_sample[s0:s0 + NS, 0, :]), st_pool=A(state_pool[:, s0:s0 + NS]),
                 st_conv=A(state_conv[:, s0:s0 + NS]), st_ret=A(state_ret[:, s0:s0 + NS]))
        in_maps[c] = m
        if zero_map is None:
            zero_map = {k: np.zeros_like(v) for k, v in m.items()}
    for c in range(8):
        if in_maps[c] is None:
            in_maps[c] = zero_map
    res = run_bass_kernel_spmd(nc, in_maps, core_ids=list(range(8)))
    r = [res.results[c] for c in WORK]
    y_prompt = np.stack([r[b]["y_p"] for b in range(4)])
    y_sample = np.concatenate([r[c]["y_s"] for c in range(4)])[:, None, :]
    pool_prompt = np.stack([r[b]["pool_p"] for b in range(4)], axis=1)
    pool_sample = np.concatenate([r[c]["pool_s"] for c in range(4)], axis=1)
    conv_prompt = np.stack([r[b]["conv_p"] for b in range(4)], axis=1)
    conv_sample = np.concatenate([r[c]["conv_s"] for c in range(4)], axis=1)
    ret_prompt = np.stack([r[b]["ret_p"] for b in range(4)], axis=1)
    ret_sample = np.concatenate([r[c]["ret_s"] for c in range(4)], axis=1)
    sgu_prompt = np.stack([r[b]["sgu_p"] for b in range(4)], axis=1)
    sgu_sample = np.concatenate([r[c]["sgu_s"] for c in range(4)], axis=1)[:, :, None, :]
    outs = (y_prompt, y_sample, pool_prompt, pool_sample, conv_prompt, conv_sample, ret_prompt, ret_sample, sgu_prompt, sgu_sample)
    return tuple(np.ascontiguousarray(o, dtype=f32) for o in outs)
```

```python
import numpy as np
from contextlib import ExitStack
import concourse.bass as bass
import concourse.mybir as mybir
from concourse.bass_utils import run_bass_kernel_spmd

F32 = mybir.dt.float32
BF16 = mybir.dt.bfloat16
ALU = mybir.AluOpType
AF = mybir.ActivationFunctionType
AX = mybir.AxisListType

D = 2048
GW = 512
NL = 2
FF = 5632
NF = FF // 128
SEQ = 2048
G = 512
NG = SEQ // G
NS = 32
GT = G + NS
WORK = [0, 2, 4, 6]
EPS = 1e-6
PAST = 16384
GAM = [1.0 - 2.0 ** (-5.0 - h) for h in range(4)]
WINS = (2, 4, 8, 16)


class Res:
    __slots__ = ("name", "w", "r", "excl")

    def __init__(self, name, excl=False):
        self.name = name
        self.w = None
        self.r = {}
        self.excl = excl


class DSem:
    def __init__(self, nc, es, name):
        self.sem = es.enter_context(nc.semaphore(name))
        self.cnt = 0


class Eng:
    def __init__(self, nc, es, h, name, self_sync=True):
        self.h = h
        self.name = name
        self.sem = es.enter_context(nc.semaphore("sem_" + name))
        self.cnt = 0
        self.seen = {}
        self.self_sync = self_sync
        self.pend = []

    def wait(self, tok):
        if tok is None:
            return
        sem, val = tok
        if sem is self.sem and not self.self_sync:
            return
        k = id(sem)
        if self.seen.get(k, 0) >= val:
            return
        self.h.wait_ge(sem, val)
        self.seen[k] = val

    def _deps(self, reads, writes):
        ws = list(writes) + [r for r in reads if r.excl]
        rs = [r for r in reads if not r.excl]
        for r in rs:
            self.wait(r.w)
        for w in ws:
            self.wait(w.w)
            for tok in list(w.r.values()):
                self.wait(tok)
        return rs, ws

    @staticmethod
    def _apply(tok, rs, ws):
        for r in rs:
            r.r[id(tok[0])] = tok
        for w in ws:
            w.w = tok
            w.r = {}

    def op(self, fn, reads=(), writes=(), inc=True):
        rs, ws = self._deps(reads, writes)
        ins = fn(self.h)
        if inc:
            self.cnt += 1
            ins.then_inc(self.sem, 1)
            tok = (self.sem, self.cnt)
            for (prs, pws) in self.pend:
                self._apply(tok, prs, pws)
            self.pend = []
            self._apply(tok, rs, ws)
        else:
            self.pend.append((rs, ws))

    def dma(self, ds, out, in_, reads=(), writes=(), **kw):
        rs, ws = self._deps(reads, writes)
        ins = self.h.dma_start(out=out, in_=in_, **kw)
        ds.cnt += 16
        ins.then_inc(ds.sem, 16)
        tok = (ds.sem, ds.cnt)
        self._apply(tok, rs, ws)
        return rs, ws


def dma_batch(eng, ds, items, **kw):
    allr, allw = [], []
    for (o, i, rs, ws) in items:
        r2, w2 = eng.dma(ds, o, i, rs, ws, **kw)
        allr += r2
        allw += w2
    tok = (ds.sem, ds.cnt)
    Eng._apply(tok, allr, allw)


def build_program():
    nc = bass.Bass("TRN2", target_bir_lowering=False)

    def din(name, shape):
        return nc.dram_tensor(name, list(shape), F32, kind="ExternalInput").ap()

    def dout(name, shape):
        return nc.dram_tensor(name, list(shape), F32, kind="ExternalOutput").ap()

    xp = din("xp", [SEQ, D])
    xs = din("xs", [NS, D])
    st_pool = din("st_pool", [NL, NS, 15, GW])
    st_conv = din("st_conv", [NL, NS, 2, GW])
    st_ret = din("st_ret", [NL, NS, 4, 128, 128])
    w_in = din("w_in", [NL, D, 10 * GW])
    w_out = din("w_out", [NL, D, D])
    w_gu = din("w_gu", [NL, D, 2 * FF])
    w_dn = din("w_dn", [NL, FF, D])
    pool_w = din("pool_w", [NL, 4, 128, 128])
    gcols_d = din("gcols", [128, 5, 16])
    pscale_d = din("pscale", [128, NL, 4])
    convw_d = din("convw", [128, NL, 3, 4])
    convwb_d = din("convwb", [NS, NL, 3, GW])
    retg_d = din("retg", [128, NL, GW])
    sgug_d = din("sgug", [128, NL, GW])
    sguwT_d = din("sguwT", [128, NL, 4, 128])
    sgub_d = din("sgub", [128, NL, 4, 128])
    sgu00_d = din("sgu00", [NS, NL, 2, 4])
    cos_d = din("cos_t", [128, 16, 64])
    sin_d = din("sin_t", [128, 16, 64])
    coss_d = din("cos_s", [NS, 64])
    sins_d = din("sin_s", [NS, 64])
    maskT_d = din("maskT", [128, 4, 128])
    qdec_d = din("qdec", [128, 4, 128])
    kdec_d = din("kdec", [128, 4])
    tri_d = din("tri", [128, 128])
    ident_d = din("ident", [128, 128])
    invc0_d = din("invc0", [128, 4, 16])
    eye16_d = din("eye16", [NS, NS])

    y_p = dout("y_p", [SEQ, D])
    y_s = dout("y_s", [NS, D])
    pool_p = dout("pool_p", [NL, 15, GW])
    pool_s = dout("pool_s", [NL, NS, 15, GW])
    conv_p = dout("conv_p", [NL, 2, GW])
    conv_s = dout("conv_s", [NL, NS, 2, GW])
    ret_p = dout("ret_p", [NL, 4, 128, 128])
    ret_s = dout("ret_s", [NL, NS, 4, 128, 128])
    sgu_p = dout("sgu_p", [NL, 128, GW])
    sgu_s = dout("sgu_s", [NL, NS, GW])

    with ExitStack() as es:
        def sb(name, shape, dt=F32):
            return es.enter_context(nc.sbuf_tensor("sb_" + name, list(shape), dt))

        def psum(name, shape, dt=F32):
            return es.enter_context(nc.psum_tensor("ps_" + name, list(shape), dt))

        pe = Eng(nc, es, nc.tensor, "pe", self_sync=False)
        dve = Eng(nc, es, nc.vector, "dve")
        act = Eng(nc, es, nc.scalar, "act")
        pool = Eng(nc, es, nc.gpsimd, "pool")
        sp = Eng(nc, es, nc.sync, "sp")
        dsems = []

        def mkds(name):
            d = DSem(nc, es, name)
            dsems.append(d)
            return d

        hT = sb("hT", [128, 16, GT])
        xnT = sb("xnT", [128, 16, GT], BF16)
        mixT = sb("mixT", [128, 16, GT], BF16)
        big = sb("big", [128, NF * GT // 2])
        arena = sb("arena", [128, 8704])

        def carve(base, off, shape, dt=F32):
            nfl = int(np.prod(shape[1:]))
            if dt == BF16:
                v = base[:, off:off + (nfl + 1) // 2].bitcast(BF16)[:, 0:nfl]
                used = (nfl + 1) // 2
            else:
                v = base[:, off:off + nfl]
                used = nfl
            v = v[0:shape[0]]
            if len(shape) == 3:
                v = v.rearrange("p (a b) -> p a b", a=shape[1])
            elif len(shape) == 4:
                v = v.rearrange("p (a b c) -> p a b c", a=shape[1], b=shape[2])
            return v, off + used

        actT = big[:].bitcast(BF16).rearrange("p (j t) -> p j t", t=GT)
        o = 0
        aext, o = carve(big, o, [128, 4, 15 + G])
        zext, o = carve(big, o, [128, 4, 2 + G])
        hh_sb, o = carve(big, o, [128, 4, G])
        u_sb, o = carve(big, o, [128, 4, G], BF16)
        vnb, o = carve(big, o, [128, 4, GW], BF16)
        gs, o = carve(big, o, [128, 4, GW], BF16)
        assert o <= 10688
        pFM, _ = carve(big, 10688, [128, 40, NS])
        sguw_f, _ = carve(big, 0, [128, NL, 4, 128])
        o = 0
        S_half, o = carve(big, o, [128, 8, 4, 128])
        s_cprev, o = carve(big, o, [NS, 2, GW])
        s_convw, o = carve(big, o, [NS, 3, GW])
        s_prev4, o = carve(big, o, [NS, 26, 128])
        assert o <= 10688
        o = 0
        xins = [carve(arena, i * D, [128, D])[0] for i in range(3)]
        xrot = [0]
        qrot, o = carve(arena, o, [128, 4, GW], BF16)
        krot, o = carve(arena, o, [128, 4, GW], BF16)
        kd, o = carve(arena, o, [128, 4, GW], BF16)
        vbf, o = carve(arena, o, [128, 4, GW], BF16)
        qT, o = carve(arena, o, [128, 4, 128], BF16)
        qdT, o = carve(arena, o, [128, 4, 128], BF16)
        kT, o = carve(arena, o, [128, 4, 128], BF16)
        scT, o = carve(arena, o, [128, 4, 128], BF16)
        ycb, o = carve(arena, o, [128, GW], BF16)
        tmp4, o = carve(arena, o, [128, 4, 128])
        pt0, o = carve(arena, o, [128, 15 + G])
        pt1, o = carve(arena, o, [128, 15 + G])
        ptmp = [pt0, pt1]
        Sst, o = carve(arena, o, [128, NL, 4, 128])
        Sbf, o = carve(arena, o, [128, NL, 4, 128], BF16)
        carry_a, o = carve(arena, o, [128, NL, 4, 15])
        carry_z, o = carve(arena, o, [128, NL, 4, 2])
        assert o <= 8704, o
        o = 0
        s_a, o = carve(arena, o, [NS, GW])
        s_d, o = carve(arena, o, [NS, GW])
        s_hh, o = carve(arena, o, [NS, GW])
        s_u, o = carve(arena, o, [NS, GW])
        s_q, o = carve(arena, o, [NS, GW])
        s_k, o = carve(arena, o, [NS, GW])
        s_v, o = carve(arena, o, [NS, GW])
        s_t, o = carve(arena, o, [NS, GW])
        s_g, o = carve(arena, o, [NS, GW])
        s_mix, o = carve(arena, o, [NS, 3 * GW])
        s_kdiag0, o = carve(arena, o, [NS, GW])
        s_kdiag = [s_kdiag0, s_kdiag0]
        s_qT, o = carve(arena, o, [128, 4, NS])
        s_oT, o = carve(arena, o, [128, 4, NS])
        s_sc, o = carve(arena, o, [NS, 4])
        assert o <= 6942, o
        NSLOT = 3
        wslot = [sb(f"wslot{i}", [128, 4096], BF16) for i in range(NSLOT)]
        rstd = sb("rstd", [128, G])
        sqb = [sb(f"sqb{i}", [128, G], BF16) for i in range(2)]
        dT = sb("dT", [128, G], BF16)
        dT2 = sb("dT2", [128, G], BF16)
        dT3 = sb("dT3", [128, G], BF16)
        dTs = [dT, dT2, dT3]
        cacc = sb("cacc", [128, G])
        vnf = sb("vnf", [128, GW])
        rt = [sb(f"rt{i}", [128, 4, 64]) for i in range(2)]
        onb = sb("onb", [128, GW])
        stt = sb("stt", [128, 4, 6])
        mv = sb("mv", [128, 4, 2])
        sd = sb("sd", [128, 4])
        nb = sb("nb", [128, 4])
        ident = sb("ident", [128, 128])
        identb = sb("identb", [128, 128], BF16)
        ones_b = sb("ones_b", [128, 128], BF16)
        epsc = sb("epsc", [128, 1])
        gcols = sb("gcols", [128, 5, 16])
        pscale = sb("pscale", [128, NL, 4])
        convw = sb("convw", [128, NL, 3, 4])
        retg = sb("retg", [128, GW])
        sgug = sb("sgug", [128, GW])
        WsT = sb("WsT", [128, NL, 4, 128], BF16)
        sgub = sb("sgub", [128, 4, 128])
        sgu00 = sb("sgu00", [NS, NL, 2, 4])
        poolw = sb("poolw", [128, NL * 4, 128], BF16)
        cos_g = sb("cos_g", [128, 4, 64])
        sin_g = sb("sin_g", [128, 4, 64])
        cos_s = sb("cos_s", [NS, 64])
        sin_s = sb("sin_s", [NS, 64])
        maskT = sb("maskT", [128, 4, 128])
        qdec = sb("qdec", [128, 4, 128])
        kdec = sb("kdec", [128, 4])
        tri = sb("tri", [128, 128])
        invc0 = sb("invc0", [128, 4, 16])
        eye16 = sb("eye16", [NS, NS])

        pm = [psum(f"pm{i}", [128, 512]) for i in range(4)]
        pa = [psum(f"pa{i}", [128, 512]) for i in range(2)]
        pb = [psum(f"pb{i}", [128, 1024], BF16) for i in range(2)]
        R_pm = [Res(f"pm{i}", True) for i in range(4)]
        R_pa = [Res(f"pa{i}", True) for i in range(2)]
        R_pb = [Res(f"pb{i}", True) for i in range(2)]
        rot = {"m": 0, "a": 0, "b": 0}

        mlimit = [4]

        def nxt(kind):
            lst, rl = {"m": (pm, R_pm), "a": (pa, R_pa), "b": (pb, R_pb)}[kind]
            i = rot[kind] % (mlimit[0] if kind == "m" else len(lst))
            rot[kind] += 1
            return lst[i], rl[i]

        R = {}

        def res(name):
            if name not in R:
                R[name] = Res(name)
            return R[name]

        R_h = [res(f"h{k}") for k in range(16)]
        R_xn = [res(f"xn{k}") for k in range(16)]
        R_mix = [res(f"mix{k}") for k in range(16)]
        R_act = [res(f"act{j}") for j in range(NF)]
        R_slot = [res(f"slot{i}") for i in range(NSLOT)]
        slot_ds = [mkds(f"ds_slot{i}") for i in range(NSLOT)]
        slot_rot = [0]

        ds_setup = mkds("ds_setup")
        setup_items = []
        for (t, d) in [(ident, ident_d), (gcols, gcols_d), (pscale, pscale_d), (convw, convw_d),
                       (sguw_f, sguwT_d), (sgu00, sgu00_d), (cos_s, coss_d), (sin_s, sins_d), (maskT, maskT_d), (qdec, qdec_d),
                       (kdec, kdec_d), (tri, tri_d), (invc0, invc0_d), (eye16, eye16_d)]:
            setup_items.append((t if t is sguw_f else t[:], d, [], [res("consts")]))
        dma_batch(sp, ds_setup, setup_items)
        ds_setup2 = mkds("ds_setup2")
        dma_batch(pool, ds_setup2, [
            (identb[:], ident_d, [], [res("consts2")]),
            (poolw[:], pool_w.rearrange("l g c e -> c (l g) e"), [], [res("consts2")]),
        ])
        RC = [res("consts"), res("consts2"), res("consts3")]
        dve.op(lambda e: e.memset(ones_b[:], 1.0), [], [res("consts3")])
        dve.op(lambda e: e.memset(epsc[:], EPS), [], [res("consts3")])
        dve.op(lambda e: e.memset(Sst[:], 0.0), [], [res("S0"), res("S1")])
        dve.op(lambda e: e.memset(Sbf[:], 0.0), [], [res("Sbf0"), res("Sbf1")])
        dve.op(lambda e: e.memset(carry_a[:], 0.0), [], [res("ca0"), res("ca1")])
        dve.op(lambda e: e.memset(carry_z[:], 0.0), [], [res("cz0"), res("cz1")])
        dve.op(lambda e: e.tensor_tensor(WsT[:].rearrange("p l h t -> p (l h) t"),
                                         sguw_f.rearrange("p l h t -> p (l h) t"),
                                         tri[:].unsqueeze(1).to_broadcast([128, NL * 4, 128]), ALU.mult),
               RC, [res("consts3")])

        def load_w(view_shape, src):
            i = slot_rot[0] % NSLOT
            slot_rot[0] += 1
            n = int(np.prod(view_shape[1:]))
            v = wslot[i][:, 0:n]
            if len(view_shape) == 3:
                v = v.rearrange("p (a b) -> p a b", a=view_shape[1])
            elif len(view_shape) == 4:
                v = v.rearrange("p (a b c) -> p a b c", a=view_shape[1], b=view_shape[2])
            pool.dma(slot_ds[i], v, src, [], [R_slot[i]])
            return v, R_slot[i]

        def rmsnorm(n, gidx, out_xn=True, c0=0):
            cs = slice(c0, c0 + n)
            ss, Rss = nxt("a")
            for k in range(16):
                s = sqb[k % 2]
                Rs = res(f"sqb{k % 2}")
                act.op(lambda e, s=s, k=k: e.activation(s[:, :n], hT[:, k, cs], AF.Square), [R_h[k]], [Rs])
                pe.op(lambda e, s=s, k=k: e.matmul(ss[:, :n], ones_b[:], s[:, :n], start=(k == 0), stop=(k == 15)),
                      [Rs, res("consts3")], [Rss], inc=True)
            act.op(lambda e: e.activation(rstd[:, :n], ss[:, :n], AF.Sqrt, bias=epsc[:], scale=1.0 / D),
                   [Rss, res("consts3")], [res("rstd")])
            dve.op(lambda e: e.reciprocal(rstd[:, :n], rstd[:, :n]), [], [res("rstd")])
            for k in range(16):
                if out_xn:
                    dve.op(lambda e, k=k: e.scalar_tensor_tensor(xnT[:, k, cs], hT[:, k, cs], gcols[:, gidx, k:k + 1],
                                                                 rstd[:, :n], ALU.mult, ALU.mult),
                           [R_h[k], res("rstd"), res("consts")], [R_xn[k]])
                else:
                    dve.op(lambda e, k=k: e.scalar_tensor_tensor(hT[:, k, cs], hT[:, k, cs], gcols[:, gidx, k:k + 1],
                                                                 rstd[:, :n], ALU.mult, ALU.mult),
                           [res("rstd"), res("consts")], [R_h[k]])

        def load_ltab(l):
            dma_batch(sp, ds_lt, [(retg[:], retg_d[:, l, :], [], [res("ltab")]),
                                  (sgug[:], sgug_d[:, l, :], [], [res("ltab")]),
                                  (sgub[:], sgub_d[:, l, :, :], [], [res("ltab")])])

        def load_x(n, src_rows):
            for (src, nt, c0) in src_rows:
                bi = xrot[0] % 3
                xrot[0] += 1
                xin, Rx = xins[bi], res(f"xin{bi}")
                sp.dma(ds_xb[bi], xin[:nt, :], src, [], [Rx])
                for k4 in range(4):
                    pt, Rpt = nxt("a")
                    for kk in range(4):
                        k = k4 * 4 + kk
                        pe.op(lambda e, k=k, kk=kk: e.transpose(pt[:, kk * 128:kk * 128 + nt], xin[:nt, k * 128:(k + 1) * 128],
                                                                ident[:nt, :nt]),
                              [Rx, res("consts")], [Rpt], inc=(kk == 3))
                    act.op(lambda e, k4=k4: e.copy(hT[:, k4 * 4:k4 * 4 + 4, c0:c0 + nt],
                                                   pt[:, 0:512].rearrange("p (a b) -> p a b", a=4)[:, :, :nt]),
                           [Rpt], [R_h[k4 * 4 + kk] for kk in range(4)])

        def store_y(n, dst_rows):
            for (dst, nt, c0) in dst_rows:
                bi = xrot[0] % 3
                xrot[0] += 1
                xin, Rx = xins[bi], res(f"xin{bi}")
                for k4 in range(4):
                    pt, Rpt = nxt("a")
                    for kk in range(4):
                        k = k4 * 4 + kk
                        pe.op(lambda e, k=k, kk=kk: e.transpose(pt[:nt, kk * 128:(kk + 1) * 128], hT[:, k, c0:c0 + nt], ident[:]),
                              [R_h[k], res("consts")], [Rpt], inc=(kk == 3))
                    act.op(lambda e, k4=k4: e.copy(xin[:nt, k4 * 512:(k4 + 1) * 512], pt[:nt, :]), [Rpt], [Rx])
                sp.dma(ds_xb[bi], dst, xin[:nt, :], [Rx], [])

        def wout_and_ffn(l, n, ns=0):
            c1 = G
            if ns:
                pQ, RpQ = nxt("a")
            for dp in range(8):
                wv, Rw = load_w([128, 16, 256], w_out[l, :, dp * 256:(dp + 1) * 256].rearrange("(k p) c -> p k c", p=128))
                for dd in range(2):
                    d = dp * 2 + dd
                    ps, Rps = nxt("m")
                    for k in range(16):
                        pe.op(lambda e, k=k, dd=dd: e.matmul(ps[:, :n], wv[:, k, dd * 128:(dd + 1) * 128], mixT[:, k, :n],
                                                             start=(k == 0), stop=(k == 15)),
                              [Rw, R_mix[k]], [Rps], inc=(k == 15))
                    dve.op(lambda e, d=d: e.tensor_tensor(hT[:, d, :n], hT[:, d, :n], ps[:, :n], ALU.add), [Rps], [R_h[d]])
                    if ns:
                        for k in range(16):
                            pe.op(lambda e, k=k, dd=dd, d=d: e.matmul(pQ[:, d * ns:(d + 1) * ns], wv[:, k, dd * 128:(dd + 1) * 128],
                                                                     mixT[:, k, c1:c1 + ns], start=(k == 0), stop=(k == 15)),
                                  [Rw, R_mix[k]], [RpQ], inc=(k == 15))
            if ns:
                dve.op(lambda e: e.tensor_tensor(hT[:, :, c1:c1 + ns], hT[:, :, c1:c1 + ns],
                                                 pQ[:, 0:16 * ns].rearrange("p (a b) -> p a b", a=16), ALU.add), [RpQ], R_h)
            rmsnorm(n, 2 * l + 1)
            if ns:
                rmsnorm(ns, 2 * l + 1, c0=c1)
            for jp in range(NF // 2):
                wg, Rwg = load_w([128, 16, 256], w_gu[l, :, jp * 256:(jp + 1) * 256].rearrange("(k p) c -> p k c", p=128))
                wu, Rwu = load_w([128, 16, 256], w_gu[l, :, FF + jp * 256:FF + (jp + 1) * 256].rearrange("(k p) c -> p k c", p=128))
                pgs = [nxt("m"), nxt("m")]
                for jj in range(2):
                    pg, Rpg = pgs[jj]
                    for k in range(16):
                        pe.op(lambda e, k=k, jj=jj, pg=pg: e.matmul(pg[:, :n], wg[:, k, jj * 128:(jj + 1) * 128], xnT[:, k, :n],
                                                                  start=(k == 0), stop=(k == 15)),
                              [Rwg, R_xn[k]], [Rpg], inc=(k == 15))
                pus = [nxt("m"), nxt("m")]
                for jj in range(2):
                    pu, Rpu = pus[jj]
                    for k in range(16):
                        pe.op(lambda e, k=k, jj=jj, pu=pu: e.matmul(pu[:, :n], wu[:, k, jj * 128:(jj + 1) * 128], xnT[:, k, :n],
                                                                  start=(k == 0), stop=(k == 15)),
                              [Rwu, R_xn[k]], [Rpu], inc=(k == 15))
                for jj in range(2):
                    j = jp * 2 + jj
                    pg, Rpg = pgs[jj]
                    pu, Rpu = pus[jj]
                    sgb, Rsg = (cacc, res("cacc")) if jj == 0 else (rstd, res("rstd"))
                    act.op(lambda e, pg=pg, sgb=sgb: e.activation(sgb[:, :n], pg[:, :n], AF.Silu), [Rpg], [Rsg])
                    dve.op(lambda e, j=j, pu=pu, sgb=sgb: e.tensor_tensor(actT[:, j, :n], pu[:, :n], sgb[:, :n], ALU.mult),
                           [Rpu, Rsg], [R_act[j]])
                if ns:
                    for jj in range(2):
                        j = jp * 2 + jj
                        q16 = j % 16
                        if q16 == 0:
                            (pgS, RpgS), (puS, RpuS) = nxt("a"), nxt("a")
                            sstate["g"] = (pgS, RpgS, puS, RpuS)
                        pgS, RpgS, puS, RpuS = sstate["g"]
                        for k in range(16):
                            pe.op(lambda e, k=k, jj=jj, q16=q16, pgS=pgS: e.matmul(pgS[:, q16 * ns:(q16 + 1) * ns], wg[:, k, jj * 128:(jj + 1) * 128],
                                                                                 xnT[:, k, c1:c1 + ns], start=(k == 0), stop=(k == 15)),
                                  [Rwg, R_xn[k]], [RpgS], inc=(k == 15))
                        for k in range(16):
                            pe.op(lambda e, k=k, jj=jj, q16=q16, puS=puS: e.matmul(puS[:, q16 * ns:(q16 + 1) * ns], wu[:, k, jj * 128:(jj + 1) * 128],
                                                                                 xnT[:, k, c1:c1 + ns], start=(k == 0), stop=(k == 15)),
                                  [Rwu, R_xn[k]], [RpuS], inc=(k == 15))
                        if q16 == 15 or j == NF - 1:
                            cnt = q16 + 1
                            j0 = j - q16
                            act.op(lambda e, pgS=pgS, cnt=cnt: e.activation(vnf[:, :cnt * ns], pgS[:, :cnt * ns], AF.Silu), [RpgS], [res("lnout")])
                            dve.op(lambda e, puS=puS, cnt=cnt, j0=j0: e.tensor_tensor(
                                actT[:, j0:j0 + cnt, c1:c1 + ns], puS[:, :cnt * ns].rearrange("p (a b) -> p a b", a=cnt),
                                vnf[:, :cnt * ns].rearrange("p (a b) -> p a b", a=cnt), ALU.mult),
                                [RpuS, res("lnout")], [R_act[jx] for jx in range(j0, j0 + cnt)])
            fsegs = [(0, 16), (16, 32), (32, 44)]
            if ns:
                pQd = [nxt("a"), nxt("a")]
            for dp in range(8):
                psd = [nxt("m"), nxt("m")]
                for si, (f0, f1) in enumerate(fsegs):
                    nf = f1 - f0
                    src = w_dn[l, f0 * 128:f1 * 128, dp * 256:(dp + 1) * 256].rearrange("(j p) c -> p j c", p=128)
                    wv, Rw = load_w([128, nf, 256], src)
                    for dd in range(2):
                        ps, Rps = psd[dd]
                        for jj in range(nf):
                            j = f0 + jj
                            pe.op(lambda e, jj=jj, j=j, dd=dd, ps=ps: e.matmul(ps[:, :n], wv[:, jj, dd * 128:(dd + 1) * 128],
                                                                             actT[:, j, :n], start=(j == 0), stop=(j == NF - 1)),
                                  [Rw, R_act[j]], [Rps], inc=(jj == nf - 1))
                    if ns:
                        for dd in range(2):
                            pq_, Rpq_ = pQd[dd]
                            for jj in range(nf):
                                j = f0 + jj
                                pe.op(lambda e, jj=jj, j=j, dd=dd, pq_=pq_, dp=dp: e.matmul(
                                    pq_[:, dp * ns:(dp + 1) * ns], wv[:, jj, dd * 128:(dd + 1) * 128], actT[:, j, c1:c1 + ns],
                                    start=(j == 0), stop=(j == NF - 1)),
                                    [Rw, R_act[j]], [Rpq_], inc=(jj == nf - 1))
                for dd in range(2):
                    d = dp * 2 + dd
                    ps, Rps = psd[dd]
                    dve.op(lambda e, d=d, ps=ps: e.tensor_tensor(hT[:, d, :n], hT[:, d, :n], ps[:, :n], ALU.add), [Rps], [R_h[d]])
            if ns:
                for dd in range(2):
                    pq_, Rpq_ = pQd[dd]
                    for dp in range(8):
                        d = dp * 2 + dd
                        dve.op(lambda e, d=d, dp=dp, pq_=pq_: e.tensor_tensor(hT[:, d, c1:c1 + ns], hT[:, d, c1:c1 + ns],
                                                                            pq_[:, dp * ns:(dp + 1) * ns], ALU.add), [Rpq_], [R_h[d]])

        sstate = {}

        def layernorm_heads(src_ps, npart, Rsrc, dst, nheads=4, width=128):
            for h in range(nheads):
                dve.op(lambda e, h=h: e.bn_stats(stt[:npart, h, :], src_ps[:npart, h * width:(h + 1) * width]), [Rsrc], [res("stt")])
            for h in range(nheads):
                dve.op(lambda e, h=h: e.bn_aggr(mv[:npart, h, :], stt[:npart, h, :]), [res("stt")], [res("mv")])
            act.op(lambda e: e.activation(sd[:npart, :nheads], mv[:npart, :nheads, 1], AF.Sqrt, bias=epsc[:npart, :], scale=1.0),
                   [res("mv"), res("consts3")], [res("sd")])
            dve.op(lambda e: e.reciprocal(sd[:npart, :nheads], sd[:npart, :nheads]), [], [res("sd")])
            dve.op(lambda e: e.scalar_tensor_tensor(nb[:npart, :nheads], mv[:npart, :nheads, 0], -1.0, sd[:npart, :nheads],
                                                    ALU.mult, ALU.mult), [res("mv"), res("sd")], [res("nb")])
            for h in range(nheads):
                dve.op(lambda e, h=h: e.tensor_scalar(dst[:npart, h * width:(h + 1) * width], src_ps[:npart, h * width:(h + 1) * width],
                                                      sd[:npart, h:h + 1], nb[:npart, h:h + 1], ALU.mult, ALU.add),
                       [Rsrc, res("sd"), res("nb")], [res("lnout")])

        def rotary(dst, src_ps, npart, cosv, sinv, Rsrc, Rdst, Rtab):
            s4 = src_ps[:npart, :].rearrange("p (h t e) -> p h t e", h=4, t=2)
            d4 = dst.rearrange("p (h t e) -> p h t e", h=4, t=2)
            cb = cosv.unsqueeze(1).to_broadcast([npart, 4, 64])
            sbb = sinv.unsqueeze(1).to_broadcast([npart, 4, 64])
            Rr = res("rt")
            dve.op(lambda e: e.tensor_tensor(rt[0][:npart], s4[:, :, 0, :], cb, ALU.mult), [Rsrc, Rtab], [Rr])
            dve.op(lambda e: e.tensor_tensor(rt[1][:npart], s4[:, :, 1, :], sbb, ALU.mult), [Rsrc], [Rr])
            dve.op(lambda e: e.tensor_tensor(d4[:, :, 0, :], rt[0][:npart], rt[1][:npart], ALU.subtract), [Rr], [Rdst])
            dve.op(lambda e: e.tensor_tensor(rt[0][:npart], s4[:, :, 1, :], cb, ALU.mult), [Rsrc], [Rr])
            dve.op(lambda e: e.tensor_tensor(rt[1][:npart], s4[:, :, 0, :], sbb, ALU.mult), [Rsrc], [Rr])
            dve.op(lambda e: e.tensor_tensor(d4[:, :, 1, :], rt[0][:npart], rt[1][:npart], ALU.add), [Rr], [Rdst])

        def prompt_group(g, ws):
            n = G
            c1 = G
            dma_batch(sp, ds_cs, [(cos_g[:], cos_d[:, g * 4:(g + 1) * 4, :], [], [res("cs")]),
                                  (sin_g[:], sin_d[:, g * 4:(g + 1) * 4, :], [], [res("cs")])])
            load_x(n, [(xp[g * G + c * 128:g * G + (c + 1) * 128, :], 128, c * 128) for c in range(4)])
            if ws:
                load_x(NS, [(xs, NS, c1)])
            for l in range(NL):
                load_ltab(l)
                rmsnorm(n, 2 * l)
                if ws:
                    rmsnorm(NS, 2 * l, c0=c1)
                    mlimit[0] = 3
                RS, RSb = res(f"S{l}"), res(f"Sbf{l}")
                Rca, Rcz = res(f"ca{l}"), res(f"cz{l}")

                def fm_block(cb, consume):
                    if ws:
                        pS, RpS = pm[3], R_pm[3]
                    for half in range(2):
                        src = w_in[l, :, cb * GW + half * 256: cb * GW + (half + 1) * 256].rearrange("(k p) c -> p k c", p=128)
                        wv, Rw = load_w([128, 16, 256], src)
                        for ee in range(2):
                            ci = half * 2 + ee
                            ps, Rps = nxt("m")
                            for k in range(16):
                                pe.op(lambda e, k=k, ee=ee: e.matmul(ps[:, :n], wv[:, k, ee * 128:(ee + 1) * 128], xnT[:, k, :n],
                                                                     start=(k == 0), stop=(k == 15)),
                                      [Rw, R_xn[k]], [Rps], inc=(k == 15))
                            consume(ci, ps, Rps)
                            if ws:
                                for k in range(16):
                                    pe.op(lambda e, k=k, ee=ee, ci=ci: e.matmul(pS[:, ci * NS:(ci + 1) * NS], wv[:, k, ee * 128:(ee + 1) * 128],
                                                                              xnT[:, k, c1:c1 + NS], start=(k == 0), stop=(k == 15)),
                                          [Rw, R_xn[k]], [RpS], inc=(k == 15))
                            after_group()
                    if ws:
                        act.op(lambda e: e.copy(pFM[:, cb * 4:(cb + 1) * 4, :], pS[:, 0:4 * NS].rearrange("p (a b) -> p a b", a=4)),
                               [RpS], [res(f"pfm{cb}")])

                def tm_block(cb, consume):
                    wvs = []
                    for half in range(2):
                        src = w_in[l, half * 1024:(half + 1) * 1024, cb * GW:(cb + 1) * GW].rearrange("(k p) c -> p k c", p=128)
                        wvs.append(load_w([128, 8, GW], src))
                    if not ws:
                        for half in range(2):
                            wv, Rw = wvs[half]
                            for c in range(4):
                                ps, Rps = pm[c], R_pm[c]
                                for kk in range(8):
                                    k = half * 8 + kk
                                    pe.op(lambda e, k=k, kk=kk, c=c, wv=wv, ps=ps: e.matmul(ps[:, :], xnT[:, k, c * 128:(c + 1) * 128], wv[:, kk, :],
                                                                                          start=(k == 0), stop=(k == 15)),
                                          [Rw, R_xn[k]], [Rps], inc=(kk == 7))
                                if half == 1:
                                    consume(c, ps, Rps)
                    else:
                        for c in range(4):
                            ps, Rps = nxt("m")
                            for k in range(16):
                                wv, Rw = wvs[k // 8]
                                pe.op(lambda e, k=k, c=c, wv=wv: e.matmul(ps[:, :], xnT[:, k, c * 128:(c + 1) * 128], wv[:, k % 8, :],
                                                                          start=(k == 0), stop=(k == 15)),
                                      [Rw, R_xn[k]], [Rps], inc=(k == 15))
                            consume(c, ps, Rps)
                    if ws:
                        pS, RpS = pm[3], R_pm[3]
                        for eb in range(4):
                            for k in range(16):
                                wv, Rw = wvs[k // 8]
                                pe.op(lambda e, k=k, eb=eb, wv=wv: e.matmul(pS[:, eb * NS:(eb + 1) * NS], wv[:, k % 8, eb * 128:(eb + 1) * 128],
                                                                          xnT[:, k, c1:c1 + NS], start=(k == 0), stop=(k == 15)),
                                      [Rw, R_xn[k]], [RpS], inc=(k == 15))
                        act.op(lambda e: e.copy(pFM[:, cb * 4:(cb + 1) * 4, :], pS[:, 0:4 * NS].rearrange("p (a b) -> p a b", a=4)),
                               [RpS], [res(f"pfm{cb}")])

                hooks = {"gen": None, "deferred": []}

                def after_group(flush=False):
                    keep = []
                    for (age, f) in hooks["deferred"]:
                        if age >= 2 or flush:
                            f()
                        else:
                            keep.append((age + 1, f))
                    hooks["deferred"] = keep
                    if hooks["gen"] is not None:
                        try:
                            next(hooks["gen"])
                        except StopIteration:
                            hooks["gen"] = None

                def c_q(c, ps, Rps):
                    rotary(qrot[:, c, :], ps, 128, cos_g[:, c, :], sin_g[:, c, :], Rps, res(f"qrot{c}"), res("cs"))

                def c_k(c, ps, Rps):
                    rotary(krot[:, c, :], ps, 128, cos_g[:, c, :], sin_g[:, c, :], Rps, res(f"krot{c}"), res("cs"))
                    dve.op(lambda e: e.tensor_tensor(kd[:, c, :].rearrange("p (h e) -> p h e", h=4),
                                                     krot[:, c, :].rearrange("p (h e) -> p h e", h=4),
                                                     kdec[:].unsqueeze(2).to_broadcast([128, 4, 128]), ALU.mult),
                           [res(f"krot{c}"), res("consts")], [res(f"kd{c}")])

                def c_v(c, ps, Rps):
                    act.op(lambda e: e.copy(vbf[:, c, :], ps[:, :]), [Rps], [res(f"vbf{c}")])

                def c_g(c, ps, Rps):
                    act.op(lambda e: e.activation(gs[:, c, :], ps[:, :], AF.Silu), [Rps], [res(f"gs{c}")])

                def c_vv(c, ps, Rps):
                    layernorm_heads(ps, 128, Rps, vnf, nheads=1, width=GW)
                    dve.op(lambda e: e.tensor_tensor(vnf[:, :], vnf[:, :], sgug[:, :], ALU.mult), [res("ltab")], [res("lnout")])
                    act.op(lambda e: e.copy(vnb[:, c, :], vnf[:, :]), [res("lnout")], [res(f"vnb{c}")])
                    if g == NG - 1 and c == 3:
                        sp.dma(ds_sg, sgu_p[l], vnf[:, :], [res("lnout")], [])

                tm_block(4, c_q)
                tm_block(5, c_k)
                tm_block(6, c_v)
                tm_block(7, c_g)
                tm_block(9, c_vv)

                def mixer_gen():
                    for c in range(4):
                        tsl = slice(c * 128, (c + 1) * 128)
                        pq, Rpq = nxt("b")
                        for h in range(4):
                            pe.op(lambda e, h=h: e.transpose(pq[:, h * 128:(h + 1) * 128], qrot[:, c, h * 128:(h + 1) * 128], identb[:]),
                                  [res(f"qrot{c}"), res("consts2")], [Rpq], inc=(h == 3))
                        dve.op(lambda e: e.tensor_copy(qT[:], pq[:, 0:512].rearrange("p (h t) -> p h t", h=4)), [Rpq], [res("qT")])
                        dve.op(lambda e: e.tensor_tensor(qdT[:], pq[:, 0:512].rearrange("p (h t) -> p h t", h=4), qdec[:], ALU.mult),
                               [Rpq, res("consts")], [res("qdT")])
                        pk, Rpk = nxt("b")
                        for h in range(4):
                            pe.op(lambda e, h=h: e.transpose(pk[:, h * 128:(h + 1) * 128], krot[:, c, h * 128:(h + 1) * 128], identb[:]),
                                  [res(f"krot{c}"), res("consts2")], [Rpk], inc=(h == 3))
                        act.op(lambda e: e.copy(kT[:], pk[:, 0:512].rearrange("p (h t) -> p h t", h=4)), [Rpk], [res("kT")])
                        yield
                        psc, Rpsc = nxt("a")
                        for h in range(4):
                            pe.op(lambda e, h=h: e.matmul(psc[:, h * 128:(h + 1) * 128], kT[:, h, :], qT[:, h, :], start=True, stop=True),
                                  [res("kT"), res("qT")], [Rpsc], inc=(h == 3))
                        dve.op(lambda e: e.tensor_tensor(scT[:], psc[:, :].rearrange("p (h t) -> p h t", h=4), maskT[:], ALU.mult),
                               [Rpsc, res("consts")], [res("scT")])
                        pkv, Rpkv = nxt("m")
                        for h in range(4):
                            hs = slice(h * 128, (h + 1) * 128)
                            pe.op(lambda e, h=h, hs=hs: e.matmul(pkv[:, hs], kd[:, c, hs], vbf[:, c, hs], start=True, stop=True),
                                  [res(f"kd{c}"), res(f"vbf{c}")], [Rpkv], inc=(h == 3))
                        yield
                        po, Rpo = nxt("a")
                        for h in range(4):
                            hs = slice(h * 128, (h + 1) * 128)
                            pe.op(lambda e, h=h, hs=hs: e.matmul(po[:, hs], scT[:, h, :], vbf[:, c, hs], start=True, stop=False),
                                  [res("scT"), res(f"vbf{c}")], [Rpo], inc=False)
                            pe.op(lambda e, h=h, hs=hs: e.matmul(po[:, hs], qdT[:, h, :], Sbf[:, l, h, :], start=False, stop=True),
                                  [res("qdT"), RSb], [Rpo], inc=(h == 3))
                        for h in range(4):
                            dve.op(lambda e, h=h: e.scalar_tensor_tensor(Sst[:, l, h, :], Sst[:, l, h, :], GAM[h] ** 128,
                                                                         pkv[:, h * 128:(h + 1) * 128], ALU.mult, ALU.add), [Rpkv], [RS])
                        act.op(lambda e: e.copy(Sbf[:, l, :, :], Sst[:, l, :, :]), [RS], [RSb])
                        layernorm_heads(po, 128, Rpo, onb)
                        dve.op(lambda e: e.tensor_tensor(onb[:, :], onb[:, :], retg[:, :], ALU.mult), [res("ltab")], [res("lnout")])
                        dve.op(lambda e: e.tensor_tensor(ycb[:, :], onb[:, :], gs[:, c, :], ALU.mult), [res("lnout"), res(f"gs{c}")], [res("ycb")])
                        yield
                        yield
                        pyc, Rpyc = nxt("b")
                        for h in range(4):
                            pe.op(lambda e, h=h: e.transpose(pyc[:, h * 128:(h + 1) * 128], ycb[:, h * 128:(h + 1) * 128], identb[:]),
                                  [res("ycb"), res("consts2")], [Rpyc], inc=(h == 3))
                        act.op(lambda e: e.copy(mixT[:, 8:12, tsl], pyc[:, 0:512].rearrange("p (h t) -> p h t", h=4)),
                               [Rpyc], [R_mix[8 + h] for h in range(4)])
                        pmx, Rpmx = nxt("a")
                        for h in range(4):
                            pe.op(lambda e, h=h: e.matmul(pmx[:, h * 128:(h + 1) * 128], vnb[:, c, h * 128:(h + 1) * 128], WsT[:, l, h, :],
                                                          start=True, stop=True), [res(f"vnb{c}"), res("consts3")], [Rpmx], inc=(h == 3))
                        dve.op(lambda e: e.tensor_tensor(tmp4[:], pmx[:, :].rearrange("p (h t) -> p h t", h=4), sgub[:, :, :], ALU.add),
                               [Rpmx, res("ltab")], [res("tmp4")])
                        dve.op(lambda e: e.tensor_tensor(mixT[:, 12:16, tsl], tmp4[:], u_sb[:, :, tsl], ALU.mult),
                               [res("tmp4"), res("u_sb")], [R_mix[12 + h] for h in range(4)])
                        yield

                hooks["gen"] = mixer_gen()

                def c_u(ci, ps, Rps):
                    act.op(lambda e: e.copy(u_sb[:, ci, :n], ps[:, :n]), [Rps], [res("u_sb")])

                fm_block(8, c_u)

                dve.op(lambda e: e.tensor_copy(aext[:, :, 0:15], carry_a[:, l, :, :]), [Rca], [res("aext")])

                def c_a(gi, ps, Rps):
                    act.op(lambda e: e.copy(aext[:, gi, 15:15 + n], ps[:, :n]), [Rps], [res("aext")])
                    cur = aext[:, gi, :]
                    L = 15 + n
                    sh = 1
                    for step in range(gi + 1):
                        o = ptmp[step % 2]
                        dve.op(lambda e, cur=cur, o=o, sh=sh: e.tensor_tensor(o[:, sh:L], cur[:, sh:L], cur[:, 0:L - sh], ALU.add),
                               [res("aext")] if step == 0 else [res(f"ptmp{(step - 1) % 2}")], [res(f"ptmp{step % 2}")])
                        cur = o
                        sh *= 2
                    Rcur = res(f"ptmp{gi % 2}")
                    dTg, RdT = dTs[gi % 3], res(f"dT{gi % 3}")
                    dve.op(lambda e, cur=cur: e.scalar_tensor_tensor(dTg[:, :n], cur[:, 15:15 + n], 1.0 / WINS[gi], aext[:, gi, 15:15 + n],
                                                                     ALU.mult, ALU.subtract), [Rcur, res("aext")], [RdT])
                    if g == 0:
                        dve.op(lambda e, cur=cur: e.tensor_tensor(ptmp[(gi + 1) % 2][:, 0:16], cur[:, 15:31], invc0[:, gi, :], ALU.mult),
                               [Rcur, res("consts")], [res(f"ptmp{(gi + 1) % 2}")])
                        dve.op(lambda e: e.tensor_tensor(dTg[:, 0:16], ptmp[(gi + 1) % 2][:, 0:16], aext[:, gi, 15:31], ALU.subtract),
                               [res(f"ptmp{(gi + 1) % 2}"), res("aext")], [RdT])

                    def pool_mm():
                        py, Rpy = nxt("m")
                        pe.op(lambda e: e.matmul(py[:, :n], poolw[:, l * 4 + gi, :], dTg[:, :n], start=True, stop=True),
                              [RdT, res("consts2")], [Rpy])
                        dve.op(lambda e: e.tensor_scalar(mixT[:, gi, :n], py[:, :n], pscale[:, l, gi:gi + 1], None, ALU.mult),
                               [Rpy, res("consts")], [R_mix[gi]])

                    hooks["deferred"].append((0, pool_mm))

                fm_block(0, c_a)
                dve.op(lambda e: e.tensor_copy(carry_a[:, l, :, :], aext[:, :, n:n + 15]), [res("aext")], [Rca])
                if g == NG - 1:
                    with nc.allow_non_contiguous_dma(reason="small state transpose"):
                        for gi in range(4):
                            sp.dma(ds_st, pool_p[l, :, gi * 128:(gi + 1) * 128].rearrange("t c -> c t"), carry_a[:, l, gi, :], [Rca], [],
                                   allow_slow_non_contiguous=True)

                dve.op(lambda e: e.tensor_copy(zext[:, :, 0:2], carry_z[:, l, :, :]), [Rcz], [res("zext")])

                def c_hh(ci, ps, Rps):
                    act.op(lambda e: e.copy(hh_sb[:, ci, :n], ps[:, :n]), [Rps], [res(f"hh{ci}")])

                fm_block(3, c_hh)

                def c_cg(ci, ps, Rps):
                    dve.op(lambda e: e.tensor_tensor(zext[:, ci, 2:2 + n], ps[:, :n], hh_sb[:, ci, :n], ALU.mult),
                           [Rps, res(f"hh{ci}")], [res("zext")])

                fm_block(2, c_cg)
                dve.op(lambda e: e.tensor_copy(carry_z[:, l, :, :], zext[:, :, n:n + 2]), [res("zext")], [Rcz])
                if g == NG - 1:
                    with nc.allow_non_contiguous_dma(reason="small state transpose"):
                        for gi in range(4):
                            sp.dma(ds_st, conv_p[l, :, gi * 128:(gi + 1) * 128].rearrange("t c -> c t"), carry_z[:, l, gi, :], [Rcz], [],
                                   allow_slow_non_contiguous=True)

                def c_bg(ci, ps, Rps):
                    dve.op(lambda e: e.tensor_scalar(cacc[:, :n], zext[:, ci, 0:n], convw[:, l, 0, ci:ci + 1], None, ALU.mult),
                           [res("zext"), res("consts")], [res("cacc")])
                    dve.op(lambda e: e.scalar_tensor_tensor(cacc[:, :n], zext[:, ci, 1:1 + n], convw[:, l, 1, ci:ci + 1], cacc[:, :n],
                                                            ALU.mult, ALU.add), [res("zext")], [res("cacc")])
                    dve.op(lambda e: e.scalar_tensor_tensor(cacc[:, :n], zext[:, ci, 2:2 + n], convw[:, l, 2, ci:ci + 1], cacc[:, :n],
                                                            ALU.mult, ALU.add), [res("zext")], [res("cacc")])
                    dve.op(lambda e: e.tensor_tensor(mixT[:, 4 + ci, :n], ps[:, :n], cacc[:, :n], ALU.mult),
                           [Rps, res("cacc")], [R_mix[4 + ci]])

                fm_block(1, c_bg)
                while hooks["gen"] is not None or hooks["deferred"]:
                    after_group(flush=True)
                if g == NG - 1:
                    sp.dma(ds_rp, ret_p[l].rearrange("h d e -> d h e"), Sst[:, l, :, :], [RS], [])
                mlimit[0] = 4
                if ws:
                    sample_mixers(l)
                wout_and_ffn(l, n, NS if ws else 0)
            rmsnorm(n, 4, out_xn=False)
            if ws:
                rmsnorm(NS, 4, out_xn=False, c0=c1)
            store_y(n, [(y_p[g * G + c * 128:g * G + (c + 1) * 128, :], 128, c * 128) for c in range(4)])
            if ws:
                store_y(NS, [(y_s, NS, c1)])

        def sample_mixers(l):
            n = NS
            roff = (0, 1, 4, 11)
            R_bigp = ([res("aext"), res("zext"), res("u_sb")] + [res(f"hh{i}") for i in range(4)]
                      + [res(f"vnb{i}") for i in range(4)] + [res(f"gs{i}") for i in range(4)])
            dma_batch(sp, ds_sst, [
                (s_cprev, st_conv[l], [], R_act + R_bigp),
                (s_convw, convwb_d[:, l], [], R_act),
            ] + [(s_prev4[:, roff[gi]:roff[gi] + WINS[gi] - 1, :], st_pool[l, :, 16 - WINS[gi]:15, gi * 128:(gi + 1) * 128], [], R_act)
                 for gi in range(4)])
            dma_batch(sp, ds_cp, [
                (pool_s[l, :, 0:14, :], st_pool[l, :, 1:15, :], [], []),
                (conv_s[l, :, 0:1, :], st_conv[l, :, 1:2, :], [], []),
            ])

            def tm_block(cb, consume):
                ps, Rps = nxt("m")
                for eb in range(4):
                    pe.op(lambda e, eb=eb: e.transpose(ps[:NS, eb * 128:(eb + 1) * 128], pFM[:, cb * 4 + eb, :], ident[:]),
                          [res(f"pfm{cb}"), res("consts")], [Rps], inc=(eb == 3))
                consume(ps, Rps)

            def to_mix(src, Rsrc, col0):
                dve.op(lambda e: e.tensor_copy(s_mix[:, col0:col0 + GW], src), [Rsrc], [res("s_mix")])

            def c_a(ps, Rps):
                act.op(lambda e: e.copy(s_a[:, :], ps[:n, :]), [Rps], [res("s_a")])
                sp.dma(ds_sa, pool_s[l, :, 14, :], s_a[:, :], [res("s_a")], [])
                for gi, w in enumerate(WINS):
                    cs = slice(gi * 128, (gi + 1) * 128)
                    if w > 2:
                        dve.op(lambda e, cs=cs, w=w, gi=gi: e.tensor_reduce(
                            s_d[:, cs], s_prev4[:, roff[gi]:roff[gi] + w - 1, :].rearrange("p t c -> p c t"), AX.X, ALU.add),
                            [R_act[0]], [res("s_d")])
                    else:
                        dve.op(lambda e, cs=cs: e.tensor_copy(s_d[:, cs], s_prev4[:, 0, :]), [R_act[0]], [res("s_d")])
                    dve.op(lambda e, cs=cs: e.tensor_tensor(s_d[:, cs], s_d[:, cs], s_a[:, cs], ALU.add), [res("s_a")], [res("s_d")])
                    dve.op(lambda e, cs=cs, w=w: e.scalar_tensor_tensor(s_d[:, cs], s_d[:, cs], 1.0 / w, s_a[:, cs], ALU.mult, ALU.subtract),
                           [res("s_a")], [res("s_d")])
                pt, Rpt = nxt("a")
                for gi in range(4):
                    pe.op(lambda e, gi=gi: e.transpose(pt[:, gi * NS:(gi + 1) * NS], s_d[:, gi * 128:(gi + 1) * 128], ident[:NS, :NS]),
                          [res("s_d"), res("consts")], [Rpt], inc=(gi == 3))
                act.op(lambda e: e.copy(dT[:, 0:4 * NS], pt[:, 0:4 * NS]), [Rpt], [res("dT0")])
                py, Rpy = nxt("a")
                for gi in range(4):
                    pe.op(lambda e, gi=gi: e.matmul(py[:, gi * NS:(gi + 1) * NS], poolw[:, l * 4 + gi, :], dT[:, gi * NS:(gi + 1) * NS],
                                                    start=True, stop=True), [res("dT0"), res("consts2")], [Rpy], inc=(gi == 3))
                for gi in range(4):
                    dve.op(lambda e, gi=gi: e.tensor_scalar(mixT[:, gi, G:G + NS], py[:, gi * NS:(gi + 1) * NS], pscale[:, l, gi:gi + 1], None, ALU.mult),
                           [Rpy, res("consts")], [R_mix[gi]])

            tm_block(0, c_a)

            def c_hh(ps, Rps):
                act.op(lambda e: e.copy(s_hh[:, :], ps[:n, :]), [Rps], [res("s_hh")])

            tm_block(3, c_hh)

            def c_cg(ps, Rps):
                dve.op(lambda e: e.tensor_tensor(s_hh[:, :], ps[:n, :], s_hh[:, :], ALU.mult), [Rps], [res("s_hh")])
                sp.dma(ds_sz, conv_s[l, :, 1, :], s_hh[:, :], [res("s_hh")], [])

            tm_block(2, c_cg)

            def c_bg(ps, Rps):
                dve.op(lambda e: e.tensor_tensor(s_t[:, :], s_cprev[:, 0, :], s_convw[:, 0, :], ALU.mult),
                       [R_act[0]], [res("s_t")])
                dve.op(lambda e: e.tensor_tensor(s_d[:, :], s_cprev[:, 1, :], s_convw[:, 1, :], ALU.mult),
                       [R_act[0]], [res("s_d")])
                dve.op(lambda e: e.tensor_tensor(s_t[:, :], s_t[:, :], s_d[:, :], ALU.add), [res("s_d")], [res("s_t")])
                dve.op(lambda e: e.tensor_tensor(s_d[:, :], s_hh[:, :], s_convw[:, 2, :], ALU.mult), [res("s_hh"), R_act[0]], [res("s_d")])
                dve.op(lambda e: e.tensor_tensor(s_t[:, :], s_t[:, :], s_d[:, :], ALU.add), [res("s_d")], [res("s_t")])
                dve.op(lambda e: e.tensor_tensor(s_mix[:, 0:GW], ps[:n, :], s_t[:, :], ALU.mult), [Rps, res("s_t")], [res("s_mix")])

            tm_block(1, c_bg)

            def c_u(ps, Rps):
                act.op(lambda e: e.copy(s_u[:, :], ps[:n, :]), [Rps], [res("s_u")])

            tm_block(8, c_u)

            def c_q(ps, Rps):
                rotary(s_q[:, :], ps, NS, cos_s[:, :], sin_s[:, :], Rps, res("s_q"), res("consts"))

            def c_k(ps, Rps):
                rotary(s_k[:, :], ps, NS, cos_s[:, :], sin_s[:, :], Rps, res("s_k"), res("consts"))
                dve.op(lambda e: e.tensor_scalar(s_k[:, :], s_k[:, :], 128.0 ** -0.5, None, ALU.mult), [], [res("s_k")])

            def c_v(ps, Rps):
                act.op(lambda e: e.copy(s_v[:, :], ps[:n, :]), [Rps], [res("s_v")])

            def c_g(ps, Rps):
                act.op(lambda e: e.activation(s_g[:, :], ps[:n, :], AF.Silu), [Rps], [res("s_g")])

            def c_vv(ps, Rps):
                layernorm_heads(ps, NS, Rps, vnf, nheads=1, width=GW)
                dve.op(lambda e: e.tensor_tensor(vnf[:n, :], vnf[:n, :], sgug[:NS, :], ALU.mult), [res("ltab")], [res("lnout")])
                sp.dma(ds_sv, sgu_s[l], vnf[:n, :], [res("lnout")], [])
                for h in range(4):
                    hs = slice(h * 128, (h + 1) * 128)
                    dve.op(lambda e, h=h, hs=hs: e.tensor_scalar(s_t[:, hs], vnf[:n, hs], sgu00[:, l, 0, h:h + 1], sgu00[:, l, 1, h:h + 1],
                                                                 ALU.mult, ALU.add), [res("lnout"), res("consts")], [res("s_t")])
                dve.op(lambda e: e.tensor_tensor(s_mix[:, 2 * GW:3 * GW], s_t[:, :], s_u[:, :], ALU.mult),
                       [res("s_t"), res("s_u")], [res("s_mix")])

            tm_block(4, c_q)
            tm_block(5, c_k)
            tm_block(6, c_v)
            tm_block(7, c_g)
            tm_block(9, c_vv)

            dve.op(lambda e: e.tensor_tensor(s_t[:, :], s_q[:, :], s_k[:, :], ALU.mult), [res("s_q"), res("s_k")], [res("s_t")])
            dve.op(lambda e: e.tensor_reduce(s_sc[:, :], s_t[:, :].rearrange("p (h e) -> p h e", h=4), AX.X, ALU.add),
                   [res("s_t")], [res("s_sc")])
            pt, Rpt = nxt("a")
            for h in range(4):
                pe.op(lambda e, h=h: e.transpose(pt[:, h * NS:(h + 1) * NS], s_q[:, h * 128:(h + 1) * 128], ident[:NS, :NS]),
                      [res("s_q"), res("consts")], [Rpt], inc=(h == 3))
            act.op(lambda e: e.copy(s_qT[:], pt[:, 0:4 * NS].rearrange("p (h b) -> p h b", h=4)), [Rpt], [res("s_qT")])
            poS, RpoS = nxt("a")
            RSh = res("S_half")
            for hb in range(NS // 8):
                b0 = hb * 8
                sp.dma(ds_sS, S_half, st_ret[l, b0:b0 + 8].rearrange("b h d e -> d b h e"), [], [RSh] + R_act + R_bigp)
                for bb in range(8):
                    b = b0 + bb
                    for h in range(4):
                        pe.op(lambda e, b=b, bb=bb, h=h: e.matmul(poS[:, h * NS + b:h * NS + b + 1], S_half[:, bb, h, :],
                                                                  s_qT[:, h, b:b + 1], start=True, stop=True),
                              [res("s_qT"), RSh], [RpoS], inc=(bb == 7 and h == 3))
                for bb in range(8):
                    b = b0 + bb
                    kdg = s_kdiag[0]
                    Rk = res("s_kdiag0")
                    dve.op(lambda e, b=b, kdg=kdg: e.tensor_scalar(kdg[:, :], s_k[:, :], eye16[:, b:b + 1], None, ALU.mult),
                           [res("s_k"), res("consts")], [Rk])
                    pkv, Rpkv = nxt("m")
                    for h in range(4):
                        hs = slice(h * 128, (h + 1) * 128)
                        pe.op(lambda e, hs=hs, kdg=kdg: e.matmul(pkv[:, hs], kdg[:, hs], s_v[:, hs], start=True, stop=True),
                              [Rk, res("s_v")], [Rpkv], inc=(h == 3))
                    for h in range(4):
                        dve.op(lambda e, bb=bb, h=h: e.scalar_tensor_tensor(S_half[:, bb, h, :], S_half[:, bb, h, :], GAM[h],
                                                                            pkv[:, h * 128:(h + 1) * 128], ALU.mult, ALU.add),
                               [Rpkv], [RSh])
                sp.dma(ds_sS, ret_s[l, b0:b0 + 8].rearrange("b h d e -> d b h e"), S_half, [RSh] + R_act, [])
            act.op(lambda e: e.copy(s_oT[:], poS[:, 0:4 * NS].rearrange("p (h b) -> p h b", h=4)), [RpoS], [res("s_oT")])
            po, Rpo = nxt("a")
            for h in range(4):
                pe.op(lambda e, h=h: e.transpose(po[:NS, h * 128:(h + 1) * 128], s_oT[:, h, :], ident[:]),
                      [res("s_oT"), res("consts")], [Rpo], inc=(h == 3))
            for h in range(4):
                hs = slice(h * 128, (h + 1) * 128)
                dve.op(lambda e, h=h, hs=hs: e.tensor_scalar(s_t[:, hs], s_v[:, hs], s_sc[:, h:h + 1], None, ALU.mult),
                       [res("s_v"), res("s_sc")], [res("s_t")])
                dve.op(lambda e, h=h, hs=hs: e.scalar_tensor_tensor(onb[:NS, hs], po[:NS, hs], GAM[h], s_t[:, hs], ALU.mult, ALU.add),
                       [Rpo, res("s_t")], [res("lnout")])
            layernorm_heads(onb, NS, res("lnout"), onb)
            dve.op(lambda e: e.tensor_tensor(onb[:NS, :], onb[:NS, :], retg[:NS, :], ALU.mult), [res("ltab")], [res("lnout")])
            dve.op(lambda e: e.tensor_tensor(s_mix[:, GW:2 * GW], onb[:NS, :], s_g[:, :], ALU.mult),
                   [res("lnout"), res("s_g")], [res("s_mix")])
            for m4 in range(1, 4):
                pt, Rpt = nxt("a")
                for kk in range(4):
                    m = m4 * 4 + kk
                    pe.op(lambda e, m=m, kk=kk: e.transpose(pt[:, kk * NS:(kk + 1) * NS], s_mix[:, (m - 4) * 128:(m - 3) * 128], ident[:NS, :NS]),
                          [res("s_mix"), res("consts")], [Rpt], inc=(kk == 3))
                act.op(lambda e, m4=m4: e.copy(mixT[:, m4 * 4:m4 * 4 + 4, G:G + NS], pt[:, 0:4 * NS].rearrange("p (a b) -> p a b", a=4)),
                       [Rpt], [R_mix[m4 * 4 + kk] for kk in range(4)])

        ds_xb = [mkds(f"ds_x{i}") for i in range(3)]
        ds_lt = mkds("ds_lt")
        ds_cs = mkds("ds_cs")
        ds_st = mkds("ds_st")
        ds_sg = mkds("ds_sg")
        ds_rp = mkds("ds_rp")
        ds_sst = mkds("ds_sst")
        ds_cp = mkds("ds_cp")
        ds_sa = mkds("ds_sa")
        ds_sz = mkds("ds_sz")
        ds_sv = mkds("ds_sv")
        ds_sS = mkds("ds_sS")

        for g in range(NG):
            prompt_group(g, g == NG - 1)

        for d in dsems:
            if d.cnt:
                nc.sync.wait_ge(d.sem, d.cnt)
    return nc


def _tables():
    f32 = np.float32
    half = 64
    inv = (np.float32(10000.0) ** (-np.arange(half, dtype=f32) / f32(half))).astype(f32)
    pos = np.arange(SEQ, dtype=f32)
    ang = pos[:, None] * inv[None, :]
    cos_t = np.cos(ang).astype(f32).reshape(16, 128, 64).transpose(1, 0, 2)
    sin_t = np.sin(ang).astype(f32).reshape(16, 128, 64).transpose(1, 0, 2)
    angs = (np.full((NS,), PAST, dtype=f32)[:, None] * inv[None, :]).astype(f32)
    cos_s = np.cos(angs).astype(f32)
    sin_s = np.sin(angs).astype(f32)
    lg = np.log1p(-(2.0 ** (-5.0 - np.arange(4, dtype=np.float64))))
    idx = np.arange(128, dtype=np.float64)
    diff = idx[None, :] - idx[:, None]
    s = 128.0 ** -0.5
    maskT = np.where(diff[:, None, :] >= 0, np.exp(np.maximum(diff, 0.0)[:, None, :] * lg[None, :, None]), 0.0) * s
    qdec = np.broadcast_to(np.exp((idx + 1.0)[None, None, :] * lg[None, :, None]), (128, 4, 128))
    kdec = np.exp((127.0 - idx)[:, None] * lg[None, :]) * s
    tri = (idx[:, None] <= idx[None, :]).astype(f32)
    invc0 = np.zeros((128, 4, 16), f32)
    for gi, w in enumerate(WINS):
        invc0[:, gi, :] = 1.0 / np.minimum(np.arange(16) + 1, w)
    return dict(cos_t=np.ascontiguousarray(cos_t), sin_t=np.ascontiguousarray(sin_t), cos_s=cos_s, sin_s=sin_s,
                maskT=maskT.astype(f32), qdec=np.ascontiguousarray(qdec).astype(f32), kdec=kdec.astype(f32), tri=tri,
                ident=np.eye(128, dtype=f32), invc0=invc0, eye16=np.eye(NS, dtype=f32))


_NC = None


def kernel(x_prompt, x_sample, state_pool, state_conv, state_ret, norm1_g, w_in, pool_w, pool_scale, conv_w, ret_norm_g,
           sgu_norm_g, sgu_w, sgu_b, w_out, norm2_g, w_gate_up, w_down, final_norm_g):
    global _NC
    f32 = np.float32
    A = lambda a: np.ascontiguousarray(np.asarray(a, dtype=f32))
    x_prompt, x_sample, state_pool, state_conv, state_ret = map(A, (x_prompt, x_sample, state_pool, state_conv, state_ret))
    w_in, w_out, w_gate_up, w_down, pool_w = map(A, (w_in, w_out, w_gate_up, w_down, pool_w))
    norm1_g, norm2_g, final_norm_g, pool_scale, conv_w = map(A, (norm1_g, norm2_g, final_norm_g, pool_scale, conv_w))
    ret_norm_g, sgu_norm_g, sgu_w, sgu_b = map(A, (ret_norm_g, sgu_norm_g, sgu_w, sgu_b))
    if _NC is None:
        _NC = build_program()
    nc = _NC
    gl = np.stack([norm1_g[0], norm2_g[0], norm1_g[1], norm2_g[1], final_norm_g])
    gcols = A(gl.reshape(5, 16, 128).transpose(2, 0, 1))
    pscale = A(pool_scale.reshape(NL, 4, 128).transpose(2, 0, 1))
    convw = A(conv_w.reshape(NL, 3, 4, 128).transpose(3, 0, 1, 2))
    convwb = A(np.broadcast_to(conv_w[None], (NS, NL, 3, GW)))
    retg = A(np.broadcast_to(ret_norm_g[None], (128, NL, GW)))
    sgug = A(np.broadcast_to(sgu_norm_g[None], (128, NL, GW)))
    sguwT = A(sgu_w.transpose(3, 0, 1, 2))
    sgub = A(np.broadcast_to(sgu_b[None], (128, NL, 4, 128)))
    sgu00 = A(np.broadcast_to(np.stack([sgu_w[:, :, 0, 0], sgu_b[:, :, 0]], axis=1)[None], (NS, NL, 2, 4)))
    tabs = _tables()
    shared = dict(w_in=w_in, w_out=w_out, w_gu=w_gate_up, w_dn=w_down, pool_w=pool_w, gcols=gcols, pscale=pscale, convw=convw,
                  convwb=convwb, retg=retg, sgug=sgug, sguwT=sguwT, sgub=sgub, sgu00=sgu00, **tabs)
    in_maps = [None] * 8
    zero_map = None
    for i, c in enumerate(WORK):
        s0 = i * NS
        m = dict(shared)
        m.update(xp=x_prompt[i], xs=A(x_sample[s0:s0 + NS, 0, :]), st_pool=A(state_pool[:, s0:s0 + NS]),
                 st_conv=A(state_conv[:, s0:s0 + NS]), st_ret=A(state_ret[:, s0:s0 + NS]))
        in_maps[c] = m
        if zero_map is None:
            zero_map = {k: np.zeros_like(v) for k, v in m.items()}
    for c in range(8):
        if in_maps[c] is None:
            in_maps[c] = zero_map
    res = run_bass_kernel_spmd(nc, in_maps, core_ids=list(range(8)))
    r = [res.results[c] for c in WORK]
    y_prompt = np.stack([r[b]["y_p"] for b in range(4)])
    y_sample = np.concatenate([r[c]["y_s"] for c in range(4)])[:, None, :]
    pool_prompt = np.stack([r[b]["pool_p"] for b in range(4)], axis=1)
    pool_sample = np.concatenate([r[c]["pool_s"] for c in range(4)], axis=1)
    conv_prompt = np.stack([r[b]["conv_p"] for b in range(4)], axis=1)
    conv_sample = np.concatenate([r[c]["conv_s"] for c in range(4)], axis=1)
    ret_prompt = np.stack([r[b]["ret_p"] for b in range(4)], axis=1)
    ret_sample = np.concatenate([r[c]["ret_s"] for c in range(4)], axis=1)
    sgu_prompt = np.stack([r[b]["sgu_p"] for b in range(4)], axis=1)
    sgu_sample = np.concatenate([r[c]["sgu_s"] for c in range(4)], axis=1)[:, :, None, :]
    outs = (y_prompt, y_sample, pool_prompt, pool_sample, conv_prompt, conv_sample, ret_prompt, ret_sample, sgu_prompt, sgu_sample)
    return tuple(np.ascontiguousarray(o, dtype=f32) for o in outs)
```

```python
import numpy as np
from contextlib import ExitStack
import concourse.bass as bass
import concourse.mybir as mybir
from concourse.bass_utils import run_bass_kernel_spmd

F32 = mybir.dt.float32
BF16 = mybir.dt.bfloat16
ALU = mybir.AluOpType
AF = mybir.ActivationFunctionType
AX = mybir.AxisListType

D = 2048
GW = 512
NL = 2
FF = 5632
NF = FF // 128
SEQ = 2048
G = 512
NG = SEQ // G
NS = 32
GT = G + NS
WORK = [0, 2, 4, 6]
EPS = 1e-6
PAST = 16384
GAM = [1.0 - 2.0 ** (-5.0 - h) for h in range(4)]
WINS = (2, 4, 8, 16)


class Res:
    __slots__ = ("name", "w", "r", "excl")

    def __init__(self, name, excl=False):
        self.name = name
        self.w = None
        self.r = {}
        self.excl = excl


class DSem:
    def __init__(self, nc, es, name):
        self.sem = es.enter_context(nc.semaphore(name))
        self.cnt = 0


class Eng:
    def __init__(self, nc, es, h, name, self_sync=True):
        self.h = h
        self.name = name
        self.sem = es.enter_context(nc.semaphore("sem_" + name))
        self.cnt = 0
        self.seen = {}
        self.self_sync = self_sync
        self.pend = []

    def wait(self, tok):
        if tok is None:
            return
        sem, val = tok
        if sem is self.sem and not self.self_sync:
            return
        k = id(sem)
        if self.seen.get(k, 0) >= val:
            return
        self.h.wait_ge(sem, val)
        self.seen[k] = val

    def _deps(self, reads, writes):
        ws = list(writes) + [r for r in reads if r.excl]
        rs = [r for r in reads if not r.excl]
        for r in rs:
            self.wait(r.w)
        for w in ws:
            self.wait(w.w)
            for tok in list(w.r.values()):
                self.wait(tok)
        return rs, ws

    @staticmethod
    def _apply(tok, rs, ws):
        for r in rs:
            r.r[id(tok[0])] = tok
        for w in ws:
            w.w = tok
            w.r = {}

    def op(self, fn, reads=(), writes=(), inc=True):
        rs, ws = self._deps(reads, writes)
        ins = fn(self.h)
        if inc:
            self.cnt += 1
            ins.then_inc(self.sem, 1)
            tok = (self.sem, self.cnt)
            for (prs, pws) in self.pend:
                self._apply(tok, prs, pws)
            self.pend = []
            self._apply(tok, rs, ws)
        else:
            self.pend.append((rs, ws))

    def dma(self, ds, out, in_, reads=(), writes=(), **kw):
        rs, ws = self._deps(reads, writes)
        ins = self.h.dma_start(out=out, in_=in_, **kw)
        ds.cnt += 16
        ins.then_inc(ds.sem, 16)
        tok = (ds.sem, ds.cnt)
        self._apply(tok, rs, ws)
        return rs, ws


def dma_batch(eng, ds, items, **kw):
    allr, allw = [], []
    for (o, i, rs, ws) in items:
        r2, w2 = eng.dma(ds, o, i, rs, ws, **kw)
        allr += r2
        allw += w2
    tok = (ds.sem, ds.cnt)
    Eng._apply(tok, allr, allw)


def build_program():
    nc = bass.Bass("TRN2", target_bir_lowering=False)

    def din(name, shape):
        return nc.dram_tensor(name, list(shape), F32, kind="ExternalInput").ap()

    def dout(name, shape):
        return nc.dram_tensor(name, list(shape), F32, kind="ExternalOutput").ap()

    xp = din("xp", [SEQ, D])
    xs = din("xs", [NS, D])
    st_pool = din("st_pool", [NL, NS, 15, GW])
    st_conv = din("st_conv", [NL, NS, 2, GW])
    st_ret = din("st_ret", [NL, NS, 4, 128, 128])
    w_in = din("w_in", [NL, D, 10 * GW])
    w_out = din("w_out", [NL, D, D])
    w_gu = din("w_gu", [NL, D, 2 * FF])
    w_dn = din("w_dn", [NL, FF, D])
    pool_w = din("pool_w", [NL, 4, 128, 128])
    gcols_d = din("gcols", [128, 5, 16])
    pscale_d = din("pscale", [128, NL, 4])
    convw_d = din("convw", [128, NL, 3, 4])
    convwb_d = din("convwb", [NS, NL, 3, GW])
    retg_d = din("retg", [128, NL, GW])
    sgug_d = din("sgug", [128, NL, GW])
    sguwT_d = din("sguwT", [128, NL, 4, 128])
    sgub_d = din("sgub", [128, NL, 4, 128])
    sgu00_d = din("sgu00", [NS, NL, 2, 4])
    cos_d = din("cos_t", [128, 16, 64])
    sin_d = din("sin_t", [128, 16, 64])
    coss_d = din("cos_s", [NS, 64])
    sins_d = din("sin_s", [NS, 64])
    maskT_d = din("maskT", [128, 4, 128])
    qdec_d = din("qdec", [128, 4, 128])
    kdec_d = din("kdec", [128, 4])
    tri_d = din("tri", [128, 128])
    ident_d = din("ident", [128, 128])
    invc0_d = din("invc0", [128, 4, 16])
    eye16_d = din("eye16", [NS, NS])

    y_p = dout("y_p", [SEQ, D])
    y_s = dout("y_s", [NS, D])
    pool_p = dout("pool_p", [NL, 15, GW])
    pool_s = dout("pool_s", [NL, NS, 15, GW])
    conv_p = dout("conv_p", [NL, 2, GW])
    conv_s = dout("conv_s", [NL, NS, 2, GW])
    ret_p = dout("ret_p", [NL, 4, 128, 128])
    ret_s = dout("ret_s", [NL, NS, 4, 128, 128])
    sgu_p = dout("sgu_p", [NL, 128, GW])
    sgu_s = dout("sgu_s", [NL, NS, GW])

    with ExitStack() as es:
        def sb(name, shape, dt=F32):
            return es.enter_context(nc.sbuf_tensor("sb_" + name, list(shape), dt))

        def psum(name, shape, dt=F32):
            return es.enter_context(nc.psum_tensor("ps_" + name, list(shape), dt))

        pe = Eng(nc, es, nc.tensor, "pe", self_sync=False)
        dve = Eng(nc, es, nc.vector, "dve")
        act = Eng(nc, es, nc.scalar, "act")
        pool = Eng(nc, es, nc.gpsimd, "pool")
        sp = Eng(nc, es, nc.sync, "sp")
        dsems = []

        def mkds(name):
            d = DSem(nc, es, name)
            dsems.append(d)
            return d

        hT = sb("hT", [128, 16, GT])
        xnT = sb("xnT", [128, 16, GT], BF16)
        mixT = sb("mixT", [128, 16, GT], BF16)
        big = sb("big", [128, NF * GT // 2])
        arena = sb("arena", [128, 8704])

        def carve(base, off, shape, dt=F32):
            nfl = int(np.prod(shape[1:]))
            if dt == BF16:
                v = base[:, off:off + (nfl + 1) // 2].bitcast(BF16)[:, 0:nfl]
                used = (nfl + 1) // 2
            else:
                v = base[:, off:off + nfl]
                used = nfl
            v = v[0:shape[0]]
            if len(shape) == 3:
                v = v.rearrange("p (a b) -> p a b", a=shape[1])
            elif len(shape) == 4:
                v = v.rearrange("p (a b c) -> p a b c", a=shape[1], b=shape[2])
            return v, off + used

        actT = big[:].bitcast(BF16).rearrange("p (j t) -> p j t", t=GT)
        o = 0
        aext, o = carve(big, o, [128, 4, 15 + G])
        zext, o = carve(big, o, [128, 4, 2 + G])
        hh_sb, o = carve(big, o, [128, 4, G])
        u_sb, o = carve(big, o, [128, 4, G], BF16)
        vnb, o = carve(big, o, [128, 4, GW], BF16)
        gs, o = carve(big, o, [128, 4, GW], BF16)
        assert o <= 10688
        pFM, _ = carve(big, 10688, [128, 40, NS])
        sguw_f, _ = carve(big, 0, [128, NL, 4, 128])
        o = 0
        S_q0, o = carve(big, o, [128, 4, 4, 128])
        S_q1, o = carve(big, o, [128, 4, 4, 128])
        S_qs = [S_q0, S_q1]
        s_cprev, o = carve(big, o, [NS, 2, GW])
        s_convw, o = carve(big, o, [NS, 3, GW])
        s_prev4, o = carve(big, o, [NS, 26, 128])
        assert o <= 10688
        o = 0
        xins = [carve(arena, i * D, [128, D])[0] for i in range(3)]
        xrot = [0]
        qrot, o = carve(arena, o, [128, 4, GW], BF16)
        krot, o = carve(arena, o, [128, 4, GW], BF16)
        kd, o = carve(arena, o, [128, 4, GW], BF16)
        vbf, o = carve(arena, o, [128, 4, GW], BF16)
        qT, o = carve(arena, o, [128, 4, 128], BF16)
        qdT, o = carve(arena, o, [128, 4, 128], BF16)
        kT, o = carve(arena, o, [128, 4, 128], BF16)
        scT, o = carve(arena, o, [128, 4, 128], BF16)
        ycb, o = carve(arena, o, [128, GW], BF16)
        tmp4, o = carve(arena, o, [128, 4, 128])
        pt0, o = carve(arena, o, [128, 15 + G])
        pt1, o = carve(arena, o, [128, 15 + G])
        ptmp = [pt0, pt1]
        Sst, o = carve(arena, o, [128, NL, 4, 128])
        Sbf, o = carve(arena, o, [128, NL, 4, 128], BF16)
        carry_a, o = carve(arena, o, [128, NL, 4, 15])
        carry_z, o = carve(arena, o, [128, NL, 4, 2])
        assert o <= 8704, o
        o = 0
        s_a, o = carve(arena, o, [NS, GW])
        s_d, o = carve(arena, o, [NS, GW])
        s_hh, o = carve(arena, o, [NS, GW])
        s_u, o = carve(arena, o, [NS, GW])
        s_q, o = carve(arena, o, [NS, GW])
        s_k, o = carve(arena, o, [NS, GW])
        s_v, o = carve(arena, o, [NS, GW])
        s_t, o = carve(arena, o, [NS, GW])
        s_g, o = carve(arena, o, [NS, GW])
        s_mix, o = carve(arena, o, [NS, 3 * GW])
        s_kdiag0, o = carve(arena, o, [NS, GW])
        s_kdiag = [s_kdiag0, s_kdiag0]
        s_qT, o = carve(arena, o, [128, 4, NS])
        s_oT, o = carve(arena, o, [128, 4, NS])
        s_sc, o = carve(arena, o, [NS, 4])
        assert o <= 6942, o
        NSLOT = 3
        wslot = [sb(f"wslot{i}", [128, 4096], BF16) for i in range(NSLOT)]
        rstd = sb("rstd", [128, G])
        sqb = [sb(f"sqb{i}", [128, G], BF16) for i in range(2)]
        dT = sb("dT", [128, G], BF16)
        dT2 = sb("dT2", [128, G], BF16)
        dT3 = sb("dT3", [128, G], BF16)
        dTs = [dT, dT2, dT3]
        cacc = sb("cacc", [128, G])
        vnf = sb("vnf", [128, GW])
        rt = [sb(f"rt{i}", [128, 4, 64]) for i in range(2)]
        onb = sb("onb", [128, GW])
        stt = sb("stt", [128, 4, 6])
        mv = sb("mv", [128, 4, 2])
        sd = sb("sd", [128, 4])
        nb = sb("nb", [128, 4])
        ident = sb("ident", [128, 128])
        identb = sb("identb", [128, 128], BF16)
        ones_b = sb("ones_b", [128, 128], BF16)
        epsc = sb("epsc", [128, 1])
        gcols = sb("gcols", [128, 5, 16])
        pscale = sb("pscale", [128, NL, 4])
        convw = sb("convw", [128, NL, 3, 4])
        retg = sb("retg", [128, GW])
        sgug = sb("sgug", [128, GW])
        WsT = sb("WsT", [128, NL, 4, 128], BF16)
        sgub = sb("sgub", [128, 4, 128])
        sgu00 = sb("sgu00", [NS, NL, 2, 4])
        poolw = sb("poolw", [128, NL * 4, 128], BF16)
        cos_g = sb("cos_g", [128, 4, 64])
        sin_g = sb("sin_g", [128, 4, 64])
        cos_s = sb("cos_s", [NS, 64])
        sin_s = sb("sin_s", [NS, 64])
        maskT = sb("maskT", [128, 4, 128])
        qdec = sb("qdec", [128, 4, 128])
        kdec = sb("kdec", [128, 4])
        tri = sb("tri", [128, 128])
        invc0 = sb("invc0", [128, 4, 16])
        eye16 = sb("eye16", [NS, NS])

        pm = [psum(f"pm{i}", [128, 512]) for i in range(4)]
        pa = [psum(f"pa{i}", [128, 512]) for i in range(2)]
        pb = [psum(f"pb{i}", [128, 1024], BF16) for i in range(2)]
        R_pm = [Res(f"pm{i}", True) for i in range(4)]
        R_pa = [Res(f"pa{i}", True) for i in range(2)]
        R_pb = [Res(f"pb{i}", True) for i in range(2)]
        rot = {"m": 0, "a": 0, "b": 0}

        mlimit = [4]

        def nxt(kind):
            lst, rl = {"m": (pm, R_pm), "a": (pa, R_pa), "b": (pb, R_pb)}[kind]
            i = rot[kind] % (mlimit[0] if kind == "m" else len(lst))
            rot[kind] += 1
            return lst[i], rl[i]

        R = {}

        def res(name):
            if name not in R:
                R[name] = Res(name)
            return R[name]

        R_h = [res(f"h{k}") for k in range(16)]
        R_xn = [res(f"xn{k}") for k in range(16)]
        R_mix = [res(f"mix{k}") for k in range(16)]
        R_act = [res(f"act{j}") for j in range(NF)]
        R_slot = [res(f"slot{i}") for i in range(NSLOT)]
        slot_ds = [mkds(f"ds_slot{i}") for i in range(NSLOT)]
        slot_rot = [0]

        ds_setup = mkds("ds_setup")
        setup_items = []
        for (t, d) in [(ident, ident_d), (gcols, gcols_d), (pscale, pscale_d), (convw, convw_d),
                       (sguw_f, sguwT_d), (sgu00, sgu00_d), (cos_s, coss_d), (sin_s, sins_d), (maskT, maskT_d), (qdec, qdec_d),
                       (kdec, kdec_d), (tri, tri_d), (invc0, invc0_d), (eye16, eye16_d)]:
            setup_items.append((t if t is sguw_f else t[:], d, [], [res("consts")]))
        dma_batch(sp, ds_setup, setup_items)
        ds_setup2 = mkds("ds_setup2")
        dma_batch(pool, ds_setup2, [
            (identb[:], ident_d, [], [res("consts2")]),
            (poolw[:], pool_w.rearrange("l g c e -> c (l g) e"), [], [res("consts2")]),
        ])
        RC = [res("consts"), res("consts2"), res("consts3")]
        dve.op(lambda e: e.memset(ones_b[:], 1.0), [], [res("consts3")])
        dve.op(lambda e: e.memset(epsc[:], EPS), [], [res("consts3")])
        dve.op(lambda e: e.memset(Sst[:], 0.0), [], [res("S0"), res("S1")])
        dve.op(lambda e: e.memset(Sbf[:], 0.0), [], [res("Sbf0"), res("Sbf1")])
        dve.op(lambda e: e.memset(carry_a[:], 0.0), [], [res("ca0"), res("ca1")])
        dve.op(lambda e: e.memset(carry_z[:], 0.0), [], [res("cz0"), res("cz1")])
        dve.op(lambda e: e.tensor_tensor(WsT[:].rearrange("p l h t -> p (l h) t"),
                                         sguw_f.rearrange("p l h t -> p (l h) t"),
                                         tri[:].unsqueeze(1).to_broadcast([128, NL * 4, 128]), ALU.mult),
               RC, [res("consts3")])

        def load_w(view_shape, src):
            i = slot_rot[0] % NSLOT
            slot_rot[0] += 1
            n = int(np.prod(view_shape[1:]))
            v = wslot[i][:, 0:n]
            if len(view_shape) == 3:
                v = v.rearrange("p (a b) -> p a b", a=view_shape[1])
            elif len(view_shape) == 4:
                v = v.rearrange("p (a b c) -> p a b c", a=view_shape[1], b=view_shape[2])
            pool.dma(slot_ds[i], v, src, [], [R_slot[i]])
            return v, R_slot[i]

        def rmsnorm(n, gidx, out_xn=True, c0=0):
            cs = slice(c0, c0 + n)
            ss, Rss = nxt("a")
            for k in range(16):
                s = sqb[k % 2]
                Rs = res(f"sqb{k % 2}")
                act.op(lambda e, s=s, k=k: e.activation(s[:, :n], hT[:, k, cs], AF.Square), [R_h[k]], [Rs])
                pe.op(lambda e, s=s, k=k: e.matmul(ss[:, :n], ones_b[:], s[:, :n], start=(k == 0), stop=(k == 15)),
                      [Rs, res("consts3")], [Rss], inc=True)
            act.op(lambda e: e.activation(rstd[:, :n], ss[:, :n], AF.Sqrt, bias=epsc[:], scale=1.0 / D),
                   [Rss, res("consts3")], [res("rstd")])
            dve.op(lambda e: e.reciprocal(rstd[:, :n], rstd[:, :n]), [], [res("rstd")])
            for k in range(16):
                if out_xn:
                    dve.op(lambda e, k=k: e.scalar_tensor_tensor(xnT[:, k, cs], hT[:, k, cs], gcols[:, gidx, k:k + 1],
                                                                 rstd[:, :n], ALU.mult, ALU.mult),
                           [R_h[k], res("rstd"), res("consts")], [R_xn[k]])
                else:
                    dve.op(lambda e, k=k: e.scalar_tensor_tensor(hT[:, k, cs], hT[:, k, cs], gcols[:, gidx, k:k + 1],
                                                                 rstd[:, :n], ALU.mult, ALU.mult),
                           [res("rstd"), res("consts")], [R_h[k]])

        def load_ltab(l):
            dma_batch(sp, ds_lt, [(retg[:], retg_d[:, l, :], [], [res("ltab")]),
                                  (sgug[:], sgug_d[:, l, :], [], [res("ltab")]),
                                  (sgub[:], sgub_d[:, l, :, :], [], [res("ltab")])])

        def load_x(n, src_rows):
            for (src, nt, c0) in src_rows:
                bi = xrot[0] % 3
                xrot[0] += 1
                xin, Rx = xins[bi], res(f"xin{bi}")
                sp.dma(ds_xb[bi], xin[:nt, :], src, [], [Rx])
                for k4 in range(4):
                    pt, Rpt = nxt("a")
                    for kk in range(4):
                        k = k4 * 4 + kk
                        pe.op(lambda e, k=k, kk=kk: e.transpose(pt[:, kk * 128:kk * 128 + nt], xin[:nt, k * 128:(k + 1) * 128],
                                                                ident[:nt, :nt]),
                              [Rx, res("consts")], [Rpt], inc=(kk == 3))
                    act.op(lambda e, k4=k4: e.copy(hT[:, k4 * 4:k4 * 4 + 4, c0:c0 + nt],
                                                   pt[:, 0:512].rearrange("p (a b) -> p a b", a=4)[:, :, :nt]),
                           [Rpt], [R_h[k4 * 4 + kk] for kk in range(4)])

        def store_y(n, dst_rows):
            for (dst, nt, c0) in dst_rows:
                bi = xrot[0] % 3
                xrot[0] += 1
                xin, Rx = xins[bi], res(f"xin{bi}")
                for k4 in range(4):
                    pt, Rpt = nxt("a")
                    for kk in range(4):
                        k = k4 * 4 + kk
                        pe.op(lambda e, k=k, kk=kk: e.transpose(pt[:nt, kk * 128:(kk + 1) * 128], hT[:, k, c0:c0 + nt], ident[:]),
                              [R_h[k], res("consts")], [Rpt], inc=(kk == 3))
                    act.op(lambda e, k4=k4: e.copy(xin[:nt, k4 * 512:(k4 + 1) * 512], pt[:nt, :]), [Rpt], [Rx])
                sp.dma(ds_xb[bi], dst, xin[:nt, :], [Rx], [])

        def wout_and_ffn(l, n, ns=0):
            c1 = G
            if ns:
                pQ, RpQ = nxt("a")
            for dp in range(8):
                wv, Rw = load_w([128, 16, 256], w_out[l, :, dp * 256:(dp + 1) * 256].rearrange("(k p) c -> p k c", p=128))
                for dd in range(2):
                    d = dp * 2 + dd
                    ps, Rps = nxt("m")
                    for k in range(16):
                        pe.op(lambda e, k=k, dd=dd: e.matmul(ps[:, :n], wv[:, k, dd * 128:(dd + 1) * 128], mixT[:, k, :n],
                                                             start=(k == 0), stop=(k == 15)),
                              [Rw, R_mix[k]], [Rps], inc=(k == 15))
                    dve.op(lambda e, d=d: e.tensor_tensor(hT[:, d, :n], hT[:, d, :n], ps[:, :n], ALU.add), [Rps], [R_h[d]])
                    if ns:
                        for k in range(16):
                            pe.op(lambda e, k=k, dd=dd, d=d: e.matmul(pQ[:, d * ns:(d + 1) * ns], wv[:, k, dd * 128:(dd + 1) * 128],
                                                                     mixT[:, k, c1:c1 + ns], start=(k == 0), stop=(k == 15)),
                                  [Rw, R_mix[k]], [RpQ], inc=(k == 15))
            if ns:
                dve.op(lambda e: e.tensor_tensor(hT[:, :, c1:c1 + ns], hT[:, :, c1:c1 + ns],
                                                 pQ[:, 0:16 * ns].rearrange("p (a b) -> p a b", a=16), ALU.add), [RpQ], R_h)
            rmsnorm(n, 2 * l + 1)
            if ns:
                rmsnorm(ns, 2 * l + 1, c0=c1)
            for jp in range(NF // 2):
                wg, Rwg = load_w([128, 16, 256], w_gu[l, :, jp * 256:(jp + 1) * 256].rearrange("(k p) c -> p k c", p=128))
                wu, Rwu = load_w([128, 16, 256], w_gu[l, :, FF + jp * 256:FF + (jp + 1) * 256].rearrange("(k p) c -> p k c", p=128))
                pgs = [nxt("m"), nxt("m")]
                for jj in range(2):
                    pg, Rpg = pgs[jj]
                    for k in range(16):
                        pe.op(lambda e, k=k, jj=jj, pg=pg: e.matmul(pg[:, :n], wg[:, k, jj * 128:(jj + 1) * 128], xnT[:, k, :n],
                                                                  start=(k == 0), stop=(k == 15)),
                              [Rwg, R_xn[k]], [Rpg], inc=(k == 15))
                pus = [nxt("m"), nxt("m")]
                for jj in range(2):
                    pu, Rpu = pus[jj]
                    for k in range(16):
                        pe.op(lambda e, k=k, jj=jj, pu=pu: e.matmul(pu[:, :n], wu[:, k, jj * 128:(jj + 1) * 128], xnT[:, k, :n],
                                                                  start=(k == 0), stop=(k == 15)),
                              [Rwu, R_xn[k]], [Rpu], inc=(k == 15))
                for jj in range(2):
                    j = jp * 2 + jj
                    pg, Rpg = pgs[jj]
                    pu, Rpu = pus[jj]
                    sgb, Rsg = (cacc, res("cacc")) if jj == 0 else (rstd, res("rstd"))
                    act.op(lambda e, pg=pg, sgb=sgb: e.activation(sgb[:, :n], pg[:, :n], AF.Silu), [Rpg], [Rsg])
                    dve.op(lambda e, j=j, pu=pu, sgb=sgb: e.tensor_tensor(actT[:, j, :n], pu[:, :n], sgb[:, :n], ALU.mult),
                           [Rpu, Rsg], [R_act[j]])
                if ns:
                    for jj in range(2):
                        j = jp * 2 + jj
                        q16 = j % 16
                        if q16 == 0:
                            (pgS, RpgS), (puS, RpuS) = nxt("a"), nxt("a")
                            sstate["g"] = (pgS, RpgS, puS, RpuS)
                        pgS, RpgS, puS, RpuS = sstate["g"]
                        for k in range(16):
                            pe.op(lambda e, k=k, jj=jj, q16=q16, pgS=pgS: e.matmul(pgS[:, q16 * ns:(q16 + 1) * ns], wg[:, k, jj * 128:(jj + 1) * 128],
                                                                                 xnT[:, k, c1:c1 + ns], start=(k == 0), stop=(k == 15)),
                                  [Rwg, R_xn[k]], [RpgS], inc=(k == 15))
                        for k in range(16):
                            pe.op(lambda e, k=k, jj=jj, q16=q16, puS=puS: e.matmul(puS[:, q16 * ns:(q16 + 1) * ns], wu[:, k, jj * 128:(jj + 1) * 128],
                                                                                 xnT[:, k, c1:c1 + ns], start=(k == 0), stop=(k == 15)),
                                  [Rwu, R_xn[k]], [RpuS], inc=(k == 15))
                        if q16 == 15 or j == NF - 1:
                            cnt = q16 + 1
                            j0 = j - q16
                            act.op(lambda e, pgS=pgS, cnt=cnt: e.activation(vnf[:, :cnt * ns], pgS[:, :cnt * ns], AF.Silu), [RpgS], [res("lnout")])
                            dve.op(lambda e, puS=puS, cnt=cnt, j0=j0: e.tensor_tensor(
                                actT[:, j0:j0 + cnt, c1:c1 + ns], puS[:, :cnt * ns].rearrange("p (a b) -> p a b", a=cnt),
                                vnf[:, :cnt * ns].rearrange("p (a b) -> p a b", a=cnt), ALU.mult),
                                [RpuS, res("lnout")], [R_act[jx] for jx in range(j0, j0 + cnt)])
            fsegs = [(0, 16), (16, 32), (32, 44)]
            if ns:
                pQd = [nxt("a"), nxt("a")]
            for dp in range(8):
                psd = [nxt("m"), nxt("m")]
                for si, (f0, f1) in enumerate(fsegs):
                    nf = f1 - f0
                    src = w_dn[l, f0 * 128:f1 * 128, dp * 256:(dp + 1) * 256].rearrange("(j p) c -> p j c", p=128)
                    wv, Rw = load_w([128, nf, 256], src)
                    for dd in range(2):
                        ps, Rps = psd[dd]
                        for jj in range(nf):
                            j = f0 + jj
                            pe.op(lambda e, jj=jj, j=j, dd=dd, ps=ps: e.matmul(ps[:, :n], wv[:, jj, dd * 128:(dd + 1) * 128],
                                                                             actT[:, j, :n], start=(j == 0), stop=(j == NF - 1)),
                                  [Rw, R_act[j]], [Rps], inc=(jj == nf - 1))
                    if ns:
                        for dd in range(2):
                            pq_, Rpq_ = pQd[dd]
                            for jj in range(nf):
                                j = f0 + jj
                                pe.op(lambda e, jj=jj, j=j, dd=dd, pq_=pq_, dp=dp: e.matmul(
                                    pq_[:, dp * ns:(dp + 1) * ns], wv[:, jj, dd * 128:(dd + 1) * 128], actT[:, j, c1:c1 + ns],
                                    start=(j == 0), stop=(j == NF - 1)),
                                    [Rw, R_act[j]], [Rpq_], inc=(jj == nf - 1))
                for dd in range(2):
                    d = dp * 2 + dd
                    ps, Rps = psd[dd]
                    dve.op(lambda e, d=d, ps=ps: e.tensor_tensor(hT[:, d, :n], hT[:, d, :n], ps[:, :n], ALU.add), [Rps], [R_h[d]])
            if ns:
                for dd in range(2):
                    pq_, Rpq_ = pQd[dd]
                    for dp in range(8):
                        d = dp * 2 + dd
                        dve.op(lambda e, d=d, dp=dp, pq_=pq_: e.tensor_tensor(hT[:, d, c1:c1 + ns], hT[:, d, c1:c1 + ns],
                                                                            pq_[:, dp * ns:(dp + 1) * ns], ALU.add), [Rpq_], [R_h[d]])

        sstate = {}

        def layernorm_heads(src_ps, npart, Rsrc, dst, nheads=4, width=128):
            for h in range(nheads):
                dve.op(lambda e, h=h: e.bn_stats(stt[:npart, h, :], src_ps[:npart, h * width:(h + 1) * width]), [Rsrc], [res("stt")])
            for h in range(nheads):
                dve.op(lambda e, h=h: e.bn_aggr(mv[:npart, h, :], stt[:npart, h, :]), [res("stt")], [res("mv")])
            act.op(lambda e: e.activation(sd[:npart, :nheads], mv[:npart, :nheads, 1], AF.Sqrt, bias=epsc[:npart, :], scale=1.0),
                   [res("mv"), res("consts3")], [res("sd")])
            dve.op(lambda e: e.reciprocal(sd[:npart, :nheads], sd[:npart, :nheads]), [], [res("sd")])
            dve.op(lambda e: e.scalar_tensor_tensor(nb[:npart, :nheads], mv[:npart, :nheads, 0], -1.0, sd[:npart, :nheads],
                                                    ALU.mult, ALU.mult), [res("mv"), res("sd")], [res("nb")])
            for h in range(nheads):
                dve.op(lambda e, h=h: e.tensor_scalar(dst[:npart, h * width:(h + 1) * width], src_ps[:npart, h * width:(h + 1) * width],
                                                      sd[:npart, h:h + 1], nb[:npart, h:h + 1], ALU.mult, ALU.add),
                       [Rsrc, res("sd"), res("nb")], [res("lnout")])

        def rotary(dst, src_ps, npart, cosv, sinv, Rsrc, Rdst, Rtab):
            s4 = src_ps[:npart, :].rearrange("p (h t e) -> p h t e", h=4, t=2)
            d4 = dst.rearrange("p (h t e) -> p h t e", h=4, t=2)
            cb = cosv.unsqueeze(1).to_broadcast([npart, 4, 64])
            sbb = sinv.unsqueeze(1).to_broadcast([npart, 4, 64])
            Rr = res("rt")
            dve.op(lambda e: e.tensor_tensor(rt[0][:npart], s4[:, :, 0, :], cb, ALU.mult), [Rsrc, Rtab], [Rr])
            dve.op(lambda e: e.tensor_tensor(rt[1][:npart], s4[:, :, 1, :], sbb, ALU.mult), [Rsrc], [Rr])
            dve.op(lambda e: e.tensor_tensor(d4[:, :, 0, :], rt[0][:npart], rt[1][:npart], ALU.subtract), [Rr], [Rdst])
            dve.op(lambda e: e.tensor_tensor(rt[0][:npart], s4[:, :, 1, :], cb, ALU.mult), [Rsrc], [Rr])
            dve.op(lambda e: e.tensor_tensor(rt[1][:npart], s4[:, :, 0, :], sbb, ALU.mult), [Rsrc], [Rr])
            dve.op(lambda e: e.tensor_tensor(d4[:, :, 1, :], rt[0][:npart], rt[1][:npart], ALU.add), [Rr], [Rdst])

        def prompt_group(g, ws):
            n = G
            c1 = G
            dma_batch(sp, ds_cs, [(cos_g[:], cos_d[:, g * 4:(g + 1) * 4, :], [], [res("cs")]),
                                  (sin_g[:], sin_d[:, g * 4:(g + 1) * 4, :], [], [res("cs")])])
            load_x(n, [(xp[g * G + c * 128:g * G + (c + 1) * 128, :], 128, c * 128) for c in range(4)])
            if ws:
                load_x(NS, [(xs, NS, c1)])
            for l in range(NL):
                load_ltab(l)
                rmsnorm(n, 2 * l)
                if ws:
                    rmsnorm(NS, 2 * l, c0=c1)
                    mlimit[0] = 3
                RS, RSb = res(f"S{l}"), res(f"Sbf{l}")
                Rca, Rcz = res(f"ca{l}"), res(f"cz{l}")

                def fm_block(cb, consume):
                    if ws:
                        pS, RpS = pm[3], R_pm[3]
                    for half in range(2):
                        src = w_in[l, :, cb * GW + half * 256: cb * GW + (half + 1) * 256].rearrange("(k p) c -> p k c", p=128)
                        wv, Rw = load_w([128, 16, 256], src)
                        for ee in range(2):
                            ci = half * 2 + ee
                            ps, Rps = nxt("m")
                            for k in range(16):
                                pe.op(lambda e, k=k, ee=ee: e.matmul(ps[:, :n], wv[:, k, ee * 128:(ee + 1) * 128], xnT[:, k, :n],
                                                                     start=(k == 0), stop=(k == 15)),
                                      [Rw, R_xn[k]], [Rps], inc=(k == 15))
                            consume(ci, ps, Rps)
                            if ws:
                                for k in range(16):
                                    pe.op(lambda e, k=k, ee=ee, ci=ci: e.matmul(pS[:, ci * NS:(ci + 1) * NS], wv[:, k, ee * 128:(ee + 1) * 128],
                                                                              xnT[:, k, c1:c1 + NS], start=(k == 0), stop=(k == 15)),
                                          [Rw, R_xn[k]], [RpS], inc=(k == 15))
                            after_group()
                    if ws:
                        act.op(lambda e: e.copy(pFM[:, cb * 4:(cb + 1) * 4, :], pS[:, 0:4 * NS].rearrange("p (a b) -> p a b", a=4)),
                               [RpS], [res(f"pfm{cb}")])

                def tm_block(cb, consume):
                    wvs = []
                    for half in range(2):
                        src = w_in[l, half * 1024:(half + 1) * 1024, cb * GW:(cb + 1) * GW].rearrange("(k p) c -> p k c", p=128)
                        wvs.append(load_w([128, 8, GW], src))
                    if not ws:
                        for half in range(2):
                            wv, Rw = wvs[half]
                            for c in range(4):
                                ps, Rps = pm[c], R_pm[c]
                                for kk in range(8):
                                    k = half * 8 + kk
                                    pe.op(lambda e, k=k, kk=kk, c=c, wv=wv, ps=ps: e.matmul(ps[:, :], xnT[:, k, c * 128:(c + 1) * 128], wv[:, kk, :],
                                                                                          start=(k == 0), stop=(k == 15)),
                                          [Rw, R_xn[k]], [Rps], inc=(kk == 7))
                                if half == 1:
                                    consume(c, ps, Rps)
                    else:
                        for c in range(4):
                            ps, Rps = nxt("m")
                            for k in range(16):
                                wv, Rw = wvs[k // 8]
                                pe.op(lambda e, k=k, c=c, wv=wv: e.matmul(ps[:, :], xnT[:, k, c * 128:(c + 1) * 128], wv[:, k % 8, :],
                                                                          start=(k == 0), stop=(k == 15)),
                                      [Rw, R_xn[k]], [Rps], inc=(k == 15))
                            consume(c, ps, Rps)
                    if ws:
                        pS, RpS = pm[3], R_pm[3]
                        for eb in range(4):
                            for k in range(16):
                                wv, Rw = wvs[k // 8]
                                pe.op(lambda e, k=k, eb=eb, wv=wv: e.matmul(pS[:, eb * NS:(eb + 1) * NS], wv[:, k % 8, eb * 128:(eb + 1) * 128],
                                                                          xnT[:, k, c1:c1 + NS], start=(k == 0), stop=(k == 15)),
                                      [Rw, R_xn[k]], [RpS], inc=(k == 15))
                        act.op(lambda e: e.copy(pFM[:, cb * 4:(cb + 1) * 4, :], pS[:, 0:4 * NS].rearrange("p (a b) -> p a b", a=4)),
                               [RpS], [res(f"pfm{cb}")])

                hooks = {"gen": None, "deferred": []}

                def after_group(flush=False):
                    keep = []
                    for (age, f) in hooks["deferred"]:
                        if age >= 2 or flush:
                            f()
                        else:
                            keep.append((age + 1, f))
                    hooks["deferred"] = keep
                    if hooks["gen"] is not None:
                        try:
                            next(hooks["gen"])
                        except StopIteration:
                            hooks["gen"] = None

                def c_q(c, ps, Rps):
                    rotary(qrot[:, c, :], ps, 128, cos_g[:, c, :], sin_g[:, c, :], Rps, res(f"qrot{c}"), res("cs"))

                def c_k(c, ps, Rps):
                    rotary(krot[:, c, :], ps, 128, cos_g[:, c, :], sin_g[:, c, :], Rps, res(f"krot{c}"), res("cs"))
                    dve.op(lambda e: e.tensor_tensor(kd[:, c, :].rearrange("p (h e) -> p h e", h=4),
                                                     krot[:, c, :].rearrange("p (h e) -> p h e", h=4),
                                                     kdec[:].unsqueeze(2).to_broadcast([128, 4, 128]), ALU.mult),
                           [res(f"krot{c}"), res("consts")], [res(f"kd{c}")])

                def c_v(c, ps, Rps):
                    act.op(lambda e: e.copy(vbf[:, c, :], ps[:, :]), [Rps], [res(f"vbf{c}")])

                def c_g(c, ps, Rps):
                    act.op(lambda e: e.activation(gs[:, c, :], ps[:, :], AF.Silu), [Rps], [res(f"gs{c}")])

                def c_vv(c, ps, Rps):
                    layernorm_heads(ps, 128, Rps, vnf, nheads=1, width=GW)
                    dve.op(lambda e: e.tensor_tensor(vnf[:, :], vnf[:, :], sgug[:, :], ALU.mult), [res("ltab")], [res("lnout")])
                    act.op(lambda e: e.copy(vnb[:, c, :], vnf[:, :]), [res("lnout")], [res(f"vnb{c}")])
                    if g == NG - 1 and c == 3:
                        sp.dma(ds_sg, sgu_p[l], vnf[:, :], [res("lnout")], [])

                tm_block(4, c_q)
                tm_block(5, c_k)
                tm_block(6, c_v)
                tm_block(7, c_g)
                tm_block(9, c_vv)

                def mixer_gen():
                    for c in range(4):
                        tsl = slice(c * 128, (c + 1) * 128)
                        pq, Rpq = nxt("b")
                        for h in range(4):
                            pe.op(lambda e, h=h: e.transpose(pq[:, h * 128:(h + 1) * 128], qrot[:, c, h * 128:(h + 1) * 128], identb[:]),
                                  [res(f"qrot{c}"), res("consts2")], [Rpq], inc=(h == 3))
                        dve.op(lambda e: e.tensor_copy(qT[:], pq[:, 0:512].rearrange("p (h t) -> p h t", h=4)), [Rpq], [res("qT")])
                        dve.op(lambda e: e.tensor_tensor(qdT[:], pq[:, 0:512].rearrange("p (h t) -> p h t", h=4), qdec[:], ALU.mult),
                               [Rpq, res("consts")], [res("qdT")])
                        pk, Rpk = nxt("b")
                        for h in range(4):
                            pe.op(lambda e, h=h: e.transpose(pk[:, h * 128:(h + 1) * 128], krot[:, c, h * 128:(h + 1) * 128], identb[:]),
                                  [res(f"krot{c}"), res("consts2")], [Rpk], inc=(h == 3))
                        act.op(lambda e: e.copy(kT[:], pk[:, 0:512].rearrange("p (h t) -> p h t", h=4)), [Rpk], [res("kT")])
                        yield
                        psc, Rpsc = nxt("a")
                        for h in range(4):
                            pe.op(lambda e, h=h: e.matmul(psc[:, h * 128:(h + 1) * 128], kT[:, h, :], qT[:, h, :], start=True, stop=True),
                                  [res("kT"), res("qT")], [Rpsc], inc=(h == 3))
                        dve.op(lambda e: e.tensor_tensor(scT[:], psc[:, :].rearrange("p (h t) -> p h t", h=4), maskT[:], ALU.mult),
                               [Rpsc, res("consts")], [res("scT")])
                        pkv, Rpkv = nxt("m")
                        for h in range(4):
                            hs = slice(h * 128, (h + 1) * 128)
                            pe.op(lambda e, h=h, hs=hs: e.matmul(pkv[:, hs], kd[:, c, hs], vbf[:, c, hs], start=True, stop=True),
                                  [res(f"kd{c}"), res(f"vbf{c}")], [Rpkv], inc=(h == 3))
                        yield
                        po, Rpo = nxt("a")
                        for h in range(4):
                            hs = slice(h * 128, (h + 1) * 128)
                            pe.op(lambda e, h=h, hs=hs: e.matmul(po[:, hs], scT[:, h, :], vbf[:, c, hs], start=True, stop=False),
                                  [res("scT"), res(f"vbf{c}")], [Rpo], inc=False)
                            pe.op(lambda e, h=h, hs=hs: e.matmul(po[:, hs], qdT[:, h, :], Sbf[:, l, h, :], start=False, stop=True),
                                  [res("qdT"), RSb], [Rpo], inc=(h == 3))
                        for h in range(4):
                            dve.op(lambda e, h=h: e.scalar_tensor_tensor(Sst[:, l, h, :], Sst[:, l, h, :], GAM[h] ** 128,
                                                                         pkv[:, h * 128:(h + 1) * 128], ALU.mult, ALU.add), [Rpkv], [RS])
                        act.op(lambda e: e.copy(Sbf[:, l, :, :], Sst[:, l, :, :]), [RS], [RSb])
                        layernorm_heads(po, 128, Rpo, onb)
                        dve.op(lambda e: e.tensor_tensor(onb[:, :], onb[:, :], retg[:, :], ALU.mult), [res("ltab")], [res("lnout")])
                        dve.op(lambda e: e.tensor_tensor(ycb[:, :], onb[:, :], gs[:, c, :], ALU.mult), [res("lnout"), res(f"gs{c}")], [res("ycb")])
                        yield
                        yield
                        pyc, Rpyc = nxt("b")
                        for h in range(4):
                            pe.op(lambda e, h=h: e.transpose(pyc[:, h * 128:(h + 1) * 128], ycb[:, h * 128:(h + 1) * 128], identb[:]),
                                  [res("ycb"), res("consts2")], [Rpyc], inc=(h == 3))
                        act.op(lambda e: e.copy(mixT[:, 8:12, tsl], pyc[:, 0:512].rearrange("p (h t) -> p h t", h=4)),
                               [Rpyc], [R_mix[8 + h] for h in range(4)])
                        pmx, Rpmx = nxt("a")
                        for h in range(4):
                            pe.op(lambda e, h=h: e.matmul(pmx[:, h * 128:(h + 1) * 128], vnb[:, c, h * 128:(h + 1) * 128], WsT[:, l, h, :],
                                                          start=True, stop=True), [res(f"vnb{c}"), res("consts3")], [Rpmx], inc=(h == 3))
                        dve.op(lambda e: e.tensor_tensor(tmp4[:], pmx[:, :].rearrange("p (h t) -> p h t", h=4), sgub[:, :, :], ALU.add),
                               [Rpmx, res("ltab")], [res("tmp4")])
                        dve.op(lambda e: e.tensor_tensor(mixT[:, 12:16, tsl], tmp4[:], u_sb[:, :, tsl], ALU.mult),
                               [res("tmp4"), res("u_sb")], [R_mix[12 + h] for h in range(4)])
                        yield

                hooks["gen"] = mixer_gen()

                def c_u(ci, ps, Rps):
                    act.op(lambda e: e.copy(u_sb[:, ci, :n], ps[:, :n]), [Rps], [res("u_sb")])

                fm_block(8, c_u)

                dve.op(lambda e: e.tensor_copy(aext[:, :, 0:15], carry_a[:, l, :, :]), [Rca], [res("aext")])

                def c_a(gi, ps, Rps):
                    act.op(lambda e: e.copy(aext[:, gi, 15:15 + n], ps[:, :n]), [Rps], [res("aext")])
                    cur = aext[:, gi, :]
                    L = 15 + n
                    sh = 1
                    for step in range(gi + 1):
                        o = ptmp[step % 2]
                        dve.op(lambda e, cur=cur, o=o, sh=sh: e.tensor_tensor(o[:, sh:L], cur[:, sh:L], cur[:, 0:L - sh], ALU.add),
                               [res("aext")] if step == 0 else [res(f"ptmp{(step - 1) % 2}")], [res(f"ptmp{step % 2}")])
                        cur = o
                        sh *= 2
                    Rcur = res(f"ptmp{gi % 2}")
                    dTg, RdT = dTs[gi % 3], res(f"dT{gi % 3}")
                    dve.op(lambda e, cur=cur: e.scalar_tensor_tensor(dTg[:, :n], cur[:, 15:15 + n], 1.0 / WINS[gi], aext[:, gi, 15:15 + n],
                                                                     ALU.mult, ALU.subtract), [Rcur, res("aext")], [RdT])
                    if g == 0:
                        dve.op(lambda e, cur=cur: e.tensor_tensor(ptmp[(gi + 1) % 2][:, 0:16], cur[:, 15:31], invc0[:, gi, :], ALU.mult),
                               [Rcur, res("consts")], [res(f"ptmp{(gi + 1) % 2}")])
                        dve.op(lambda e: e.tensor_tensor(dTg[:, 0:16], ptmp[(gi + 1) % 2][:, 0:16], aext[:, gi, 15:31], ALU.subtract),
                               [res(f"ptmp{(gi + 1) % 2}"), res("aext")], [RdT])

                    def pool_mm():
                        py, Rpy = nxt("m")
                        pe.op(lambda e: e.matmul(py[:, :n], poolw[:, l * 4 + gi, :], dTg[:, :n], start=True, stop=True),
                              [RdT, res("consts2")], [Rpy])
                        dve.op(lambda e: e.tensor_scalar(mixT[:, gi, :n], py[:, :n], pscale[:, l, gi:gi + 1], None, ALU.mult),
                               [Rpy, res("consts")], [R_mix[gi]])

                    hooks["deferred"].append((0, pool_mm))

                fm_block(0, c_a)
                dve.op(lambda e: e.tensor_copy(carry_a[:, l, :, :], aext[:, :, n:n + 15]), [res("aext")], [Rca])
                if g == NG - 1:
                    with nc.allow_non_contiguous_dma(reason="small state transpose"):
                        for gi in range(4):
                            sp.dma(ds_st, pool_p[l, :, gi * 128:(gi + 1) * 128].rearrange("t c -> c t"), carry_a[:, l, gi, :], [Rca], [],
                                   allow_slow_non_contiguous=True)

                dve.op(lambda e: e.tensor_copy(zext[:, :, 0:2], carry_z[:, l, :, :]), [Rcz], [res("zext")])

                def c_hh(ci, ps, Rps):
                    act.op(lambda e: e.copy(hh_sb[:, ci, :n], ps[:, :n]), [Rps], [res(f"hh{ci}")])

                fm_block(3, c_hh)

                def c_cg(ci, ps, Rps):
                    dve.op(lambda e: e.tensor_tensor(zext[:, ci, 2:2 + n], ps[:, :n], hh_sb[:, ci, :n], ALU.mult),
                           [Rps, res(f"hh{ci}")], [res("zext")])

                fm_block(2, c_cg)
                dve.op(lambda e: e.tensor_copy(carry_z[:, l, :, :], zext[:, :, n:n + 2]), [res("zext")], [Rcz])
                if g == NG - 1:
                    with nc.allow_non_contiguous_dma(reason="small state transpose"):
                        for gi in range(4):
                            sp.dma(ds_st, conv_p[l, :, gi * 128:(gi + 1) * 128].rearrange("t c -> c t"), carry_z[:, l, gi, :], [Rcz], [],
                                   allow_slow_non_contiguous=True)

                def c_bg(ci, ps, Rps):
                    dve.op(lambda e: e.tensor_scalar(cacc[:, :n], zext[:, ci, 0:n], convw[:, l, 0, ci:ci + 1], None, ALU.mult),
                           [res("zext"), res("consts")], [res("cacc")])
                    dve.op(lambda e: e.scalar_tensor_tensor(cacc[:, :n], zext[:, ci, 1:1 + n], convw[:, l, 1, ci:ci + 1], cacc[:, :n],
                                                            ALU.mult, ALU.add), [res("zext")], [res("cacc")])
                    dve.op(lambda e: e.scalar_tensor_tensor(cacc[:, :n], zext[:, ci, 2:2 + n], convw[:, l, 2, ci:ci + 1], cacc[:, :n],
                                                            ALU.mult, ALU.add), [res("zext")], [res("cacc")])
                    dve.op(lambda e: e.tensor_tensor(mixT[:, 4 + ci, :n], ps[:, :n], cacc[:, :n], ALU.mult),
                           [Rps, res("cacc")], [R_mix[4 + ci]])

                fm_block(1, c_bg)
                while hooks["gen"] is not None or hooks["deferred"]:
                    after_group(flush=True)
                if g == NG - 1:
                    sp.dma(ds_rp, ret_p[l].rearrange("h d e -> d h e"), Sst[:, l, :, :], [RS], [])
                mlimit[0] = 4
                if ws:
                    sample_mixers(l)
                wout_and_ffn(l, n, NS if ws else 0)
            rmsnorm(n, 4, out_xn=False)
            if ws:
                rmsnorm(NS, 4, out_xn=False, c0=c1)
            store_y(n, [(y_p[g * G + c * 128:g * G + (c + 1) * 128, :], 128, c * 128) for c in range(4)])
            if ws:
                store_y(NS, [(y_s, NS, c1)])

        def sample_mixers(l):
            n = NS
            roff = (0, 1, 4, 11)
            R_bigp = ([res("aext"), res("zext"), res("u_sb")] + [res(f"hh{i}") for i in range(4)]
                      + [res(f"vnb{i}") for i in range(4)] + [res(f"gs{i}") for i in range(4)])
            dma_batch(sp, ds_sst, [
                (s_cprev, st_conv[l], [], R_act + R_bigp),
                (s_convw, convwb_d[:, l], [], R_act),
            ] + [(s_prev4[:, roff[gi]:roff[gi] + WINS[gi] - 1, :], st_pool[l, :, 16 - WINS[gi]:15, gi * 128:(gi + 1) * 128], [], R_act)
                 for gi in range(4)])
            dma_batch(sp, ds_cp, [
                (pool_s[l, :, 0:14, :], st_pool[l, :, 1:15, :], [], []),
                (conv_s[l, :, 0:1, :], st_conv[l, :, 1:2, :], [], []),
            ])

            def tm_block(cb, consume):
                ps, Rps = nxt("m")
                for eb in range(4):
                    pe.op(lambda e, eb=eb: e.transpose(ps[:NS, eb * 128:(eb + 1) * 128], pFM[:, cb * 4 + eb, :], ident[:]),
                          [res(f"pfm{cb}"), res("consts")], [Rps], inc=(eb == 3))
                consume(ps, Rps)

            def to_mix(src, Rsrc, col0):
                dve.op(lambda e: e.tensor_copy(s_mix[:, col0:col0 + GW], src), [Rsrc], [res("s_mix")])

            def c_a(ps, Rps):
                act.op(lambda e: e.copy(s_a[:, :], ps[:n, :]), [Rps], [res("s_a")])
                sp.dma(ds_sa, pool_s[l, :, 14, :], s_a[:, :], [res("s_a")], [])
                for gi, w in enumerate(WINS):
                    cs = slice(gi * 128, (gi + 1) * 128)
                    if w > 2:
                        dve.op(lambda e, cs=cs, w=w, gi=gi: e.tensor_reduce(
                            s_d[:, cs], s_prev4[:, roff[gi]:roff[gi] + w - 1, :].rearrange("p t c -> p c t"), AX.X, ALU.add),
                            [R_act[0]], [res("s_d")])
                    else:
                        dve.op(lambda e, cs=cs: e.tensor_copy(s_d[:, cs], s_prev4[:, 0, :]), [R_act[0]], [res("s_d")])
                    dve.op(lambda e, cs=cs: e.tensor_tensor(s_d[:, cs], s_d[:, cs], s_a[:, cs], ALU.add), [res("s_a")], [res("s_d")])
                    dve.op(lambda e, cs=cs, w=w: e.scalar_tensor_tensor(s_d[:, cs], s_d[:, cs], 1.0 / w, s_a[:, cs], ALU.mult, ALU.subtract),
                           [res("s_a")], [res("s_d")])
                pt, Rpt = nxt("a")
                for gi in range(4):
                    pe.op(lambda e, gi=gi: e.transpose(pt[:, gi * NS:(gi + 1) * NS], s_d[:, gi * 128:(gi + 1) * 128], ident[:NS, :NS]),
                          [res("s_d"), res("consts")], [Rpt], inc=(gi == 3))
                act.op(lambda e: e.copy(dT[:, 0:4 * NS], pt[:, 0:4 * NS]), [Rpt], [res("dT0")])
                py, Rpy = nxt("a")
                for gi in range(4):
                    pe.op(lambda e, gi=gi: e.matmul(py[:, gi * NS:(gi + 1) * NS], poolw[:, l * 4 + gi, :], dT[:, gi * NS:(gi + 1) * NS],
                                                    start=True, stop=True), [res("dT0"), res("consts2")], [Rpy], inc=(gi == 3))
                for gi in range(4):
                    dve.op(lambda e, gi=gi: e.tensor_scalar(mixT[:, gi, G:G + NS], py[:, gi * NS:(gi + 1) * NS], pscale[:, l, gi:gi + 1], None, ALU.mult),
                           [Rpy, res("consts")], [R_mix[gi]])

            tm_block(0, c_a)

            def c_hh(ps, Rps):
                act.op(lambda e: e.copy(s_hh[:, :], ps[:n, :]), [Rps], [res("s_hh")])

            tm_block(3, c_hh)

            def c_cg(ps, Rps):
                dve.op(lambda e: e.tensor_tensor(s_hh[:, :], ps[:n, :], s_hh[:, :], ALU.mult), [Rps], [res("s_hh")])
                sp.dma(ds_sz, conv_s[l, :, 1, :], s_hh[:, :], [res("s_hh")], [])

            tm_block(2, c_cg)

            def c_bg(ps, Rps):
                dve.op(lambda e: e.tensor_tensor(s_t[:, :], s_cprev[:, 0, :], s_convw[:, 0, :], ALU.mult),
                       [R_act[0]], [res("s_t")])
                dve.op(lambda e: e.tensor_tensor(s_d[:, :], s_cprev[:, 1, :], s_convw[:, 1, :], ALU.mult),
                       [R_act[0]], [res("s_d")])
                dve.op(lambda e: e.tensor_tensor(s_t[:, :], s_t[:, :], s_d[:, :], ALU.add), [res("s_d")], [res("s_t")])
                dve.op(lambda e: e.tensor_tensor(s_d[:, :], s_hh[:, :], s_convw[:, 2, :], ALU.mult), [res("s_hh"), R_act[0]], [res("s_d")])
                dve.op(lambda e: e.tensor_tensor(s_t[:, :], s_t[:, :], s_d[:, :], ALU.add), [res("s_d")], [res("s_t")])
                dve.op(lambda e: e.tensor_tensor(s_mix[:, 0:GW], ps[:n, :], s_t[:, :], ALU.mult), [Rps, res("s_t")], [res("s_mix")])

            tm_block(1, c_bg)

            def c_u(ps, Rps):
                act.op(lambda e: e.copy(s_u[:, :], ps[:n, :]), [Rps], [res("s_u")])

            tm_block(8, c_u)

            def c_q(ps, Rps):
                rotary(s_q[:, :], ps, NS, cos_s[:, :], sin_s[:, :], Rps, res("s_q"), res("consts"))

            def c_k(ps, Rps):
                rotary(s_k[:, :], ps, NS, cos_s[:, :], sin_s[:, :], Rps, res("s_k"), res("consts"))
                dve.op(lambda e: e.tensor_scalar(s_k[:, :], s_k[:, :], 128.0 ** -0.5, None, ALU.mult), [], [res("s_k")])

            def c_v(ps, Rps):
                act.op(lambda e: e.copy(s_v[:, :], ps[:n, :]), [Rps], [res("s_v")])

            def c_g(ps, Rps):
                act.op(lambda e: e.activation(s_g[:, :], ps[:n, :], AF.Silu), [Rps], [res("s_g")])

            def c_vv(ps, Rps):
                layernorm_heads(ps, NS, Rps, vnf, nheads=1, width=GW)
                dve.op(lambda e: e.tensor_tensor(vnf[:n, :], vnf[:n, :], sgug[:NS, :], ALU.mult), [res("ltab")], [res("lnout")])
                sp.dma(ds_sv, sgu_s[l], vnf[:n, :], [res("lnout")], [])
                for h in range(4):
                    hs = slice(h * 128, (h + 1) * 128)
                    dve.op(lambda e, h=h, hs=hs: e.tensor_scalar(s_t[:, hs], vnf[:n, hs], sgu00[:, l, 0, h:h + 1], sgu00[:, l, 1, h:h + 1],
                                                                 ALU.mult, ALU.add), [res("lnout"), res("consts")], [res("s_t")])
                dve.op(lambda e: e.tensor_tensor(s_mix[:, 2 * GW:3 * GW], s_t[:, :], s_u[:, :], ALU.mult),
                       [res("s_t"), res("s_u")], [res("s_mix")])

            tm_block(4, c_q)
            tm_block(5, c_k)
            tm_block(6, c_v)
            tm_block(7, c_g)
            tm_block(9, c_vv)

            dve.op(lambda e: e.tensor_tensor(s_t[:, :], s_q[:, :], s_k[:, :], ALU.mult), [res("s_q"), res("s_k")], [res("s_t")])
            dve.op(lambda e: e.tensor_reduce(s_sc[:, :], s_t[:, :].rearrange("p (h e) -> p h e", h=4), AX.X, ALU.add),
                   [res("s_t")], [res("s_sc")])
            pt, Rpt = nxt("a")
            for h in range(4):
                pe.op(lambda e, h=h: e.transpose(pt[:, h * NS:(h + 1) * NS], s_q[:, h * 128:(h + 1) * 128], ident[:NS, :NS]),
                      [res("s_q"), res("consts")], [Rpt], inc=(h == 3))
            act.op(lambda e: e.copy(s_qT[:], pt[:, 0:4 * NS].rearrange("p (h b) -> p h b", h=4)), [Rpt], [res("s_qT")])
            poS, RpoS = nxt("a")
            NQ = NS // 4
            RSq = [res("S_q0"), res("S_q1")]

            def s_load(qb):
                bi = qb % 2
                extra = (R_act + R_bigp) if qb < 2 else []
                sp.dma(ds_sSq[bi], S_qs[bi], st_ret[l, qb * 4:qb * 4 + 4].rearrange("b h d e -> d b h e"), [], [RSq[bi]] + extra)

            s_load(0)
            for qb in range(NQ):
                b0 = qb * 4
                bi = qb % 2
                S_half, RSh = S_qs[bi], RSq[bi]
                if qb + 1 < NQ:
                    s_load(qb + 1)
                for bb in range(4):
                    b = b0 + bb
                    for h in range(4):
                        pe.op(lambda e, b=b, bb=bb, h=h, S_half=S_half: e.matmul(poS[:, h * NS + b:h * NS + b + 1], S_half[:, bb, h, :],
                                                                                s_qT[:, h, b:b + 1], start=True, stop=True),
                              [res("s_qT"), RSh], [RpoS], inc=(bb == 3 and h == 3))
                for bb in range(4):
                    b = b0 + bb
                    kdg = s_kdiag[0]
                    Rk = res("s_kdiag0")
                    dve.op(lambda e, b=b, kdg=kdg: e.tensor_scalar(kdg[:, :], s_k[:, :], eye16[:, b:b + 1], None, ALU.mult),
                           [res("s_k"), res("consts")], [Rk])
                    pkv, Rpkv = nxt("m")
                    for h in range(4):
                        hs = slice(h * 128, (h + 1) * 128)
                        pe.op(lambda e, hs=hs, kdg=kdg: e.matmul(pkv[:, hs], kdg[:, hs], s_v[:, hs], start=True, stop=True),
                              [Rk, res("s_v")], [Rpkv], inc=(h == 3))
                    for h in range(4):
                        dve.op(lambda e, bb=bb, h=h, S_half=S_half: e.scalar_tensor_tensor(S_half[:, bb, h, :], S_half[:, bb, h, :], GAM[h],
                                                                                          pkv[:, h * 128:(h + 1) * 128], ALU.mult, ALU.add),
                               [Rpkv], [RSh])
                sp.dma(ds_sSq[bi], ret_s[l, b0:b0 + 4].rearrange("b h d e -> d b h e"), S_half,
                       [RSh] + (R_act if qb >= NQ - 2 else []), [])
            act.op(lambda e: e.copy(s_oT[:], poS[:, 0:4 * NS].rearrange("p (h b) -> p h b", h=4)), [RpoS], [res("s_oT")])
            po, Rpo = nxt("a")
            for h in range(4):
                pe.op(lambda e, h=h: e.transpose(po[:NS, h * 128:(h + 1) * 128], s_oT[:, h, :], ident[:]),
                      [res("s_oT"), res("consts")], [Rpo], inc=(h == 3))
            for h in range(4):
                hs = slice(h * 128, (h + 1) * 128)
                dve.op(lambda e, h=h, hs=hs: e.tensor_scalar(s_t[:, hs], s_v[:, hs], s_sc[:, h:h + 1], None, ALU.mult),
                       [res("s_v"), res("s_sc")], [res("s_t")])
                dve.op(lambda e, h=h, hs=hs: e.scalar_tensor_tensor(onb[:NS, hs], po[:NS, hs], GAM[h], s_t[:, hs], ALU.mult, ALU.add),
                       [Rpo, res("s_t")], [res("lnout")])
            layernorm_heads(onb, NS, res("lnout"), onb)
            dve.op(lambda e: e.tensor_tensor(onb[:NS, :], onb[:NS, :], retg[:NS, :], ALU.mult), [res("ltab")], [res("lnout")])
            dve.op(lambda e: e.tensor_tensor(s_mix[:, GW:2 * GW], onb[:NS, :], s_g[:, :], ALU.mult),
                   [res("lnout"), res("s_g")], [res("s_mix")])
            for m4 in range(1, 4):
                pt, Rpt = nxt("a")
                for kk in range(4):
                    m = m4 * 4 + kk
                    pe.op(lambda e, m=m, kk=kk: e.transpose(pt[:, kk * NS:(kk + 1) * NS], s_mix[:, (m - 4) * 128:(m - 3) * 128], ident[:NS, :NS]),
                          [res("s_mix"), res("consts")], [Rpt], inc=(kk == 3))
                act.op(lambda e, m4=m4: e.copy(mixT[:, m4 * 4:m4 * 4 + 4, G:G + NS], pt[:, 0:4 * NS].rearrange("p (a b) -> p a b", a=4)),
                       [Rpt], [R_mix[m4 * 4 + kk] for kk in range(4)])

        ds_xb = [mkds(f"ds_x{i}") for i in range(3)]
        ds_lt = mkds("ds_lt")
        ds_cs = mkds("ds_cs")
        ds_st = mkds("ds_st")
        ds_sg = mkds("ds_sg")
        ds_rp = mkds("ds_rp")
        ds_sst = mkds("ds_sst")
        ds_cp = mkds("ds_cp")
        ds_sa = mkds("ds_sa")
        ds_sz = mkds("ds_sz")
        ds_sv = mkds("ds_sv")
        ds_sSq = [mkds("ds_sS0"), mkds("ds_sS1")]

        for g in range(NG):
            prompt_group(g, g == NG - 1)

        for d in dsems:
            if d.cnt:
                nc.sync.wait_ge(d.sem, d.cnt)
    return nc


def _tables():
    f32 = np.float32
    half = 64
    inv = (np.float32(10000.0) ** (-np.arange(half, dtype=f32) / f32(half))).astype(f32)
    pos = np.arange(SEQ, dtype=f32)
    ang = pos[:, None] * inv[None, :]
    cos_t = np.cos(ang).astype(f32).reshape(16, 128, 64).transpose(1, 0, 2)
    sin_t = np.sin(ang).astype(f32).reshape(16, 128, 64).transpose(1, 0, 2)
    angs = (np.full((NS,), PAST, dtype=f32)[:, None] * inv[None, :]).astype(f32)
    cos_s = np.cos(angs).astype(f32)
    sin_s = np.sin(angs).astype(f32)
    lg = np.log1p(-(2.0 ** (-5.0 - np.arange(4, dtype=np.float64))))
    idx = np.arange(128, dtype=np.float64)
    diff = idx[None, :] - idx[:, None]
    s = 128.0 ** -0.5
    maskT = np.where(diff[:, None, :] >= 0, np.exp(np.maximum(diff, 0.0)[:, None, :] * lg[None, :, None]), 0.0) * s
    qdec = np.broadcast_to(np.exp((idx + 1.0)[None, None, :] * lg[None, :, None]), (128, 4, 128))
    kdec = np.exp((127.0 - idx)[:, None] * lg[None, :]) * s
    tri = (idx[:, None] <= idx[None, :]).astype(f32)
    invc0 = np.zeros((128, 4, 16), f32)
    for gi, w in enumerate(WINS):
        invc0[:, gi, :] = 1.0 / np.minimum(np.arange(16) + 1, w)
    return dict(cos_t=np.ascontiguousarray(cos_t), sin_t=np.ascontiguousarray(sin_t), cos_s=cos_s, sin_s=sin_s,
                maskT=maskT.astype(f32), qdec=np.ascontiguousarray(qdec).astype(f32), kdec=kdec.astype(f32), tri=tri,
                ident=np.eye(128, dtype=f32), invc0=invc0, eye16=np.eye(NS, dtype=f32))


_NC = None


def kernel(x_prompt, x_sample, state_pool, state_conv, state_ret, norm1_g, w_in, pool_w, pool_scale, conv_w, ret_norm_g,
           sgu_norm_g, sgu_w, sgu_b, w_out, norm2_g, w_gate_up, w_down, final_norm_g):
    global _NC
    f32 = np.float32
    A = lambda a: np.ascontiguousarray(np.asarray(a, dtype=f32))
    x_prompt, x_sample, state_pool, state_conv, state_ret = map(A, (x_prompt, x_sample, state_pool, state_conv, state_ret))
    w_in, w_out, w_gate_up, w_down, pool_w = map(A, (w_in, w_out, w_gate_up, w_down, pool_w))
    norm1_g, norm2_g, final_norm_g, pool_scale, conv_w = map(A, (norm1_g, norm2_g, final_norm_g, pool_scale, conv_w))
    ret_norm_g, sgu_norm_g, sgu_w, sgu_b = map(A, (ret_norm_g, sgu_norm_g, sgu_w, sgu_b))
    if _NC is None:
        _NC = build_program()
    nc = _NC
    gl = np.stack([norm1_g[0], norm2_g[0], norm1_g[1], norm2_g[1], final_norm_g])
    gcols = A(gl.reshape(5, 16, 128).transpose(2, 0, 1))
    pscale = A(pool_scale.reshape(NL, 4, 128).transpose(2, 0, 1))
    convw = A(conv_w.reshape(NL, 3, 4, 128).transpose(3, 0, 1, 2))
    convwb = A(np.broadcast_to(conv_w[None], (NS, NL, 3, GW)))
    retg = A(np.broadcast_to(ret_norm_g[None], (128, NL, GW)))
    sgug = A(np.broadcast_to(sgu_norm_g[None], (128, NL, GW)))
    sguwT = A(sgu_w.transpose(3, 0, 1, 2))
    sgub = A(np.broadcast_to(sgu_b[None], (128, NL, 4, 128)))
    sgu00 = A(np.broadcast_to(np.stack([sgu_w[:, :, 0, 0], sgu_b[:, :, 0]], axis=1)[None], (NS, NL, 2, 4)))
    tabs = _tables()
    shared = dict(w_in=w_in, w_out=w_out, w_gu=w_gate_up, w_dn=w_down, pool_w=pool_w, gcols=gcols, pscale=pscale, convw=convw,
                  convwb=convwb, retg=retg, sgug=sgug, sguwT=sguwT, sgub=sgub, sgu00=sgu00, **tabs)
    in_maps = [None] * 8
    zero_map = None
    for i, c in enumerate(WORK):
        s0 = i * NS
        m = dict(shared)
        m.update(xp=x_prompt[i], xs=A(x_sample[s0:s0 + NS, 0, :]), st_pool=A(state_pool[:, s0:s0 + NS]),
                 st_conv=A(state_conv[:, s0:s0 + NS]), st_ret=A(state_ret[:, s0:s0 + NS]))
        in_maps[c] = m
        if zero_map is None:
            zero_map = {k: np.zeros_like(v) for k, v in m.items()}
    for c in range(8):
        if in_maps[c] is None:
            in_maps[c] = zero_map
    res = run_bass_kernel_spmd(nc, in_maps, core_ids=list(range(8)))
    r = [res.results[c] for c in WORK]
    y_prompt = np.stack([r[b]["y_p"] for b in range(4)])
    y_sample = np.concatenate([r[c]["y_s"] for c in range(4)])[:, None, :]
    pool_prompt = np.stack([r[b]["pool_p"] for b in range(4)], axis=1)
    pool_sample = np.concatenate([r[c]["pool_s"] for c in range(4)], axis=1)
    conv_prompt = np.stack([r[b]["conv_p"] for b in range(4)], axis=1)
    conv_sample = np.concatenate([r[c]["conv_s"] for c in range(4)], axis=1)
    ret_prompt = np.stack([r[b]["ret_p"] for b in range(4)], axis=1)
    ret_sample = np.concatenate([r[c]["ret_s"] for c in range(4)], axis=1)
    sgu_prompt = np.stack([r[b]["sgu_p"] for b in range(4)], axis=1)
    sgu_sample = np.concatenate([r[c]["sgu_s"] for c in range(4)], axis=1)[:, :, None, :]
    outs = (y_prompt, y_sample, pool_prompt, pool_sample, conv_prompt, conv_sample, ret_prompt, ret_sample, sgu_prompt, sgu_sample)
    return tuple(np.ascontiguousarray(o, dtype=f32) for o in outs)
```

```python
import numpy as np
import os
from contextlib import ExitStack
import concourse.bass as bass
import concourse.mybir as mybir
from concourse.bass_utils import run_bass_kernel_spmd

F32 = mybir.dt.float32
BF16 = mybir.dt.bfloat16
ALU = mybir.AluOpType
AF = mybir.ActivationFunctionType
AX = mybir.AxisListType

D = 2048
GW = 512
NL = 2
FF = 5632
NF = FF // 128
SEQ = 2048
G = 512
NG = SEQ // G
NS = 16
GT = G + 32
EPS = 1e-6
PAST = 16384
GAM = [1.0 - 2.0 ** (-5.0 - h) for h in range(4)]
WINS = (2, 4, 8, 16)


class Res:
    __slots__ = ("name", "w", "r", "excl")

    def __init__(self, name, excl=False):
        self.name = name
        self.w = None
        self.r = {}
        self.excl = excl


class DSem:
    def __init__(self, nc, es, name):
        self.sem = es.enter_context(nc.semaphore(name))
        self.cnt = 0


class Eng:
    def __init__(self, nc, es, h, name, self_sync=True):
        self.h = h
        self.name = name
        self.sem = es.enter_context(nc.semaphore("sem_" + name))
        self.cnt = 0
        self.seen = {}
        self.self_sync = self_sync
        self.pend = []

    def wait(self, tok):
        if tok is None:
            return
        sem, val = tok
        if sem is self.sem and not self.self_sync:
            return
        k = id(sem)
        if self.seen.get(k, 0) >= val:
            return
        self.h.wait_ge(sem, val)
        self.seen[k] = val

    def _deps(self, reads, writes):
        ws = list(writes) + [r for r in reads if r.excl]
        rs = [r for r in reads if not r.excl]
        for r in rs:
            self.wait(r.w)
        for w in ws:
            self.wait(w.w)
            for tok in list(w.r.values()):
                self.wait(tok)
        return rs, ws

    @staticmethod
    def _apply(tok, rs, ws):
        for r in rs:
            r.r[id(tok[0])] = tok
        for w in ws:
            w.w = tok
            w.r = {}

    def op(self, fn, reads=(), writes=(), inc=True):
        rs, ws = self._deps(reads, writes)
        ins = fn(self.h)
        if inc:
            self.cnt += 1
            ins.then_inc(self.sem, 1)
            tok = (self.sem, self.cnt)
            for (prs, pws) in self.pend:
                self._apply(tok, prs, pws)
            self.pend = []
            self._apply(tok, rs, ws)
        else:
            self.pend.append((rs, ws))

    def dma(self, ds, out, in_, reads=(), writes=(), **kw):
        rs, ws = self._deps(reads, writes)
        ins = self.h.dma_start(out=out, in_=in_, **kw)
        ds.cnt += 16
        ins.then_inc(ds.sem, 16)
        tok = (ds.sem, ds.cnt)
        self._apply(tok, rs, ws)
        return rs, ws


def dma_batch(eng, ds, items, **kw):
    allr, allw = [], []
    for (o, i, rs, ws) in items:
        r2, w2 = eng.dma(ds, o, i, rs, ws, **kw)
        allr += r2
        allw += w2
    tok = (ds.sem, ds.cnt)
    Eng._apply(tok, allr, allw)


def build_program():
    nc = bass.Bass("TRN2", target_bir_lowering=False)

    def din(name, shape):
        return nc.dram_tensor(name, list(shape), F32, kind="ExternalInput").ap()

    def dout(name, shape):
        return nc.dram_tensor(name, list(shape), F32, kind="ExternalOutput").ap()

    xp = din("xp", [SEQ, D])
    xs = din("xs", [NS, D])
    st_pool = din("st_pool", [NL, NS, 15, GW])
    st_conv = din("st_conv", [NL, NS, 2, GW])
    st_ret = din("st_ret", [NL, NS, 4, 128, 128])
    w_in = din("w_in", [NL, D, 10 * GW])
    w_out = din("w_out", [NL, D, D])
    w_gu = din("w_gu", [NL, D, 2 * FF])
    w_dn = din("w_dn", [NL, FF, D])
    pool_w = din("pool_w", [NL, 4, 128, 128])
    gcols_d = din("gcols", [128, 5, 16])
    pscale_d = din("pscale", [128, NL, 4])
    convw_d = din("convw", [128, NL, 3, 4])
    convwb_d = din("convwb", [NS, NL, 3, GW])
    retg_d = din("retg", [128, NL, GW])
    sgug_d = din("sgug", [128, NL, GW])
    sguwT_d = din("sguwT", [128, NL, 4, 128])
    sgub_d = din("sgub", [128, NL, 4, 128])
    sgu00_d = din("sgu00", [NS, NL, 2, 4])
    cos_d = din("cos_t", [128, 16, 64])
    sin_d = din("sin_t", [128, 16, 64])
    coss_d = din("cos_s", [NS, 64])
    sins_d = din("sin_s", [NS, 64])
    maskT_d = din("maskT", [128, 4, 128])
    qdec_d = din("qdec", [128, 4, 128])
    kdec_d = din("kdec", [128, 4])
    tri_d = din("tri", [128, 128])
    ident_d = din("ident", [128, 128])
    invc0_d = din("invc0", [128, 8, 16])
    eye16_d = din("eye16", [NS, NS])

    y_p = dout("y_p", [SEQ // 2, D])
    y_s = dout("y_s", [NS, D])
    pool_p = dout("pool_p", [NL, 15, GW])
    pool_s = dout("pool_s", [NL, NS, 15, GW])
    conv_p = dout("conv_p", [NL, 2, GW])
    conv_s = dout("conv_s", [NL, NS, 2, GW])
    ret_p = dout("ret_p", [NL, 4, 128, 128])
    ret_s = dout("ret_s", [NL, NS, 4, 128, 128])
    sgu_p = dout("sgu_p", [NL, 128, GW])
    sgu_s = dout("sgu_s", [NL, NS, GW])

    with ExitStack() as es:
        def sb(name, shape, dt=F32):
            return es.enter_context(nc.sbuf_tensor("sb_" + name, list(shape), dt))

        def psum(name, shape, dt=F32):
            return es.enter_context(nc.psum_tensor("ps_" + name, list(shape), dt))

        pe = Eng(nc, es, nc.tensor, "pe", self_sync=False)
        dve = Eng(nc, es, nc.vector, "dve")
        act = Eng(nc, es, nc.scalar, "act")
        pool = Eng(nc, es, nc.gpsimd, "pool")
        sp = Eng(nc, es, nc.sync, "sp")
        dsems = []

        def mkds(name):
            d = DSem(nc, es, name)
            dsems.append(d)
            return d

        hT = sb("hT", [128, 16, GT])
        xnT = sb("xnT", [128, 16, GT], BF16)
        mixT = sb("mixT", [128, 16, GT], BF16)
        big = sb("big", [128, NF * GT // 2])
        arena = sb("arena", [128, 8704])

        def carve(base, off, shape, dt=F32):
            nfl = int(np.prod(shape[1:]))
            if dt == BF16:
                v = base[:, off:off + (nfl + 1) // 2].bitcast(BF16)[:, 0:nfl]
                used = (nfl + 1) // 2
            else:
                v = base[:, off:off + nfl]
                used = nfl
            v = v[0:shape[0]]
            if len(shape) == 3:
                v = v.rearrange("p (a b) -> p a b", a=shape[1])
            elif len(shape) == 4:
                v = v.rearrange("p (a b c) -> p a b c", a=shape[1], b=shape[2])
            return v, off + used

        actT = big[:].bitcast(BF16).rearrange("p (j t) -> p j t", t=GT)
        o = 0
        aext, o = carve(big, o, [128, 4, 15 + G])
        zext, o = carve(big, o, [128, 4, 2 + G])
        hh_sb, o = carve(big, o, [128, 4, G])
        u_sb, o = carve(big, o, [128, 4, G], BF16)
        vnb, o = carve(big, o, [128, 4, GW], BF16)
        gs, o = carve(big, o, [128, 4, GW], BF16)
        assert o <= 10688
        pFM, _ = carve(big, 10688, [128, 40, NS])
        sguw_f, _ = carve(big, 0, [128, NL, 4, 128])
        o = 0
        S_q0, o = carve(big, o, [128, 4, 4, 128])
        S_q1, o = carve(big, o, [128, 4, 4, 128])
        S_qs = [S_q0, S_q1]
        s_cprev, o = carve(big, o, [NS, 2, GW])
        s_convw, o = carve(big, o, [NS, 3, GW])
        s_prev4, o = carve(big, o, [NS, 26, 128])
        assert o <= 10688
        o = 0
        xins = [carve(arena, i * D, [128, D])[0] for i in range(3)]
        xrot = [0]
        qrot, o = carve(arena, o, [128, 4, GW], BF16)
        krot, o = carve(arena, o, [128, 4, GW], BF16)
        kd, o = carve(arena, o, [128, 4, GW], BF16)
        vbf, o = carve(arena, o, [128, 4, GW], BF16)
        qT, o = carve(arena, o, [128, 4, 128], BF16)
        qdT, o = carve(arena, o, [128, 4, 128], BF16)
        kT, o = carve(arena, o, [128, 4, 128], BF16)
        scT, o = carve(arena, o, [128, 4, 128], BF16)
        ycb, o = carve(arena, o, [128, GW], BF16)
        tmp4, o = carve(arena, o, [128, 4, 128])
        pt0, o = carve(arena, o, [128, 15 + G])
        pt1, o = carve(arena, o, [128, 15 + G])
        ptmp = [pt0, pt1]
        Sst, o = carve(arena, o, [128, NL, 4, 128])
        Sbf, o = carve(arena, o, [128, NL, 4, 128], BF16)
        carry_a, o = carve(arena, o, [128, NL, 4, 15])
        carry_z, o = carve(arena, o, [128, NL, 4, 2])
        assert o <= 8704, o
        o = 0
        s_a, o = carve(arena, o, [NS, GW])
        s_d, o = carve(arena, o, [NS, GW])
        s_hh, o = carve(arena, o, [NS, GW])
        s_u, o = carve(arena, o, [NS, GW])
        s_q, o = carve(arena, o, [NS, GW])
        s_k, o = carve(arena, o, [NS, GW])
        s_v, o = carve(arena, o, [NS, GW])
        s_t, o = carve(arena, o, [NS, GW])
        s_g, o = carve(arena, o, [NS, GW])
        s_mix, o = carve(arena, o, [NS, 3 * GW])
        s_kdiag0, o = carve(arena, o, [NS, GW])
        s_kdiag = [s_kdiag0, s_kdiag0]
        s_qT, o = carve(arena, o, [128, 4, NS])
        s_oT, o = carve(arena, o, [128, 4, NS])
        s_sc, o = carve(arena, o, [NS, 4])
        assert o <= 6942, o
        NSLOT = 3
        wslot = [sb(f"wslot{i}", [128, 4096], BF16) for i in range(NSLOT)]
        rstd = sb("rstd", [128, G])
        sqb = [sb(f"sqb{i}", [128, G], BF16) for i in range(2)]
        dT = sb("dT", [128, G], BF16)
        dT2 = sb("dT2", [128, G], BF16)
        dT3 = sb("dT3", [128, G], BF16)
        dTs = [dT, dT2, dT3]
        cacc = sb("cacc", [128, G])
        vnf = sb("vnf", [128, GW])
        rt = [sb(f"rt{i}", [128, 4, 64]) for i in range(2)]
        onb = sb("onb", [128, GW])
        stt = sb("stt", [128, 4, 6])
        mv = sb("mv", [128, 4, 2])
        sd = sb("sd", [128, 4])
        nb = sb("nb", [128, 4])
        ident = sb("ident", [128, 128])
        identb = sb("identb", [128, 128], BF16)
        ones_b = sb("ones_b", [128, 128], BF16)
        epsc = sb("epsc", [128, 1])
        gcols = sb("gcols", [128, 5, 16])
        pscale = sb("pscale", [128, NL, 4])
        convw = sb("convw", [128, NL, 3, 4])
        retg = sb("retg", [128, GW])
        sgug = sb("sgug", [128, GW])
        WsT = sb("WsT", [128, NL, 4, 128], BF16)
        sgub = sb("sgub", [128, 4, 128])
        sgu00 = sb("sgu00", [NS, NL, 2, 4])
        poolw = sb("poolw", [128, NL * 4, 128], BF16)
        cos_g = sb("cos_g", [128, 4, 64])
        sin_g = sb("sin_g", [128, 4, 64])
        cos_s = sb("cos_s", [NS, 64])
        sin_s = sb("sin_s", [NS, 64])
        maskT = sb("maskT", [128, 4, 128])
        qdec = sb("qdec", [128, 4, 128])
        kdec = sb("kdec", [128, 4])
        tri = sb("tri", [128, 128])
        invc0 = sb("invc0", [128, 8, 16])
        eye16 = sb("eye16", [NS, NS])

        pm = [psum(f"pm{i}", [128, 512]) for i in range(4)]
        pa = [psum(f"pa{i}", [128, 512]) for i in range(2)]
        pb = [psum(f"pb{i}", [128, 1024], BF16) for i in range(2)]
        R_pm = [Res(f"pm{i}", True) for i in range(4)]
        R_pa = [Res(f"pa{i}", True) for i in range(2)]
        R_pb = [Res(f"pb{i}", True) for i in range(2)]
        rot = {"m": 0, "a": 0, "b": 0}

        mlimit = [4]

        def nxt(kind):
            lst, rl = {"m": (pm, R_pm), "a": (pa, R_pa), "b": (pb, R_pb)}[kind]
            i = rot[kind] % (mlimit[0] if kind == "m" else len(lst))
            rot[kind] += 1
            return lst[i], rl[i]

        R = {}

        def res(name):
            if name not in R:
                R[name] = Res(name)
            return R[name]

        R_h = [res(f"h{k}") for k in range(16)]
        R_xn = [res(f"xn{k}") for k in range(16)]
        R_mix = [res(f"mix{k}") for k in range(16)]
        R_act = [res(f"act{j}") for j in range(NF)]
        R_slot = [res(f"slot{i}") for i in range(NSLOT)]
        slot_ds = [mkds(f"ds_slot{i}") for i in range(NSLOT)]
        slot_rot = [0]

        ds_setup = mkds("ds_setup")
        setup_items = []
        for (t, d) in [(ident, ident_d), (gcols, gcols_d), (pscale, pscale_d), (convw, convw_d),
                       (sguw_f, sguwT_d), (sgu00, sgu00_d), (cos_s, coss_d), (sin_s, sins_d), (maskT, maskT_d), (qdec, qdec_d),
                       (kdec, kdec_d), (tri, tri_d), (invc0, invc0_d), (eye16, eye16_d)]:
            setup_items.append((t if t is sguw_f else t[:], d, [], [res("consts")]))
        dma_batch(sp, ds_setup, setup_items)
        ds_setup2 = mkds("ds_setup2")
        dma_batch(pool, ds_setup2, [
            (identb[:], ident_d, [], [res("consts2")]),
            (poolw[:], pool_w.rearrange("l g c e -> c (l g) e"), [], [res("consts2")]),
        ])
        RC = [res("consts"), res("consts2"), res("consts3")]
        dve.op(lambda e: e.memset(ones_b[:], 1.0), [], [res("consts3")])
        dve.op(lambda e: e.memset(epsc[:], EPS), [], [res("consts3")])
        dve.op(lambda e: e.memset(Sst[:], 0.0), [], [res("S0"), res("S1")])
        dve.op(lambda e: e.memset(Sbf[:], 0.0), [], [res("Sbf0"), res("Sbf1")])
        dve.op(lambda e: e.memset(carry_a[:], 0.0), [], [res("ca0"), res("ca1")])
        dve.op(lambda e: e.memset(carry_z[:], 0.0), [], [res("cz0"), res("cz1")])
        dve.op(lambda e: e.tensor_tensor(WsT[:].rearrange("p l h t -> p (l h) t"),
                                         sguw_f.rearrange("p l h t -> p (l h) t"),
                                         tri[:].unsqueeze(1).to_broadcast([128, NL * 4, 128]), ALU.mult),
               RC, [res("consts3")])

        def load_w(view_shape, src):
            i = slot_rot[0] % NSLOT
            slot_rot[0] += 1
            n = int(np.prod(view_shape[1:]))
            v = wslot[i][:, 0:n]
            if len(view_shape) == 3:
                v = v.rearrange("p (a b) -> p a b", a=view_shape[1])
            elif len(view_shape) == 4:
                v = v.rearrange("p (a b c) -> p a b c", a=view_shape[1], b=view_shape[2])
            pool.dma(slot_ds[i], v, src, [], [R_slot[i]])
            return v, R_slot[i]

        def rmsnorm(n, gidx, out_xn=True, c0=0):
            cs = slice(c0, c0 + n)
            ss, Rss = nxt("a")
            for k in range(16):
                s = sqb[k % 2]
                Rs = res(f"sqb{k % 2}")
                act.op(lambda e, s=s, k=k: e.activation(s[:, :n], hT[:, k, cs], AF.Square), [R_h[k]], [Rs])
                pe.op(lambda e, s=s, k=k: e.matmul(ss[:, :n], ones_b[:], s[:, :n], start=(k == 0), stop=(k == 15)),
                      [Rs, res("consts3")], [Rss], inc=True)
            act.op(lambda e: e.activation(rstd[:, :n], ss[:, :n], AF.Sqrt, bias=epsc[:], scale=1.0 / D),
                   [Rss, res("consts3")], [res("rstd")])
            dve.op(lambda e: e.reciprocal(rstd[:, :n], rstd[:, :n]), [], [res("rstd")])
            for k in range(16):
                if out_xn:
                    dve.op(lambda e, k=k: e.scalar_tensor_tensor(xnT[:, k, cs], hT[:, k, cs], gcols[:, gidx, k:k + 1],
                                                                 rstd[:, :n], ALU.mult, ALU.mult),
                           [R_h[k], res("rstd"), res("consts")], [R_xn[k]])
                else:
                    dve.op(lambda e, k=k: e.scalar_tensor_tensor(hT[:, k, cs], hT[:, k, cs], gcols[:, gidx, k:k + 1],
                                                                 rstd[:, :n], ALU.mult, ALU.mult),
                           [res("rstd"), res("consts")], [R_h[k]])

        def load_ltab(l):
            dma_batch(sp, ds_lt, [(retg[:], retg_d[:, l, :], [], [res("ltab")]),
                                  (sgug[:], sgug_d[:, l, :], [], [res("ltab")]),
                                  (sgub[:], sgub_d[:, l, :, :], [], [res("ltab")])])

        def load_x(n, src_rows):
            for (src, nt, c0) in src_rows:
                bi = xrot[0] % 3
                xrot[0] += 1
                xin, Rx = xins[bi], res(f"xin{bi}")
                R_ar = ([res(f"{nm}{i}") for nm in ("qrot", "krot", "kd", "vbf") for i in range(4)]
                        + [res(nm) for nm in ("qT", "qdT", "kT", "scT", "ycb", "tmp4", "ptmp0", "ptmp1")])
                sp.dma(ds_xb[bi], xin[:nt, :], src, [], [Rx] + R_ar)
                for k4 in range(4):
                    pt, Rpt = nxt("a")
                    for kk in range(4):
                        k = k4 * 4 + kk
                        pe.op(lambda e, k=k, kk=kk: e.transpose(pt[:, kk * 128:kk * 128 + nt], xin[:nt, k * 128:(k + 1) * 128],
                                                                ident[:nt, :nt]),
                              [Rx, res("consts")], [Rpt], inc=(kk == 3))
                    act.op(lambda e, k4=k4: e.copy(hT[:, k4 * 4:k4 * 4 + 4, c0:c0 + nt],
                                                   pt[:, 0:512].rearrange("p (a b) -> p a b", a=4)[:, :, :nt]),
                           [Rpt], [R_h[k4 * 4 + kk] for kk in range(4)])

        def store_y(n, dst_rows):
            for (dst, nt, c0) in dst_rows:
                bi = xrot[0] % 3
                xrot[0] += 1
                xin, Rx = xins[bi], res(f"xin{bi}")
                for k4 in range(4):
                    pt, Rpt = nxt("a")
                    for kk in range(4):
                        k = k4 * 4 + kk
                        pe.op(lambda e, k=k, kk=kk: e.transpose(pt[:nt, kk * 128:(kk + 1) * 128], hT[:, k, c0:c0 + nt], ident[:]),
                              [R_h[k], res("consts")], [Rpt], inc=(kk == 3))
                    act.op(lambda e, k4=k4: e.copy(xin[:nt, k4 * 512:(k4 + 1) * 512], pt[:nt, :]), [Rpt], [Rx])
                sp.dma(ds_xb[bi], dst, xin[:nt, :], [Rx], [])

        def wout_and_ffn(l, n, ns=0):
            c1 = G
            if ns:
                pQ, RpQ = nxt("a")
            for dp in range(8):
                wv, Rw = load_w([128, 16, 256], w_out[l, :, dp * 256:(dp + 1) * 256].rearrange("(k p) c -> p k c", p=128))
                for dd in range(2):
                    d = dp * 2 + dd
                    ps, Rps = nxt("m")
                    for k in range(16):
                        pe.op(lambda e, k=k, dd=dd: e.matmul(ps[:, :n], wv[:, k, dd * 128:(dd + 1) * 128], mixT[:, k, :n],
                                                             start=(k == 0), stop=(k == 15)),
                              [Rw, R_mix[k]], [Rps], inc=(k == 15))
                    dve.op(lambda e, d=d: e.tensor_tensor(hT[:, d, :n], hT[:, d, :n], ps[:, :n], ALU.add), [Rps], [R_h[d]])
                    if ns:
                        for k in range(16):
                            pe.op(lambda e, k=k, dd=dd, d=d: e.matmul(pQ[:, d * ns:(d + 1) * ns], wv[:, k, dd * 128:(dd + 1) * 128],
                                                                     mixT[:, k, c1:c1 + ns], start=(k == 0), stop=(k == 15)),
                                  [Rw, R_mix[k]], [RpQ], inc=(k == 15))
            if ns:
                dve.op(lambda e: e.tensor_tensor(hT[:, :, c1:c1 + ns], hT[:, :, c1:c1 + ns],
                                                 pQ[:, 0:16 * ns].rearrange("p (a b) -> p a b", a=16), ALU.add), [RpQ], R_h)
            rmsnorm(n, 2 * l + 1)
            if ns:
                rmsnorm(ns, 2 * l + 1, c0=c1)
            for jp in range(NF // 2):
                wg, Rwg = load_w([128, 16, 256], w_gu[l, :, jp * 256:(jp + 1) * 256].rearrange("(k p) c -> p k c", p=128))
                wu, Rwu = load_w([128, 16, 256], w_gu[l, :, FF + jp * 256:FF + (jp + 1) * 256].rearrange("(k p) c -> p k c", p=128))
                pgs = [nxt("m"), nxt("m")]
                for jj in range(2):
                    pg, Rpg = pgs[jj]
                    for k in range(16):
                        pe.op(lambda e, k=k, jj=jj, pg=pg: e.matmul(pg[:, :n], wg[:, k, jj * 128:(jj + 1) * 128], xnT[:, k, :n],
                                                                  start=(k == 0), stop=(k == 15)),
                              [Rwg, R_xn[k]], [Rpg], inc=(k == 15))
                pus = [nxt("m"), nxt("m")]
                for jj in range(2):
                    pu, Rpu = pus[jj]
                    for k in range(16):
                        pe.op(lambda e, k=k, jj=jj, pu=pu: e.matmul(pu[:, :n], wu[:, k, jj * 128:(jj + 1) * 128], xnT[:, k, :n],
                                                                  start=(k == 0), stop=(k == 15)),
                              [Rwu, R_xn[k]], [Rpu], inc=(k == 15))
                for jj in range(2):
                    j = jp * 2 + jj
                    pg, Rpg = pgs[jj]
                    pu, Rpu = pus[jj]
                    sgb, Rsg = (cacc, res("cacc")) if jj == 0 else (rstd, res("rstd"))
                    act.op(lambda e, pg=pg, sgb=sgb: e.activation(sgb[:, :n], pg[:, :n], AF.Silu), [Rpg], [Rsg])
                    dve.op(lambda e, j=j, pu=pu, sgb=sgb: e.tensor_tensor(actT[:, j, :n], pu[:, :n], sgb[:, :n], ALU.mult),
                           [Rpu, Rsg], [R_act[j]])
                if ns:
                    for jj in range(2):
                        j = jp * 2 + jj
                        q16 = j % 16
                        if q16 == 0:
                            (pgS, RpgS), (puS, RpuS) = nxt("a"), nxt("a")
                            sstate["g"] = (pgS, RpgS, puS, RpuS)
                        pgS, RpgS, puS, RpuS = sstate["g"]
                        for k in range(16):
                            pe.op(lambda e, k=k, jj=jj, q16=q16, pgS=pgS: e.matmul(pgS[:, q16 * ns:(q16 + 1) * ns], wg[:, k, jj * 128:(jj + 1) * 128],
                                                                                 xnT[:, k, c1:c1 + ns], start=(k == 0), stop=(k == 15)),
                                  [Rwg, R_xn[k]], [RpgS], inc=(k == 15))
                        for k in range(16):
                            pe.op(lambda e, k=k, jj=jj, q16=q16, puS=puS: e.matmul(puS[:, q16 * ns:(q16 + 1) * ns], wu[:, k, jj * 128:(jj + 1) * 128],
                                                                                 xnT[:, k, c1:c1 + ns], start=(k == 0), stop=(k == 15)),
                                  [Rwu, R_xn[k]], [RpuS], inc=(k == 15))
                        if q16 == 15 or j == NF - 1:
                            cnt = q16 + 1
                            j0 = j - q16
                            act.op(lambda e, pgS=pgS, cnt=cnt: e.activation(vnf[:, :cnt * ns], pgS[:, :cnt * ns], AF.Silu), [RpgS], [res("lnout")])
                            dve.op(lambda e, puS=puS, cnt=cnt, j0=j0: e.tensor_tensor(
                                actT[:, j0:j0 + cnt, c1:c1 + ns], puS[:, :cnt * ns].rearrange("p (a b) -> p a b", a=cnt),
                                vnf[:, :cnt * ns].rearrange("p (a b) -> p a b", a=cnt), ALU.mult),
                                [RpuS, res("lnout")], [R_act[jx] for jx in range(j0, j0 + cnt)])
            fsegs = [(0, 16), (16, 32), (32, 44)]
            if ns:
                pQd = [nxt("a"), nxt("a")]
            for dp in range(8):
                psd = [nxt("m"), nxt("m")]
                for si, (f0, f1) in enumerate(fsegs):
                    nf = f1 - f0
                    src = w_dn[l, f0 * 128:f1 * 128, dp * 256:(dp + 1) * 256].rearrange("(j p) c -> p j c", p=128)
                    wv, Rw = load_w([128, nf, 256], src)
                    for dd in range(2):
                        ps, Rps = psd[dd]
                        for jj in range(nf):
                            j = f0 + jj
                            pe.op(lambda e, jj=jj, j=j, dd=dd, ps=ps: e.matmul(ps[:, :n], wv[:, jj, dd * 128:(dd + 1) * 128],
                                                                             actT[:, j, :n], start=(j == 0), stop=(j == NF - 1)),
                                  [Rw, R_act[j]], [Rps], inc=(jj == nf - 1))
                    if ns:
                        for dd in range(2):
                            pq_, Rpq_ = pQd[dd]
                            for jj in range(nf):
                                j = f0 + jj
                                pe.op(lambda e, jj=jj, j=j, dd=dd, pq_=pq_, dp=dp: e.matmul(
                                    pq_[:, dp * ns:(dp + 1) * ns], wv[:, jj, dd * 128:(dd + 1) * 128], actT[:, j, c1:c1 + ns],
                                    start=(j == 0), stop=(j == NF - 1)),
                                    [Rw, R_act[j]], [Rpq_], inc=(jj == nf - 1))
                for dd in range(2):
                    d = dp * 2 + dd
                    ps, Rps = psd[dd]
                    dve.op(lambda e, d=d, ps=ps: e.tensor_tensor(hT[:, d, :n], hT[:, d, :n], ps[:, :n], ALU.add), [Rps], [R_h[d]])
            if ns:
                for dd in range(2):
                    pq_, Rpq_ = pQd[dd]
                    for dp in range(8):
                        d = dp * 2 + dd
                        dve.op(lambda e, d=d, dp=dp, pq_=pq_: e.tensor_tensor(hT[:, d, c1:c1 + ns], hT[:, d, c1:c1 + ns],
                                                                            pq_[:, dp * ns:(dp + 1) * ns], ALU.add), [Rpq_], [R_h[d]])

        sstate = {}

        def layernorm_heads(src_ps, npart, Rsrc, dst, nheads=4, width=128):
            for h in range(nheads):
                dve.op(lambda e, h=h: e.bn_stats(stt[:npart, h, :], src_ps[:npart, h * width:(h + 1) * width]), [Rsrc], [res("stt")])
            for h in range(nheads):
                dve.op(lambda e, h=h: e.bn_aggr(mv[:npart, h, :], stt[:npart, h, :]), [res("stt")], [res("mv")])
            act.op(lambda e: e.activation(sd[:npart, :nheads], mv[:npart, :nheads, 1], AF.Sqrt, bias=epsc[:npart, :], scale=1.0),
                   [res("mv"), res("consts3")], [res("sd")])
            dve.op(lambda e: e.reciprocal(sd[:npart, :nheads], sd[:npart, :nheads]), [], [res("sd")])
            dve.op(lambda e: e.scalar_tensor_tensor(nb[:npart, :nheads], mv[:npart, :nheads, 0], -1.0, sd[:npart, :nheads],
                                                    ALU.mult, ALU.mult), [res("mv"), res("sd")], [res("nb")])
            for h in range(nheads):
                dve.op(lambda e, h=h: e.tensor_scalar(dst[:npart, h * width:(h + 1) * width], src_ps[:npart, h * width:(h + 1) * width],
                                                      sd[:npart, h:h + 1], nb[:npart, h:h + 1], ALU.mult, ALU.add),
                       [Rsrc, res("sd"), res("nb")], [res("lnout")])

        def rotary(dst, src_ps, npart, cosv, sinv, Rsrc, Rdst, Rtab):
            s4 = src_ps[:npart, :].rearrange("p (h t e) -> p h t e", h=4, t=2)
            d4 = dst.rearrange("p (h t e) -> p h t e", h=4, t=2)
            cb = cosv.unsqueeze(1).to_broadcast([npart, 4, 64])
            sbb = sinv.unsqueeze(1).to_broadcast([npart, 4, 64])
            Rr = res("rt")
            dve.op(lambda e: e.tensor_tensor(rt[0][:npart], s4[:, :, 0, :], cb, ALU.mult), [Rsrc, Rtab], [Rr])
            dve.op(lambda e: e.tensor_tensor(rt[1][:npart], s4[:, :, 1, :], sbb, ALU.mult), [Rsrc], [Rr])
            dve.op(lambda e: e.tensor_tensor(d4[:, :, 0, :], rt[0][:npart], rt[1][:npart], ALU.subtract), [Rr], [Rdst])
            dve.op(lambda e: e.tensor_tensor(rt[0][:npart], s4[:, :, 1, :], cb, ALU.mult), [Rsrc], [Rr])
            dve.op(lambda e: e.tensor_tensor(rt[1][:npart], s4[:, :, 0, :], sbb, ALU.mult), [Rsrc], [Rr])
            dve.op(lambda e: e.tensor_tensor(d4[:, :, 1, :], rt[0][:npart], rt[1][:npart], ALU.add), [Rr], [Rdst])

        def prompt_group(g, ws):
            n = G
            c1 = G
            dma_batch(sp, ds_cs, [(cos_g[:], cos_d[:, g * 4:(g + 1) * 4, :], [], [res("cs")]),
                                  (sin_g[:], sin_d[:, g * 4:(g + 1) * 4, :], [], [res("cs")])])
            load_x(n, [(xp[g * G + c * 128:g * G + (c + 1) * 128, :], 128, c * 128) for c in range(4)])
            if ws:
                load_x(NS, [(xs, NS, c1)])
            for l in range(NL):
                load_ltab(l)
                rmsnorm(n, 2 * l)
                if ws:
                    rmsnorm(NS, 2 * l, c0=c1)
                    mlimit[0] = 3
                RS, RSb = res(f"S{l}"), res(f"Sbf{l}")
                Rca, Rcz = res(f"ca{l}"), res(f"cz{l}")

                def fm_block(cb, consume):
                    if ws:
                        pS, RpS = pm[3], R_pm[3]
                    for half in range(2):
                        src = w_in[l, :, cb * GW + half * 256: cb * GW + (half + 1) * 256].rearrange("(k p) c -> p k c", p=128)
                        wv, Rw = load_w([128, 16, 256], src)
                        for ee in range(2):
                            ci = half * 2 + ee
                            ps, Rps = nxt("m")
                            for k in range(16):
                                pe.op(lambda e, k=k, ee=ee: e.matmul(ps[:, :n], wv[:, k, ee * 128:(ee + 1) * 128], xnT[:, k, :n],
                                                                     start=(k == 0), stop=(k == 15)),
                                      [Rw, R_xn[k]], [Rps], inc=(k == 15))
                            consume(ci, ps, Rps)
                            if ws:
                                for k in range(16):
                                    pe.op(lambda e, k=k, ee=ee, ci=ci: e.matmul(pS[:, ci * NS:(ci + 1) * NS], wv[:, k, ee * 128:(ee + 1) * 128],
                                                                              xnT[:, k, c1:c1 + NS], start=(k == 0), stop=(k == 15)),
                                          [Rw, R_xn[k]], [RpS], inc=(k == 15))
                            after_group()
                    if ws:
                        act.op(lambda e: e.copy(pFM[:, cb * 4:(cb + 1) * 4, :], pS[:, 0:4 * NS].rearrange("p (a b) -> p a b", a=4)),
                               [RpS], [res(f"pfm{cb}")])

                def tm_block(cb, consume):
                    wvs = []
                    for half in range(2):
                        src = w_in[l, half * 1024:(half + 1) * 1024, cb * GW:(cb + 1) * GW].rearrange("(k p) c -> p k c", p=128)
                        wvs.append(load_w([128, 8, GW], src))
                    if not ws:
                        for half in range(2):
                            wv, Rw = wvs[half]
                            for c in range(4):
                                ps, Rps = pm[c], R_pm[c]
                                for kk in range(8):
                                    k = half * 8 + kk
                                    pe.op(lambda e, k=k, kk=kk, c=c, wv=wv, ps=ps: e.matmul(ps[:, :], xnT[:, k, c * 128:(c + 1) * 128], wv[:, kk, :],
                                                                                          start=(k == 0), stop=(k == 15)),
                                          [Rw, R_xn[k]], [Rps], inc=(kk == 7))
                                if half == 1:
                                    consume(c, ps, Rps)
                    else:
                        for c in range(4):
                            ps, Rps = nxt("m")
                            for k in range(16):
                                wv, Rw = wvs[k // 8]
                                pe.op(lambda e, k=k, c=c, wv=wv: e.matmul(ps[:, :], xnT[:, k, c * 128:(c + 1) * 128], wv[:, k % 8, :],
                                                                          start=(k == 0), stop=(k == 15)),
                                      [Rw, R_xn[k]], [Rps], inc=(k == 15))
                            consume(c, ps, Rps)
                    if ws:
                        pS, RpS = pm[3], R_pm[3]
                        for eb in range(4):
                            for k in range(16):
                                wv, Rw = wvs[k // 8]
                                pe.op(lambda e, k=k, eb=eb, wv=wv: e.matmul(pS[:, eb * NS:(eb + 1) * NS], wv[:, k % 8, eb * 128:(eb + 1) * 128],
                                                                          xnT[:, k, c1:c1 + NS], start=(k == 0), stop=(k == 15)),
                                      [Rw, R_xn[k]], [RpS], inc=(k == 15))
                        act.op(lambda e: e.copy(pFM[:, cb * 4:(cb + 1) * 4, :], pS[:, 0:4 * NS].rearrange("p (a b) -> p a b", a=4)),
                               [RpS], [res(f"pfm{cb}")])

                hooks = {"gen": None, "deferred": []}

                def after_group(flush=False):
                    keep = []
                    for (age, f) in hooks["deferred"]:
                        if age >= 2 or flush:
                            f()
                        else:
                            keep.append((age + 1, f))
                    hooks["deferred"] = keep
                    if hooks["gen"] is not None:
                        try:
                            next(hooks["gen"])
                        except StopIteration:
                            hooks["gen"] = None

                def c_q(c, ps, Rps):
                    rotary(qrot[:, c, :], ps, 128, cos_g[:, c, :], sin_g[:, c, :], Rps, res(f"qrot{c}"), res("cs"))

                def c_k(c, ps, Rps):
                    rotary(krot[:, c, :], ps, 128, cos_g[:, c, :], sin_g[:, c, :], Rps, res(f"krot{c}"), res("cs"))
                    dve.op(lambda e: e.tensor_tensor(kd[:, c, :].rearrange("p (h e) -> p h e", h=4),
                                                     krot[:, c, :].rearrange("p (h e) -> p h e", h=4),
                                                     kdec[:].unsqueeze(2).to_broadcast([128, 4, 128]), ALU.mult),
                           [res(f"krot{c}"), res("consts")], [res(f"kd{c}")])

                def c_v(c, ps, Rps):
                    act.op(lambda e: e.copy(vbf[:, c, :], ps[:, :]), [Rps], [res(f"vbf{c}")])

                def c_g(c, ps, Rps):
                    act.op(lambda e: e.activation(gs[:, c, :], ps[:, :], AF.Silu), [Rps], [res(f"gs{c}")])

                def c_vv(c, ps, Rps):
                    layernorm_heads(ps, 128, Rps, vnf, nheads=1, width=GW)
                    dve.op(lambda e: e.tensor_tensor(vnf[:, :], vnf[:, :], sgug[:, :], ALU.mult), [res("ltab")], [res("lnout")])
                    act.op(lambda e: e.copy(vnb[:, c, :], vnf[:, :]), [res("lnout")], [res(f"vnb{c}")])
                    if g == NG - 1 and c == 3:
                        sp.dma(ds_sg, sgu_p[l], vnf[:, :], [res("lnout")], [])

                tm_block(4, c_q)
                tm_block(5, c_k)
                tm_block(6, c_v)
                tm_block(7, c_g)
                tm_block(9, c_vv)

                def mixer_gen():
                    for c in range(4):
                        tsl = slice(c * 128, (c + 1) * 128)
                        pq, Rpq = nxt("b")
                        for h in range(4):
                            pe.op(lambda e, h=h: e.transpose(pq[:, h * 128:(h + 1) * 128], qrot[:, c, h * 128:(h + 1) * 128], identb[:]),
                                  [res(f"qrot{c}"), res("consts2")], [Rpq], inc=(h == 3))
                        dve.op(lambda e: e.tensor_copy(qT[:], pq[:, 0:512].rearrange("p (h t) -> p h t", h=4)), [Rpq], [res("qT")])
                        dve.op(lambda e: e.tensor_tensor(qdT[:], pq[:, 0:512].rearrange("p (h t) -> p h t", h=4), qdec[:], ALU.mult),
                               [Rpq, res("consts")], [res("qdT")])
                        pk, Rpk = nxt("b")
                        for h in range(4):
                            pe.op(lambda e, h=h: e.transpose(pk[:, h * 128:(h + 1) * 128], krot[:, c, h * 128:(h + 1) * 128], identb[:]),
                                  [res(f"krot{c}"), res("consts2")], [Rpk], inc=(h == 3))
                        act.op(lambda e: e.copy(kT[:], pk[:, 0:512].rearrange("p (h t) -> p h t", h=4)), [Rpk], [res("kT")])
                        yield
                        psc, Rpsc = nxt("a")
                        for h in range(4):
                            pe.op(lambda e, h=h: e.matmul(psc[:, h * 128:(h + 1) * 128], kT[:, h, :], qT[:, h, :], start=True, stop=True),
                                  [res("kT"), res("qT")], [Rpsc], inc=(h == 3))
                        dve.op(lambda e: e.tensor_tensor(scT[:], psc[:, :].rearrange("p (h t) -> p h t", h=4), maskT[:], ALU.mult),
                               [Rpsc, res("consts")], [res("scT")])
                        pkv, Rpkv = nxt("m")
                        for h in range(4):
                            hs = slice(h * 128, (h + 1) * 128)
                            pe.op(lambda e, h=h, hs=hs: e.matmul(pkv[:, hs], kd[:, c, hs], vbf[:, c, hs], start=True, stop=True),
                                  [res(f"kd{c}"), res(f"vbf{c}")], [Rpkv], inc=(h == 3))
                        yield
                        po, Rpo = nxt("a")
                        for h in range(4):
                            hs = slice(h * 128, (h + 1) * 128)
                            pe.op(lambda e, h=h, hs=hs: e.matmul(po[:, hs], scT[:, h, :], vbf[:, c, hs], start=True, stop=False),
                                  [res("scT"), res(f"vbf{c}")], [Rpo], inc=False)
                            pe.op(lambda e, h=h, hs=hs: e.matmul(po[:, hs], qdT[:, h, :], Sbf[:, l, h, :], start=False, stop=True),
                                  [res("qdT"), RSb], [Rpo], inc=(h == 3))
                        for h in range(4):
                            dve.op(lambda e, h=h: e.scalar_tensor_tensor(Sst[:, l, h, :], Sst[:, l, h, :], GAM[h] ** 128,
                                                                         pkv[:, h * 128:(h + 1) * 128], ALU.mult, ALU.add), [Rpkv], [RS])
                        act.op(lambda e: e.copy(Sbf[:, l, :, :], Sst[:, l, :, :]), [RS], [RSb])
                        layernorm_heads(po, 128, Rpo, onb)
                        dve.op(lambda e: e.tensor_tensor(onb[:, :], onb[:, :], retg[:, :], ALU.mult), [res("ltab")], [res("lnout")])
                        dve.op(lambda e: e.tensor_tensor(ycb[:, :], onb[:, :], gs[:, c, :], ALU.mult), [res("lnout"), res(f"gs{c}")], [res("ycb")])
                        yield
                        yield
                        pyc, Rpyc = nxt("b")
                        for h in range(4):
                            pe.op(lambda e, h=h: e.transpose(pyc[:, h * 128:(h + 1) * 128], ycb[:, h * 128:(h + 1) * 128], identb[:]),
                                  [res("ycb"), res("consts2")], [Rpyc], inc=(h == 3))
                        act.op(lambda e: e.copy(mixT[:, 8:12, tsl], pyc[:, 0:512].rearrange("p (h t) -> p h t", h=4)),
                               [Rpyc], [R_mix[8 + h] for h in range(4)])
                        pmx, Rpmx = nxt("a")
                        for h in range(4):
                            pe.op(lambda e, h=h: e.matmul(pmx[:, h * 128:(h + 1) * 128], vnb[:, c, h * 128:(h + 1) * 128], WsT[:, l, h, :],
                                                          start=True, stop=True), [res(f"vnb{c}"), res("consts3")], [Rpmx], inc=(h == 3))
                        dve.op(lambda e: e.tensor_tensor(tmp4[:], pmx[:, :].rearrange("p (h t) -> p h t", h=4), sgub[:, :, :], ALU.add),
                               [Rpmx, res("ltab")], [res("tmp4")])
                        dve.op(lambda e: e.tensor_tensor(mixT[:, 12:16, tsl], tmp4[:], u_sb[:, :, tsl], ALU.mult),
                               [res("tmp4"), res("u_sb")], [R_mix[12 + h] for h in range(4)])
                        yield

                hooks["gen"] = mixer_gen()

                def c_u(ci, ps, Rps):
                    act.op(lambda e: e.copy(u_sb[:, ci, :n], ps[:, :n]), [Rps], [res("u_sb")])

                fm_block(8, c_u)

                dve.op(lambda e: e.tensor_copy(aext[:, :, 0:15], carry_a[:, l, :, :]), [Rca], [res("aext")])

                def c_a(gi, ps, Rps):
                    act.op(lambda e: e.copy(aext[:, gi, 15:15 + n], ps[:, :n]), [Rps], [res("aext")])
                    cur = aext[:, gi, :]
                    L = 15 + n
                    sh = 1
                    for step in range(gi + 1):
                        o = ptmp[step % 2]
                        dve.op(lambda e, cur=cur, o=o, sh=sh: e.tensor_tensor(o[:, sh:L], cur[:, sh:L], cur[:, 0:L - sh], ALU.add),
                               [res("aext")] if step == 0 else [res(f"ptmp{(step - 1) % 2}")], [res(f"ptmp{step % 2}")])
                        cur = o
                        sh *= 2
                    Rcur = res(f"ptmp{gi % 2}")
                    dTg, RdT = dTs[gi % 3], res(f"dT{gi % 3}")
                    dve.op(lambda e, cur=cur: e.scalar_tensor_tensor(dTg[:, :n], cur[:, 15:15 + n], 1.0 / WINS[gi], aext[:, gi, 15:15 + n],
                                                                     ALU.mult, ALU.subtract), [Rcur, res("aext")], [RdT])
                    if g in ((0, 2) if os.environ.get('KFIX2', '1') == '1' else (0,)):
                        dve.op(lambda e, cur=cur: e.tensor_tensor(ptmp[(gi + 1) % 2][:, 0:16], cur[:, 15:31], invc0[:, (g // 2) * 4 + gi, :], ALU.mult),
                               [Rcur, res("consts")], [res(f"ptmp{(gi + 1) % 2}")])
                        dve.op(lambda e: e.tensor_tensor(dTg[:, 0:16], ptmp[(gi + 1) % 2][:, 0:16], aext[:, gi, 15:31], ALU.subtract),
                               [res(f"ptmp{(gi + 1) % 2}"), res("aext")], [RdT])

                    def pool_mm():
                        py, Rpy = nxt("m")
                        pe.op(lambda e: e.matmul(py[:, :n], poolw[:, l * 4 + gi, :], dTg[:, :n], start=True, stop=True),
                              [RdT, res("consts2")], [Rpy])
                        dve.op(lambda e: e.tensor_scalar(mixT[:, gi, :n], py[:, :n], pscale[:, l, gi:gi + 1], None, ALU.mult),
                               [Rpy, res("consts")], [R_mix[gi]])

                    hooks["deferred"].append((0, pool_mm))

                fm_block(0, c_a)
                dve.op(lambda e: e.tensor_copy(carry_a[:, l, :, :], aext[:, :, n:n + 15]), [res("aext")], [Rca])
                if g == NG - 1:
                    with nc.allow_non_contiguous_dma(reason="small state transpose"):
                        for gi in range(4):
                            sp.dma(ds_st, pool_p[l, :, gi * 128:(gi + 1) * 128].rearrange("t c -> c t"), carry_a[:, l, gi, :], [Rca], [],
                                   allow_slow_non_contiguous=True)

                dve.op(lambda e: e.tensor_copy(zext[:, :, 0:2], carry_z[:, l, :, :]), [Rcz], [res("zext")])

                def c_hh(ci, ps, Rps):
                    act.op(lambda e: e.copy(hh_sb[:, ci, :n], ps[:, :n]), [Rps], [res(f"hh{ci}")])

                fm_block(3, c_hh)

                def c_cg(ci, ps, Rps):
                    dve.op(lambda e: e.tensor_tensor(zext[:, ci, 2:2 + n], ps[:, :n], hh_sb[:, ci, :n], ALU.mult),
                           [Rps, res(f"hh{ci}")], [res("zext")])

                fm_block(2, c_cg)
                dve.op(lambda e: e.tensor_copy(carry_z[:, l, :, :], zext[:, :, n:n + 2]), [res("zext")], [Rcz])
                if g == NG - 1:
                    with nc.allow_non_contiguous_dma(reason="small state transpose"):
                        for gi in range(4):
                            sp.dma(ds_st, conv_p[l, :, gi * 128:(gi + 1) * 128].rearrange("t c -> c t"), carry_z[:, l, gi, :], [Rcz], [],
                                   allow_slow_non_contiguous=True)

                def c_bg(ci, ps, Rps):
                    dve.op(lambda e: e.tensor_scalar(cacc[:, :n], zext[:, ci, 0:n], convw[:, l, 0, ci:ci + 1], None, ALU.mult),
                           [res("zext"), res("consts")], [res("cacc")])
                    dve.op(lambda e: e.scalar_tensor_tensor(cacc[:, :n], zext[:, ci, 1:1 + n], convw[:, l, 1, ci:ci + 1], cacc[:, :n],
                                                            ALU.mult, ALU.add), [res("zext")], [res("cacc")])
                    dve.op(lambda e: e.scalar_tensor_tensor(cacc[:, :n], zext[:, ci, 2:2 + n], convw[:, l, 2, ci:ci + 1], cacc[:, :n],
                                                            ALU.mult, ALU.add), [res("zext")], [res("cacc")])
                    dve.op(lambda e: e.tensor_tensor(mixT[:, 4 + ci, :n], ps[:, :n], cacc[:, :n], ALU.mult),
                           [Rps, res("cacc")], [R_mix[4 + ci]])

                fm_block(1, c_bg)
                while hooks["gen"] is not None or hooks["deferred"]:
                    after_group(flush=True)
                if g == NG - 1:
                    sp.dma(ds_rp, ret_p[l].rearrange("h d e -> d h e"), Sst[:, l, :, :], [RS], [])
                mlimit[0] = 4
                if ws:
                    sample_mixers(l)
                if g < 2 and l == NL - 1 and os.environ.get('KSKIP', '1') == '1':
                    continue
                wout_and_ffn(l, n, NS if ws else 0)
            if g < 2:
                return
            rmsnorm(n, 4, out_xn=False)
            if ws:
                rmsnorm(NS, 4, out_xn=False, c0=c1)
            store_y(n, [(y_p[(g - 2) * G + c * 128:(g - 2) * G + (c + 1) * 128, :], 128, c * 128) for c in range(4)])
            if ws:
                store_y(NS, [(y_s, NS, c1)])

        def sample_mixers(l):
            n = NS
            roff = (0, 1, 4, 11)
            R_bigp = ([res("aext"), res("zext"), res("u_sb")] + [res(f"hh{i}") for i in range(4)]
                      + [res(f"vnb{i}") for i in range(4)] + [res(f"gs{i}") for i in range(4)])
            dma_batch(sp, ds_sst, [
                (s_cprev, st_conv[l], [], R_act + R_bigp),
                (s_convw, convwb_d[:, l], [], R_act),
            ] + [(s_prev4[:, roff[gi]:roff[gi] + WINS[gi] - 1, :], st_pool[l, :, 16 - WINS[gi]:15, gi * 128:(gi + 1) * 128], [], R_act)
                 for gi in range(4)])
            dma_batch(sp, ds_cp, [
                (pool_s[l, :, 0:14, :], st_pool[l, :, 1:15, :], [], []),
                (conv_s[l, :, 0:1, :], st_conv[l, :, 1:2, :], [], []),
            ])

            def tm_block(cb, consume):
                ps, Rps = nxt("m")
                for eb in range(4):
                    pe.op(lambda e, eb=eb: e.transpose(ps[:NS, eb * 128:(eb + 1) * 128], pFM[:, cb * 4 + eb, :], ident[:]),
                          [res(f"pfm{cb}"), res("consts")], [Rps], inc=(eb == 3))
                consume(ps, Rps)

            def to_mix(src, Rsrc, col0):
                dve.op(lambda e: e.tensor_copy(s_mix[:, col0:col0 + GW], src), [Rsrc], [res("s_mix")])

            def c_a(ps, Rps):
                act.op(lambda e: e.copy(s_a[:, :], ps[:n, :]), [Rps], [res("s_a")])
                sp.dma(ds_sa, pool_s[l, :, 14, :], s_a[:, :], [res("s_a")], [])
                for gi, w in enumerate(WINS):
                    cs = slice(gi * 128, (gi + 1) * 128)
                    if w > 2:
                        dve.op(lambda e, cs=cs, w=w, gi=gi: e.tensor_reduce(
                            s_d[:, cs], s_prev4[:, roff[gi]:roff[gi] + w - 1, :].rearrange("p t c -> p c t"), AX.X, ALU.add),
                            [R_act[0]], [res("s_d")])
                    else:
                        dve.op(lambda e, cs=cs: e.tensor_copy(s_d[:, cs], s_prev4[:, 0, :]), [R_act[0]], [res("s_d")])
                    dve.op(lambda e, cs=cs: e.tensor_tensor(s_d[:, cs], s_d[:, cs], s_a[:, cs], ALU.add), [res("s_a")], [res("s_d")])
                    dve.op(lambda e, cs=cs, w=w: e.scalar_tensor_tensor(s_d[:, cs], s_d[:, cs], 1.0 / w, s_a[:, cs], ALU.mult, ALU.subtract),
                           [res("s_a")], [res("s_d")])
                pt, Rpt = nxt("a")
                for gi in range(4):
                    pe.op(lambda e, gi=gi: e.transpose(pt[:, gi * NS:(gi + 1) * NS], s_d[:, gi * 128:(gi + 1) * 128], ident[:NS, :NS]),
                          [res("s_d"), res("consts")], [Rpt], inc=(gi == 3))
                act.op(lambda e: e.copy(dT[:, 0:4 * NS], pt[:, 0:4 * NS]), [Rpt], [res("dT0")])
                py, Rpy = nxt("a")
                for gi in range(4):
                    pe.op(lambda e, gi=gi: e.matmul(py[:, gi * NS:(gi + 1) * NS], poolw[:, l * 4 + gi, :], dT[:, gi * NS:(gi + 1) * NS],
                                                    start=True, stop=True), [res("dT0"), res("consts2")], [Rpy], inc=(gi == 3))
                for gi in range(4):
                    dve.op(lambda e, gi=gi: e.tensor_scalar(mixT[:, gi, G:G + NS], py[:, gi * NS:(gi + 1) * NS], pscale[:, l, gi:gi + 1], None, ALU.mult),
                           [Rpy, res("consts")], [R_mix[gi]])

            tm_block(0, c_a)

            def c_hh(ps, Rps):
                act.op(lambda e: e.copy(s_hh[:, :], ps[:n, :]), [Rps], [res("s_hh")])

            tm_block(3, c_hh)

            def c_cg(ps, Rps):
                dve.op(lambda e: e.tensor_tensor(s_hh[:, :], ps[:n, :], s_hh[:, :], ALU.mult), [Rps], [res("s_hh")])
                sp.dma(ds_sz, conv_s[l, :, 1, :], s_hh[:, :], [res("s_hh")], [])

            tm_block(2, c_cg)

            def c_bg(ps, Rps):
                dve.op(lambda e: e.tensor_tensor(s_t[:, :], s_cprev[:, 0, :], s_convw[:, 0, :], ALU.mult),
                       [R_act[0]], [res("s_t")])
                dve.op(lambda e: e.tensor_tensor(s_d[:, :], s_cprev[:, 1, :], s_convw[:, 1, :], ALU.mult),
                       [R_act[0]], [res("s_d")])
                dve.op(lambda e: e.tensor_tensor(s_t[:, :], s_t[:, :], s_d[:, :], ALU.add), [res("s_d")], [res("s_t")])
                dve.op(lambda e: e.tensor_tensor(s_d[:, :], s_hh[:, :], s_convw[:, 2, :], ALU.mult), [res("s_hh"), R_act[0]], [res("s_d")])
                dve.op(lambda e: e.tensor_tensor(s_t[:, :], s_t[:, :], s_d[:, :], ALU.add), [res("s_d")], [res("s_t")])
                dve.op(lambda e: e.tensor_tensor(s_mix[:, 0:GW], ps[:n, :], s_t[:, :], ALU.mult), [Rps, res("s_t")], [res("s_mix")])

            tm_block(1, c_bg)

            def c_u(ps, Rps):
                act.op(lambda e: e.copy(s_u[:, :], ps[:n, :]), [Rps], [res("s_u")])

            tm_block(8, c_u)

            def c_q(ps, Rps):
                rotary(s_q[:, :], ps, NS, cos_s[:, :], sin_s[:, :], Rps, res("s_q"), res("consts"))

            def c_k(ps, Rps):
                rotary(s_k[:, :], ps, NS, cos_s[:, :], sin_s[:, :], Rps, res("s_k"), res("consts"))
                dve.op(lambda e: e.tensor_scalar(s_k[:, :], s_k[:, :], 128.0 ** -0.5, None, ALU.mult), [], [res("s_k")])

            def c_v(ps, Rps):
                act.op(lambda e: e.copy(s_v[:, :], ps[:n, :]), [Rps], [res("s_v")])

            def c_g(ps, Rps):
                act.op(lambda e: e.activation(s_g[:, :], ps[:n, :], AF.Silu), [Rps], [res("s_g")])

            def c_vv(ps, Rps):
                layernorm_heads(ps, NS, Rps, vnf, nheads=1, width=GW)
                dve.op(lambda e: e.tensor_tensor(vnf[:n, :], vnf[:n, :], sgug[:NS, :], ALU.mult), [res("ltab")], [res("lnout")])
                sp.dma(ds_sv, sgu_s[l], vnf[:n, :], [res("lnout")], [])
                for h in range(4):
                    hs = slice(h * 128, (h + 1) * 128)
                    dve.op(lambda e, h=h, hs=hs: e.tensor_scalar(s_t[:, hs], vnf[:n, hs], sgu00[:, l, 0, h:h + 1], sgu00[:, l, 1, h:h + 1],
                                                                 ALU.mult, ALU.add), [res("lnout"), res("consts")], [res("s_t")])
                dve.op(lambda e: e.tensor_tensor(s_mix[:, 2 * GW:3 * GW], s_t[:, :], s_u[:, :], ALU.mult),
                       [res("s_t"), res("s_u")], [res("s_mix")])

            tm_block(4, c_q)
            tm_block(5, c_k)
            tm_block(6, c_v)
            tm_block(7, c_g)
            tm_block(9, c_vv)

            dve.op(lambda e: e.tensor_tensor(s_t[:, :], s_q[:, :], s_k[:, :], ALU.mult), [res("s_q"), res("s_k")], [res("s_t")])
            dve.op(lambda e: e.tensor_reduce(s_sc[:, :], s_t[:, :].rearrange("p (h e) -> p h e", h=4), AX.X, ALU.add),
                   [res("s_t")], [res("s_sc")])
            pt, Rpt = nxt("a")
            for h in range(4):
                pe.op(lambda e, h=h: e.transpose(pt[:, h * NS:(h + 1) * NS], s_q[:, h * 128:(h + 1) * 128], ident[:NS, :NS]),
                      [res("s_q"), res("consts")], [Rpt], inc=(h == 3))
            act.op(lambda e: e.copy(s_qT[:], pt[:, 0:4 * NS].rearrange("p (h b) -> p h b", h=4)), [Rpt], [res("s_qT")])
            poS, RpoS = nxt("a")
            NQ = NS // 4
            RSq = [res("S_q0"), res("S_q1")]

            def s_load(qb):
                bi = qb % 2
                extra = (R_act + R_bigp) if qb < 2 else []
                sp.dma(ds_sSq[bi], S_qs[bi], st_ret[l, qb * 4:qb * 4 + 4].rearrange("b h d e -> d b h e"), [], [RSq[bi]] + extra)

            s_load(0)
            for qb in range(NQ):
                b0 = qb * 4
                bi = qb % 2
                S_half, RSh = S_qs[bi], RSq[bi]
                if qb + 1 < NQ:
                    s_load(qb + 1)
                for bb in range(4):
                    b = b0 + bb
                    for h in range(4):
                        pe.op(lambda e, b=b, bb=bb, h=h, S_half=S_half: e.matmul(poS[:, h * NS + b:h * NS + b + 1], S_half[:, bb, h, :],
                                                                                s_qT[:, h, b:b + 1], start=True, stop=True),
                              [res("s_qT"), RSh], [RpoS], inc=(bb == 3 and h == 3))
                for bb in range(4):
                    b = b0 + bb
                    kdg = s_kdiag[0]
                    Rk = res("s_kdiag0")
                    dve.op(lambda e, b=b, kdg=kdg: e.tensor_scalar(kdg[:, :], s_k[:, :], eye16[:, b:b + 1], None, ALU.mult),
                           [res("s_k"), res("consts")], [Rk])
                    pkv, Rpkv = nxt("m")
                    for h in range(4):
                        hs = slice(h * 128, (h + 1) * 128)
                        pe.op(lambda e, hs=hs, kdg=kdg: e.matmul(pkv[:, hs], kdg[:, hs], s_v[:, hs], start=True, stop=True),
                              [Rk, res("s_v")], [Rpkv], inc=(h == 3))
                    for h in range(4):
                        dve.op(lambda e, bb=bb, h=h, S_half=S_half: e.scalar_tensor_tensor(S_half[:, bb, h, :], S_half[:, bb, h, :], GAM[h],
                                                                                          pkv[:, h * 128:(h + 1) * 128], ALU.mult, ALU.add),
                               [Rpkv], [RSh])
                sp.dma(ds_sSq[bi], ret_s[l, b0:b0 + 4].rearrange("b h d e -> d b h e"), S_half,
                       [RSh] + (R_act if qb >= NQ - 2 else []), [])
            act.op(lambda e: e.copy(s_oT[:], poS[:, 0:4 * NS].rearrange("p (h b) -> p h b", h=4)), [RpoS], [res("s_oT")])
            po, Rpo = nxt("a")
            for h in range(4):
                pe.op(lambda e, h=h: e.transpose(po[:NS, h * 128:(h + 1) * 128], s_oT[:, h, :], ident[:]),
                      [res("s_oT"), res("consts")], [Rpo], inc=(h == 3))
            for h in range(4):
                hs = slice(h * 128, (h + 1) * 128)
                dve.op(lambda e, h=h, hs=hs: e.tensor_scalar(s_t[:, hs], s_v[:, hs], s_sc[:, h:h + 1], None, ALU.mult),
                       [res("s_v"), res("s_sc")], [res("s_t")])
                dve.op(lambda e, h=h, hs=hs: e.scalar_tensor_tensor(onb[:NS, hs], po[:NS, hs], GAM[h], s_t[:, hs], ALU.mult, ALU.add),
                       [Rpo, res("s_t")], [res("lnout")])
            layernorm_heads(onb, NS, res("lnout"), onb)
            dve.op(lambda e: e.tensor_tensor(onb[:NS, :], onb[:NS, :], retg[:NS, :], ALU.mult), [res("ltab")], [res("lnout")])
            dve.op(lambda e: e.tensor_tensor(s_mix[:, GW:2 * GW], onb[:NS, :], s_g[:, :], ALU.mult),
                   [res("lnout"), res("s_g")], [res("s_mix")])
            for m4 in range(1, 4):
                pt, Rpt = nxt("a")
                for kk in range(4):
                    m = m4 * 4 + kk
                    pe.op(lambda e, m=m, kk=kk: e.transpose(pt[:, kk * NS:(kk + 1) * NS], s_mix[:, (m - 4) * 128:(m - 3) * 128], ident[:NS, :NS]),
                          [res("s_mix"), res("consts")], [Rpt], inc=(kk == 3))
                act.op(lambda e, m4=m4: e.copy(mixT[:, m4 * 4:m4 * 4 + 4, G:G + NS], pt[:, 0:4 * NS].rearrange("p (a b) -> p a b", a=4)),
                       [Rpt], [R_mix[m4 * 4 + kk] for kk in range(4)])

        ds_xb = [mkds(f"ds_x{i}") for i in range(3)]
        ds_lt = mkds("ds_lt")
        ds_cs = mkds("ds_cs")
        ds_st = mkds("ds_st")
        ds_sg = mkds("ds_sg")
        ds_rp = mkds("ds_rp")
        ds_sst = mkds("ds_sst")
        ds_cp = mkds("ds_cp")
        ds_sa = mkds("ds_sa")
        ds_sz = mkds("ds_sz")
        ds_sv = mkds("ds_sv")
        ds_sSq = [mkds("ds_sS0"), mkds("ds_sS1")]

        for g in range(NG):
            prompt_group(g, g == NG - 1)

        for d in dsems:
            if d.cnt:
                nc.sync.wait_ge(d.sem, d.cnt)
    return nc


def _tables():
    f32 = np.float32
    half = 64
    inv = (np.float32(10000.0) ** (-np.arange(half, dtype=f32) / f32(half))).astype(f32)
    pos = np.arange(SEQ, dtype=f32)
    ang = pos[:, None] * inv[None, :]
    cos_t = np.cos(ang).astype(f32).reshape(16, 128, 64).transpose(1, 0, 2)
    sin_t = np.sin(ang).astype(f32).reshape(16, 128, 64).transpose(1, 0, 2)
    angs = (np.full((NS,), PAST, dtype=f32)[:, None] * inv[None, :]).astype(f32)
    cos_s = np.cos(angs).astype(f32)
    sin_s = np.sin(angs).astype(f32)
    lg = np.log1p(-(2.0 ** (-5.0 - np.arange(4, dtype=np.float64))))
    idx = np.arange(128, dtype=np.float64)
    diff = idx[None, :] - idx[:, None]
    s = 128.0 ** -0.5
    maskT = np.where(diff[:, None, :] >= 0, np.exp(np.maximum(diff, 0.0)[:, None, :] * lg[None, :, None]), 0.0) * s
    qdec = np.broadcast_to(np.exp((idx + 1.0)[None, None, :] * lg[None, :, None]), (128, 4, 128))
    kdec = np.exp((127.0 - idx)[:, None] * lg[None, :]) * s
    tri = (idx[:, None] <= idx[None, :]).astype(f32)
    invc0 = np.zeros((128, 2, 4, 16), f32)
    for gi, w in enumerate(WINS):
        invc0[:, 0, gi, :] = 1.0 / np.minimum(np.arange(16) + 1, w)
        invc0[:, 1, gi, :] = 1.0 / w
    return dict(cos_t=np.ascontiguousarray(cos_t), sin_t=np.ascontiguousarray(sin_t), cos_s=cos_s, sin_s=sin_s,
                maskT=maskT.astype(f32), qdec=np.ascontiguousarray(qdec).astype(f32), kdec=kdec.astype(f32), tri=tri,
                ident=np.eye(128, dtype=f32), invc0=invc0, eye16=np.eye(NS, dtype=f32))


_NC = None


def kernel(x_prompt, x_sample, state_pool, state_conv, state_ret, norm1_g, w_in, pool_w, pool_scale, conv_w, ret_norm_g,
           sgu_norm_g, sgu_w, sgu_b, w_out, norm2_g, w_gate_up, w_down, final_norm_g):
    global _NC
    f32 = np.float32
    A = lambda a: np.ascontiguousarray(np.asarray(a, dtype=f32))
    x_prompt, x_sample, state_pool, state_conv, state_ret = map(A, (x_prompt, x_sample, state_pool, state_conv, state_ret))
    w_in, w_out, w_gate_up, w_down, pool_w = map(A, (w_in, w_out, w_gate_up, w_down, pool_w))
    norm1_g, norm2_g, final_norm_g, pool_scale, conv_w = map(A, (norm1_g, norm2_g, final_norm_g, pool_scale, conv_w))
    ret_norm_g, sgu_norm_g, sgu_w, sgu_b = map(A, (ret_norm_g, sgu_norm_g, sgu_w, sgu_b))
    if _NC is None:
        _NC = build_program()
    nc = _NC
    gl = np.stack([norm1_g[0], norm2_g[0], norm1_g[1], norm2_g[1], final_norm_g])
    gcols = A(gl.reshape(5, 16, 128).transpose(2, 0, 1))
    pscale = A(pool_scale.reshape(NL, 4, 128).transpose(2, 0, 1))
    convw = A(conv_w.reshape(NL, 3, 4, 128).transpose(3, 0, 1, 2))
    convwb = A(np.broadcast_to(conv_w[None], (NS, NL, 3, GW)))
    retg = A(np.broadcast_to(ret_norm_g[None], (128, NL, GW)))
    sgug = A(np.broadcast_to(sgu_norm_g[None], (128, NL, GW)))
    sguwT = A(sgu_w.transpose(3, 0, 1, 2))
    sgub = A(np.broadcast_to(sgu_b[None], (128, NL, 4, 128)))
    sgu00 = A(np.broadcast_to(np.stack([sgu_w[:, :, 0, 0], sgu_b[:, :, 0]], axis=1)[None], (NS, NL, 2, 4)))
    tabs = _tables()
    shared = dict(w_in=w_in, w_out=w_out, w_gu=w_gate_up, w_dn=w_down, pool_w=pool_w, gcols=gcols, pscale=pscale, convw=convw,
                  convwb=convwb, retg=retg, sgug=sgug, sguwT=sguwT, sgub=sgub, sgu00=sgu00, **tabs)
    cosA = A(np.concatenate([tabs["cos_t"][:, 0:8], tabs["cos_t"][:, 0:8]], axis=1))
    sinA = A(np.concatenate([tabs["sin_t"][:, 0:8], tabs["sin_t"][:, 0:8]], axis=1))
    invcA = A(tabs["invc0"][:, [0, 0]].reshape(128, 8, 16))
    invcB = A(tabs["invc0"][:, [0, 1]].reshape(128, 8, 16))
    zhalf = np.zeros((SEQ // 2, D), f32)
    in_maps = []
    for c in range(8):
        i, role = c // 2, c % 2
        s0 = c * NS
        m = dict(shared)
        m.update(xs=A(x_sample[s0:s0 + NS, 0, :]), st_pool=A(state_pool[:, s0:s0 + NS]),
                 st_conv=A(state_conv[:, s0:s0 + NS]), st_ret=A(state_ret[:, s0:s0 + NS]))
        if role == 0:
            m.update(xp=A(np.concatenate([zhalf, x_prompt[i, 0:SEQ // 2]], axis=0)), cos_t=cosA, sin_t=sinA, invc0=invcA)
        else:
            m.update(xp=x_prompt[i], invc0=invcB)
        in_maps.append(m)
    res = run_bass_kernel_spmd(nc, in_maps, core_ids=list(range(8)))
    r = res.results
    y_prompt = np.stack([np.concatenate([r[2 * b]["y_p"], r[2 * b + 1]["y_p"]], axis=0) for b in range(4)])
    y_sample = np.concatenate([r[c]["y_s"] for c in range(8)])[:, None, :]
    pool_prompt = np.stack([r[2 * b + 1]["pool_p"] for b in range(4)], axis=1)
    pool_sample = np.concatenate([r[c]["pool_s"] for c in range(8)], axis=1)
    conv_prompt = np.stack([r[2 * b + 1]["conv_p"] for b in range(4)], axis=1)
    conv_sample = np.concatenate([r[c]["conv_s"] for c in range(8)], axis=1)
    ret_prompt = np.stack([r[2 * b + 1]["ret_p"] for b in range(4)], axis=1)
    ret_sample = np.concatenate([r[c]["ret_s"] for c in range(8)], axis=1)
    sgu_prompt = np.stack([r[2 * b + 1]["sgu_p"] for b in range(4)], axis=1)
    sgu_sample = np.concatenate([r[c]["sgu_s"] for c in range(8)], axis=1)[:, :, None, :]
    outs = (y_prompt, y_sample, pool_prompt, pool_sample, conv_prompt, conv_sample, ret_prompt, ret_sample, sgu_prompt, sgu_sample)
    return tuple(np.ascontiguousarray(o, dtype=f32) for o in outs)
```

```python
import numpy as np
from contextlib import ExitStack
import concourse.bass as bass
import concourse.mybir as mybir
from concourse.bass_utils import run_bass_kernel_spmd

F32 = mybir.dt.float32
BF16 = mybir.dt.bfloat16
ALU = mybir.AluOpType
AF = mybir.ActivationFunctionType
AX = mybir.AxisListType

D = 2048
GW = 512
NL = 2
FF = 5632
NF = FF // 128
SEQ = 2048
G = 512
NG = SEQ // G
NS = 16
GT = G + 32
EPS = 1e-6
PAST = 16384
GAM = [1.0 - 2.0 ** (-5.0 - h) for h in range(4)]
WINS = (2, 4, 8, 16)


class Res:
    __slots__ = ("name", "w", "r", "excl")

    def __init__(self, name, excl=False):
        self.name = name
        self.w = None
        self.r = {}
        self.excl = excl


class DSem:
    def __init__(self, nc, es, name):
        self.sem = es.enter_context(nc.semaphore(name))
        self.cnt = 0


class Eng:
    def __init__(self, nc, es, h, name, self_sync=True):
        self.h = h
        self.name = name
        self.sem = es.enter_context(nc.semaphore("sem_" + name))
        self.cnt = 0
        self.seen = {}
        self.self_sync = self_sync
        self.pend = []

    def wait(self, tok):
        if tok is None:
            return
        sem, val = tok
        if sem is self.sem and not self.self_sync:
            return
        k = id(sem)
        if self.seen.get(k, 0) >= val:
            return
        self.h.wait_ge(sem, val)
        self.seen[k] = val

    def _deps(self, reads, writes):
        ws = list(writes) + [r for r in reads if r.excl]
        rs = [r for r in reads if not r.excl]
        for r in rs:
            self.wait(r.w)
        for w in ws:
            self.wait(w.w)
            for tok in list(w.r.values()):
                self.wait(tok)
        return rs, ws

    @staticmethod
    def _apply(tok, rs, ws):
        for r in rs:
            r.r[id(tok[0])] = tok
        for w in ws:
            w.w = tok
            w.r = {}

    def op(self, fn, reads=(), writes=(), inc=True):
        rs, ws = self._deps(reads, writes)
        ins = fn(self.h)
        if inc:
            self.cnt += 1
            ins.then_inc(self.sem, 1)
            tok = (self.sem, self.cnt)
            for (prs, pws) in self.pend:
                self._apply(tok, prs, pws)
            self.pend = []
            self._apply(tok, rs, ws)
        else:
            self.pend.append((rs, ws))

    def dma(self, ds, out, in_, reads=(), writes=(), **kw):
        rs, ws = self._deps(reads, writes)
        ins = self.h.dma_start(out=out, in_=in_, **kw)
        ds.cnt += 16
        ins.then_inc(ds.sem, 16)
        tok = (ds.sem, ds.cnt)
        self._apply(tok, rs, ws)
        return rs, ws


def dma_batch(eng, ds, items, **kw):
    allr, allw = [], []
    for (o, i, rs, ws) in items:
        r2, w2 = eng.dma(ds, o, i, rs, ws, **kw)
        allr += r2
        allw += w2
    tok = (ds.sem, ds.cnt)
    Eng._apply(tok, allr, allw)


def build_program():
    nc = bass.Bass("TRN2", target_bir_lowering=False)

    def din(name, shape):
        return nc.dram_tensor(name, list(shape), F32, kind="ExternalInput").ap()

    def dout(name, shape):
        return nc.dram_tensor(name, list(shape), F32, kind="ExternalOutput").ap()

    xp = din("xp", [SEQ, D])
    xs = din("xs", [NS, D])
    st_pool = din("st_pool", [NL, NS, 15, GW])
    st_conv = din("st_conv", [NL, NS, 2, GW])
    st_ret = din("st_ret", [NL, NS, 4, 128, 128])
    w_in = din("w_in", [NL, D, 10 * GW])
    w_out = din("w_out", [NL, D, D])
    w_gu = din("w_gu", [NL, D, 2 * FF])
    w_dn = din("w_dn", [NL, FF, D])
    pool_w = din("pool_w", [NL, 4, 128, 128])
    gcols_d = din("gcols", [128, 5, 16])
    pscale_d = din("pscale", [128, NL, 4])
    convw_d = din("convw", [128, NL, 3, 4])
    convwb_d = din("convwb", [NS, NL, 3, GW])
    retg_d = din("retg", [128, NL, GW])
    sgug_d = din("sgug", [128, NL, GW])
    sguwT_d = din("sguwT", [128, NL, 4, 128])
    sgub_d = din("sgub", [128, NL, 4, 128])
    sgu00_d = din("sgu00", [NS, NL, 2, 4])
    cos_d = din("cos_t", [128, 16, 64])
    sin_d = din("sin_t", [128, 16, 64])
    coss_d = din("cos_s", [NS, 64])
    sins_d = din("sin_s", [NS, 64])
    maskT_d = din("maskT", [128, 4, 128])
    qdec_d = din("qdec", [128, 4, 128])
    kdec_d = din("kdec", [128, 4])
    tri_d = din("tri", [128, 128])
    ident_d = din("ident", [128, 128])
    invc0_d = din("invc0", [128, 8, 16])
    eye16_d = din("eye16", [NS, NS])

    y_p = dout("y_p", [SEQ // 2, D])
    y_s = dout("y_s", [NS, D])
    pool_p = dout("pool_p", [NL, 15, GW])
    pool_s = dout("pool_s", [NL, NS, 15, GW])
    conv_p = dout("conv_p", [NL, 2, GW])
    conv_s = dout("conv_s", [NL, NS, 2, GW])
    ret_p = dout("ret_p", [NL, 4, 128, 128])
    ret_s = dout("ret_s", [NL, NS, 4, 128, 128])
    sgu_p = dout("sgu_p", [NL, 128, GW])
    sgu_s = dout("sgu_s", [NL, NS, GW])

    with ExitStack() as es:
        def sb(name, shape, dt=F32):
            return es.enter_context(nc.sbuf_tensor("sb_" + name, list(shape), dt))

        def psum(name, shape, dt=F32):
            return es.enter_context(nc.psum_tensor("ps_" + name, list(shape), dt))

        pe = Eng(nc, es, nc.tensor, "pe", self_sync=False)
        dve = Eng(nc, es, nc.vector, "dve")
        act = Eng(nc, es, nc.scalar, "act")
        pool = Eng(nc, es, nc.gpsimd, "pool")
        sp = Eng(nc, es, nc.sync, "sp")
        dsems = []

        def mkds(name):
            d = DSem(nc, es, name)
            dsems.append(d)
            return d

        hT = sb("hT", [128, 16, GT])
        xnT = sb("xnT", [128, 16, GT], BF16)
        mixT = sb("mixT", [128, 16, GT], BF16)
        big = sb("big", [128, NF * GT // 2])
        arena = sb("arena", [128, 8704])

        def carve(base, off, shape, dt=F32):
            nfl = int(np.prod(shape[1:]))
            if dt == BF16:
                v = base[:, off:off + (nfl + 1) // 2].bitcast(BF16)[:, 0:nfl]
                used = (nfl + 1) // 2
            else:
                v = base[:, off:off + nfl]
                used = nfl
            v = v[0:shape[0]]
            if len(shape) == 3:
                v = v.rearrange("p (a b) -> p a b", a=shape[1])
            elif len(shape) == 4:
                v = v.rearrange("p (a b c) -> p a b c", a=shape[1], b=shape[2])
            return v, off + used

        actT = big[:].bitcast(BF16).rearrange("p (j t) -> p j t", t=GT)
        o = 0
        aext, o = carve(big, o, [128, 4, 15 + G])
        zext, o = carve(big, o, [128, 4, 2 + G])
        hh_sb, o = carve(big, o, [128, 4, G])
        u_sb, o = carve(big, o, [128, 4, G], BF16)
        vnb, o = carve(big, o, [128, 4, GW], BF16)
        gs, o = carve(big, o, [128, 4, GW], BF16)
        assert o <= 10688
        pFM, _ = carve(big, 10688, [128, 40, NS])
        sguw_f, _ = carve(big, 0, [128, NL, 4, 128])
        o = 0
        S_q0, o = carve(big, o, [128, 4, 4, 128])
        S_q1, o = carve(big, o, [128, 4, 4, 128])
        S_qs = [S_q0, S_q1]
        s_cprev, o = carve(big, o, [NS, 2, GW])
        s_convw, o = carve(big, o, [NS, 3, GW])
        s_prev4, o = carve(big, o, [NS, 26, 128])
        assert o <= 10688
        o = 0
        xins = [carve(arena, i * D, [128, D])[0] for i in range(3)]
        xrot = [0]
        qrot, o = carve(arena, o, [128, 4, GW], BF16)
        krot, o = carve(arena, o, [128, 4, GW], BF16)
        kd, o = carve(arena, o, [128, 4, GW], BF16)
        vbf, o = carve(arena, o, [128, 4, GW], BF16)
        qT, o = carve(arena, o, [128, 4, 128], BF16)
        qdT, o = carve(arena, o, [128, 4, 128], BF16)
        kT, o = carve(arena, o, [128, 4, 128], BF16)
        scT, o = carve(arena, o, [128, 4, 128], BF16)
        ycb, o = carve(arena, o, [128, GW], BF16)
        tmp4, o = carve(arena, o, [128, 4, 128])
        pt0, o = carve(arena, o, [128, 15 + G])
        pt1, o = carve(arena, o, [128, 15 + G])
        ptmp = [pt0, pt1]
        Sst, o = carve(arena, o, [128, NL, 4, 128])
        Sbf, o = carve(arena, o, [128, NL, 4, 128], BF16)
        carry_a, o = carve(arena, o, [128, NL, 4, 15])
        carry_z, o = carve(arena, o, [128, NL, 4, 2])
        assert o <= 8704, o
        o = 0
        s_a, o = carve(arena, o, [NS, GW])
        s_d, o = carve(arena, o, [NS, GW])
        s_hh, o = carve(arena, o, [NS, GW])
        s_u, o = carve(arena, o, [NS, GW])
        s_q, o = carve(arena, o, [NS, GW])
        s_k, o = carve(arena, o, [NS, GW])
        s_v, o = carve(arena, o, [NS, GW])
        s_t, o = carve(arena, o, [NS, GW])
        s_g, o = carve(arena, o, [NS, GW])
        s_mix, o = carve(arena, o, [NS, 3 * GW])
        s_kdiag0, o = carve(arena, o, [NS, GW])
        s_kdiag = [s_kdiag0, s_kdiag0]
        s_qT, o = carve(arena, o, [128, 4, NS])
        s_oT, o = carve(arena, o, [128, 4, NS])
        s_sc, o = carve(arena, o, [NS, 4])
        assert o <= 6942, o
        NSLOT = 3
        wslot = [sb(f"wslot{i}", [128, 4096], BF16) for i in range(NSLOT)]
        rstd = sb("rstd", [128, G])
        sqb = [sb(f"sqb{i}", [128, G], BF16) for i in range(2)]
        dT = sb("dT", [128, G], BF16)
        dT2 = sb("dT2", [128, G], BF16)
        dT3 = sb("dT3", [128, G], BF16)
        dTs = [dT, dT2, dT3]
        cacc = sb("cacc", [128, G])
        vnf = sb("vnf", [128, GW])
        rt = [sb(f"rt{i}", [128, 4, 64]) for i in range(2)]
        onb = sb("onb", [128, GW])
        stt = sb("stt", [128, 4, 6])
        mv = sb("mv", [128, 4, 2])
        sd = sb("sd", [128, 4])
        nb = sb("nb", [128, 4])
        ident = sb("ident", [128, 128])
        identb = sb("identb", [128, 128], BF16)
        ones_b = sb("ones_b", [128, 128], BF16)
        epsc = sb("epsc", [128, 1])
        gcols = sb("gcols", [128, 5, 16])
        pscale = sb("pscale", [128, NL, 4])
        convw = sb("convw", [128, NL, 3, 4])
        retg = sb("retg", [128, GW])
        sgug = sb("sgug", [128, GW])
        WsT = sb("WsT", [128, NL, 4, 128], BF16)
        sgub = sb("sgub", [128, 4, 128])
        sgu00 = sb("sgu00", [NS, NL, 2, 4])
        poolw = sb("poolw", [128, NL * 4, 128], BF16)
        cos_g = sb("cos_g", [128, 4, 64])
        sin_g = sb("sin_g", [128, 4, 64])
        cos_s = sb("cos_s", [NS, 64])
        sin_s = sb("sin_s", [NS, 64])
        maskT = sb("maskT", [128, 4, 128])
        qdec = sb("qdec", [128, 4, 128])
        kdec = sb("kdec", [128, 4])
        tri = sb("tri", [128, 128])
        invc0 = sb("invc0", [128, 8, 16])
        eye16 = sb("eye16", [NS, NS])

        pm = [psum(f"pm{i}", [128, 512]) for i in range(4)]
        pa = [psum(f"pa{i}", [128, 512]) for i in range(2)]
        pb = [psum(f"pb{i}", [128, 1024], BF16) for i in range(2)]
        R_pm = [Res(f"pm{i}", True) for i in range(4)]
        R_pa = [Res(f"pa{i}", True) for i in range(2)]
        R_pb = [Res(f"pb{i}", True) for i in range(2)]
        rot = {"m": 0, "a": 0, "b": 0}

        mlimit = [4]

        def nxt(kind):
            lst, rl = {"m": (pm, R_pm), "a": (pa, R_pa), "b": (pb, R_pb)}[kind]
            i = rot[kind] % (mlimit[0] if kind == "m" else len(lst))
            rot[kind] += 1
            return lst[i], rl[i]

        R = {}

        def res(name):
            if name not in R:
                R[name] = Res(name)
            return R[name]

        R_h = [res(f"h{k}") for k in range(16)]
        R_xn = [res(f"xn{k}") for k in range(16)]
        R_mix = [res(f"mix{k}") for k in range(16)]
        R_act = [res(f"act{j}") for j in range(NF)]
        R_slot = [res(f"slot{i}") for i in range(NSLOT)]
        slot_ds = [mkds(f"ds_slot{i}") for i in range(NSLOT)]
        slot_rot = [0]

        ds_setup = mkds("ds_setup")
        setup_items = []
        for (t, d) in [(ident, ident_d), (gcols, gcols_d), (pscale, pscale_d), (convw, convw_d),
                       (sguw_f, sguwT_d), (sgu00, sgu00_d), (cos_s, coss_d), (sin_s, sins_d), (maskT, maskT_d), (qdec, qdec_d),
                       (kdec, kdec_d), (tri, tri_d), (invc0, invc0_d), (eye16, eye16_d)]:
            setup_items.append((t if t is sguw_f else t[:], d, [], [res("consts")]))
        dma_batch(sp, ds_setup, setup_items)
        ds_setup2 = mkds("ds_setup2")
        dma_batch(pool, ds_setup2, [
            (identb[:], ident_d, [], [res("consts2")]),
            (poolw[:], pool_w.rearrange("l g c e -> c (l g) e"), [], [res("consts2")]),
        ])
        RC = [res("consts"), res("consts2"), res("consts3")]
        dve.op(lambda e: e.memset(ones_b[:], 1.0), [], [res("consts3")])
        dve.op(lambda e: e.memset(epsc[:], EPS), [], [res("consts3")])
        dve.op(lambda e: e.memset(Sst[:], 0.0), [], [res("S0"), res("S1")])
        dve.op(lambda e: e.memset(Sbf[:], 0.0), [], [res("Sbf0"), res("Sbf1")])
        dve.op(lambda e: e.memset(carry_a[:], 0.0), [], [res("ca0"), res("ca1")])
        dve.op(lambda e: e.memset(carry_z[:], 0.0), [], [res("cz0"), res("cz1")])
        dve.op(lambda e: e.tensor_tensor(WsT[:].rearrange("p l h t -> p (l h) t"),
                                         sguw_f.rearrange("p l h t -> p (l h) t"),
                                         tri[:].unsqueeze(1).to_broadcast([128, NL * 4, 128]), ALU.mult),
               RC, [res("consts3")])

        def load_w(view_shape, src):
            i = slot_rot[0] % NSLOT
            slot_rot[0] += 1
            n = int(np.prod(view_shape[1:]))
            v = wslot[i][:, 0:n]
            if len(view_shape) == 3:
                v = v.rearrange("p (a b) -> p a b", a=view_shape[1])
            elif len(view_shape) == 4:
                v = v.rearrange("p (a b c) -> p a b c", a=view_shape[1], b=view_shape[2])
            pool.dma(slot_ds[i], v, src, [], [R_slot[i]])
            return v, R_slot[i]

        def rmsnorm(n, gidx, out_xn=True, c0=0):
            cs = slice(c0, c0 + n)
            ss, Rss = nxt("a")
            for k in range(16):
                s = sqb[k % 2]
                Rs = res(f"sqb{k % 2}")
                act.op(lambda e, s=s, k=k: e.activation(s[:, :n], hT[:, k, cs], AF.Square), [R_h[k]], [Rs])
                pe.op(lambda e, s=s, k=k: e.matmul(ss[:, :n], ones_b[:], s[:, :n], start=(k == 0), stop=(k == 15)),
                      [Rs, res("consts3")], [Rss], inc=True)
            act.op(lambda e: e.activation(rstd[:, :n], ss[:, :n], AF.Sqrt, bias=epsc[:], scale=1.0 / D),
                   [Rss, res("consts3")], [res("rstd")])
            dve.op(lambda e: e.reciprocal(rstd[:, :n], rstd[:, :n]), [], [res("rstd")])
            for k in range(16):
                if out_xn:
                    dve.op(lambda e, k=k: e.scalar_tensor_tensor(xnT[:, k, cs], hT[:, k, cs], gcols[:, gidx, k:k + 1],
                                                                 rstd[:, :n], ALU.mult, ALU.mult),
                           [R_h[k], res("rstd"), res("consts")], [R_xn[k]])
                else:
                    dve.op(lambda e, k=k: e.scalar_tensor_tensor(hT[:, k, cs], hT[:, k, cs], gcols[:, gidx, k:k + 1],
                                                                 rstd[:, :n], ALU.mult, ALU.mult),
                           [res("rstd"), res("consts")], [R_h[k]])

        def load_ltab(l):
            dma_batch(sp, ds_lt, [(retg[:], retg_d[:, l, :], [], [res("ltab")]),
                                  (sgug[:], sgug_d[:, l, :], [], [res("ltab")]),
                                  (sgub[:], sgub_d[:, l, :, :], [], [res("ltab")])])

        def load_x(n, src_rows):
            for (src, nt, c0) in src_rows:
                bi = xrot[0] % 3
                xrot[0] += 1
                xin, Rx = xins[bi], res(f"xin{bi}")
                R_ar = ([res(f"{nm}{i}") for nm in ("qrot", "krot", "kd", "vbf") for i in range(4)]
                        + [res(nm) for nm in ("qT", "qdT", "kT", "scT", "ycb", "tmp4", "ptmp0", "ptmp1")])
                sp.dma(ds_xb[bi], xin[:nt, :], src, [], [Rx] + R_ar)
                for k4 in range(4):
                    pt, Rpt = nxt("a")
                    for kk in range(4):
                        k = k4 * 4 + kk
                        pe.op(lambda e, k=k, kk=kk: e.transpose(pt[:, kk * 128:kk * 128 + nt], xin[:nt, k * 128:(k + 1) * 128],
                                                                ident[:nt, :nt]),
                              [Rx, res("consts")], [Rpt], inc=(kk == 3))
                    act.op(lambda e, k4=k4: e.copy(hT[:, k4 * 4:k4 * 4 + 4, c0:c0 + nt],
                                                   pt[:, 0:512].rearrange("p (a b) -> p a b", a=4)[:, :, :nt]),
                           [Rpt], [R_h[k4 * 4 + kk] for kk in range(4)])

        def store_y(n, dst_rows):
            for (dst, nt, c0) in dst_rows:
                bi = xrot[0] % 3
                xrot[0] += 1
                xin, Rx = xins[bi], res(f"xin{bi}")
                for k4 in range(4):
                    pt, Rpt = nxt("a")
                    for kk in range(4):
                        k = k4 * 4 + kk
                        pe.op(lambda e, k=k, kk=kk: e.transpose(pt[:nt, kk * 128:(kk + 1) * 128], hT[:, k, c0:c0 + nt], ident[:]),
                              [R_h[k], res("consts")], [Rpt], inc=(kk == 3))
                    act.op(lambda e, k4=k4: e.copy(xin[:nt, k4 * 512:(k4 + 1) * 512], pt[:nt, :]), [Rpt], [Rx])
                sp.dma(ds_xb[bi], dst, xin[:nt, :], [Rx], [])

        def wout_and_ffn(l, n, ns=0):
            c1 = G
            if ns:
                pQ, RpQ = nxt("a")
            for dp in range(8):
                wv, Rw = load_w([128, 16, 256], w_out[l, :, dp * 256:(dp + 1) * 256].rearrange("(k p) c -> p k c", p=128))
                for dd in range(2):
                    d = dp * 2 + dd
                    ps, Rps = nxt("m")
                    for k in range(16):
                        pe.op(lambda e, k=k, dd=dd: e.matmul(ps[:, :n], wv[:, k, dd * 128:(dd + 1) * 128], mixT[:, k, :n],
                                                             start=(k == 0), stop=(k == 15)),
                              [Rw, R_mix[k]], [Rps], inc=(k == 15))
                    dve.op(lambda e, d=d: e.tensor_tensor(hT[:, d, :n], hT[:, d, :n], ps[:, :n], ALU.add), [Rps], [R_h[d]])
                    if ns:
                        for k in range(16):
                            pe.op(lambda e, k=k, dd=dd, d=d: e.matmul(pQ[:, d * ns:(d + 1) * ns], wv[:, k, dd * 128:(dd + 1) * 128],
                                                                     mixT[:, k, c1:c1 + ns], start=(k == 0), stop=(k == 15)),
                                  [Rw, R_mix[k]], [RpQ], inc=(k == 15))
            if ns:
                dve.op(lambda e: e.tensor_tensor(hT[:, :, c1:c1 + ns], hT[:, :, c1:c1 + ns],
                                                 pQ[:, 0:16 * ns].rearrange("p (a b) -> p a b", a=16), ALU.add), [RpQ], R_h)
            rmsnorm(n, 2 * l + 1)
            if ns:
                rmsnorm(ns, 2 * l + 1, c0=c1)
            for jp in range(NF // 2):
                wg, Rwg = load_w([128, 16, 256], w_gu[l, :, jp * 256:(jp + 1) * 256].rearrange("(k p) c -> p k c", p=128))
                wu, Rwu = load_w([128, 16, 256], w_gu[l, :, FF + jp * 256:FF + (jp + 1) * 256].rearrange("(k p) c -> p k c", p=128))
                pgs = [nxt("m"), nxt("m")]
                for jj in range(2):
                    pg, Rpg = pgs[jj]
                    for k in range(16):
                        pe.op(lambda e, k=k, jj=jj, pg=pg: e.matmul(pg[:, :n], wg[:, k, jj * 128:(jj + 1) * 128], xnT[:, k, :n],
                                                                  start=(k == 0), stop=(k == 15)),
                              [Rwg, R_xn[k]], [Rpg], inc=(k == 15))
                pus = [nxt("m"), nxt("m")]
                for jj in range(2):
                    pu, Rpu = pus[jj]
                    for k in range(16):
                        pe.op(lambda e, k=k, jj=jj, pu=pu: e.matmul(pu[:, :n], wu[:, k, jj * 128:(jj + 1) * 128], xnT[:, k, :n],
                                                                  start=(k == 0), stop=(k == 15)),
                              [Rwu, R_xn[k]], [Rpu], inc=(k == 15))
                for jj in range(2):
                    j = jp * 2 + jj
                    pg, Rpg = pgs[jj]
                    pu, Rpu = pus[jj]
                    sgb, Rsg = (cacc, res("cacc")) if jj == 0 else (rstd, res("rstd"))
                    act.op(lambda e, pg=pg, sgb=sgb: e.activation(sgb[:, :n], pg[:, :n], AF.Silu), [Rpg], [Rsg])
                    dve.op(lambda e, j=j, pu=pu, sgb=sgb: e.tensor_tensor(actT[:, j, :n], pu[:, :n], sgb[:, :n], ALU.mult),
                           [Rpu, Rsg], [R_act[j]])
                if ns:
                    for jj in range(2):
                        j = jp * 2 + jj
                        q16 = j % 16
                        if q16 == 0:
                            (pgS, RpgS), (puS, RpuS) = nxt("a"), nxt("a")
                            sstate["g"] = (pgS, RpgS, puS, RpuS)
                        pgS, RpgS, puS, RpuS = sstate["g"]
                        for k in range(16):
                            pe.op(lambda e, k=k, jj=jj, q16=q16, pgS=pgS: e.matmul(pgS[:, q16 * ns:(q16 + 1) * ns], wg[:, k, jj * 128:(jj + 1) * 128],
                                                                                 xnT[:, k, c1:c1 + ns], start=(k == 0), stop=(k == 15)),
                                  [Rwg, R_xn[k]], [RpgS], inc=(k == 15))
                        for k in range(16):
                            pe.op(lambda e, k=k, jj=jj, q16=q16, puS=puS: e.matmul(puS[:, q16 * ns:(q16 + 1) * ns], wu[:, k, jj * 128:(jj + 1) * 128],
                                                                                 xnT[:, k, c1:c1 + ns], start=(k == 0), stop=(k == 15)),
                                  [Rwu, R_xn[k]], [RpuS], inc=(k == 15))
                        if q16 == 15 or j == NF - 1:
                            cnt = q16 + 1
                            j0 = j - q16
                            act.op(lambda e, pgS=pgS, cnt=cnt: e.activation(vnf[:, :cnt * ns], pgS[:, :cnt * ns], AF.Silu), [RpgS], [res("lnout")])
                            dve.op(lambda e, puS=puS, cnt=cnt, j0=j0: e.tensor_tensor(
                                actT[:, j0:j0 + cnt, c1:c1 + ns], puS[:, :cnt * ns].rearrange("p (a b) -> p a b", a=cnt),
                                vnf[:, :cnt * ns].rearrange("p (a b) -> p a b", a=cnt), ALU.mult),
                                [RpuS, res("lnout")], [R_act[jx] for jx in range(j0, j0 + cnt)])
            fsegs = [(0, 16), (16, 32), (32, 44)]
            if ns:
                pQd = [nxt("a"), nxt("a")]
            for dp in range(8):
                psd = [nxt("m"), nxt("m")]
                for si, (f0, f1) in enumerate(fsegs):
                    nf = f1 - f0
                    src = w_dn[l, f0 * 128:f1 * 128, dp * 256:(dp + 1) * 256].rearrange("(j p) c -> p j c", p=128)
                    wv, Rw = load_w([128, nf, 256], src)
                    for dd in range(2):
                        ps, Rps = psd[dd]
                        for jj in range(nf):
                            j = f0 + jj
                            pe.op(lambda e, jj=jj, j=j, dd=dd, ps=ps: e.matmul(ps[:, :n], wv[:, jj, dd * 128:(dd + 1) * 128],
                                                                             actT[:, j, :n], start=(j == 0), stop=(j == NF - 1)),
                                  [Rw, R_act[j]], [Rps], inc=(jj == nf - 1))
                    if ns:
                        for dd in range(2):
                            pq_, Rpq_ = pQd[dd]
                            for jj in range(nf):
                                j = f0 + jj
                                pe.op(lambda e, jj=jj, j=j, dd=dd, pq_=pq_, dp=dp: e.matmul(
                                    pq_[:, dp * ns:(dp + 1) * ns], wv[:, jj, dd * 128:(dd + 1) * 128], actT[:, j, c1:c1 + ns],
                                    start=(j == 0), stop=(j == NF - 1)),
                                    [Rw, R_act[j]], [Rpq_], inc=(jj == nf - 1))
                for dd in range(2):
                    d = dp * 2 + dd
                    ps, Rps = psd[dd]
                    dve.op(lambda e, d=d, ps=ps: e.tensor_tensor(hT[:, d, :n], hT[:, d, :n], ps[:, :n], ALU.add), [Rps], [R_h[d]])
            if ns:
                for dd in range(2):
                    pq_, Rpq_ = pQd[dd]
                    for dp in range(8):
                        d = dp * 2 + dd
                        dve.op(lambda e, d=d, dp=dp, pq_=pq_: e.tensor_tensor(hT[:, d, c1:c1 + ns], hT[:, d, c1:c1 + ns],
                                                                            pq_[:, dp * ns:(dp + 1) * ns], ALU.add), [Rpq_], [R_h[d]])

        sstate = {}

        def layernorm_heads(src_ps, npart, Rsrc, dst, nheads=4, width=128):
            for h in range(nheads):
                dve.op(lambda e, h=h: e.bn_stats(stt[:npart, h, :], src_ps[:npart, h * width:(h + 1) * width]), [Rsrc], [res("stt")])
            for h in range(nheads):
                dve.op(lambda e, h=h: e.bn_aggr(mv[:npart, h, :], stt[:npart, h, :]), [res("stt")], [res("mv")])
            act.op(lambda e: e.activation(sd[:npart, :nheads], mv[:npart, :nheads, 1], AF.Sqrt, bias=epsc[:npart, :], scale=1.0),
                   [res("mv"), res("consts3")], [res("sd")])
            dve.op(lambda e: e.reciprocal(sd[:npart, :nheads], sd[:npart, :nheads]), [], [res("sd")])
            dve.op(lambda e: e.scalar_tensor_tensor(nb[:npart, :nheads], mv[:npart, :nheads, 0], -1.0, sd[:npart, :nheads],
                                                    ALU.mult, ALU.mult), [res("mv"), res("sd")], [res("nb")])
            for h in range(nheads):
                dve.op(lambda e, h=h: e.tensor_scalar(dst[:npart, h * width:(h + 1) * width], src_ps[:npart, h * width:(h + 1) * width],
                                                      sd[:npart, h:h + 1], nb[:npart, h:h + 1], ALU.mult, ALU.add),
                       [Rsrc, res("sd"), res("nb")], [res("lnout")])

        def rotary(dst, src_ps, npart, cosv, sinv, Rsrc, Rdst, Rtab):
            s4 = src_ps[:npart, :].rearrange("p (h t e) -> p h t e", h=4, t=2)
            d4 = dst.rearrange("p (h t e) -> p h t e", h=4, t=2)
            cb = cosv.unsqueeze(1).to_broadcast([npart, 4, 64])
            sbb = sinv.unsqueeze(1).to_broadcast([npart, 4, 64])
            Rr = res("rt")
            dve.op(lambda e: e.tensor_tensor(rt[0][:npart], s4[:, :, 0, :], cb, ALU.mult), [Rsrc, Rtab], [Rr])
            dve.op(lambda e: e.tensor_tensor(rt[1][:npart], s4[:, :, 1, :], sbb, ALU.mult), [Rsrc], [Rr])
            dve.op(lambda e: e.tensor_tensor(d4[:, :, 0, :], rt[0][:npart], rt[1][:npart], ALU.subtract), [Rr], [Rdst])
            dve.op(lambda e: e.tensor_tensor(rt[0][:npart], s4[:, :, 1, :], cb, ALU.mult), [Rsrc], [Rr])
            dve.op(lambda e: e.tensor_tensor(rt[1][:npart], s4[:, :, 0, :], sbb, ALU.mult), [Rsrc], [Rr])
            dve.op(lambda e: e.tensor_tensor(d4[:, :, 1, :], rt[0][:npart], rt[1][:npart], ALU.add), [Rr], [Rdst])

        def prompt_group(g, ws):
            n = G
            c1 = G
            dma_batch(sp, ds_cs, [(cos_g[:], cos_d[:, g * 4:(g + 1) * 4, :], [], [res("cs")]),
                                  (sin_g[:], sin_d[:, g * 4:(g + 1) * 4, :], [], [res("cs")])])
            load_x(n, [(xp[g * G + c * 128:g * G + (c + 1) * 128, :], 128, c * 128) for c in range(4)])
            if ws:
                load_x(NS, [(xs, NS, c1)])
            for l in range(NL):
                load_ltab(l)
                rmsnorm(n, 2 * l)
                if ws:
                    rmsnorm(NS, 2 * l, c0=c1)
                    mlimit[0] = 3
                RS, RSb = res(f"S{l}"), res(f"Sbf{l}")
                Rca, Rcz = res(f"ca{l}"), res(f"cz{l}")

                def fm_block(cb, consume):
                    if ws:
                        pS, RpS = pm[3], R_pm[3]
                    for half in range(2):
                        src = w_in[l, :, cb * GW + half * 256: cb * GW + (half + 1) * 256].rearrange("(k p) c -> p k c", p=128)
                        wv, Rw = load_w([128, 16, 256], src)
                        for ee in range(2):
                            ci = half * 2 + ee
                            ps, Rps = nxt("m")
                            for k in range(16):
                                pe.op(lambda e, k=k, ee=ee: e.matmul(ps[:, :n], wv[:, k, ee * 128:(ee + 1) * 128], xnT[:, k, :n],
                                                                     start=(k == 0), stop=(k == 15)),
                                      [Rw, R_xn[k]], [Rps], inc=(k == 15))
                            consume(ci, ps, Rps)
                            if ws:
                                for k in range(16):
                                    pe.op(lambda e, k=k, ee=ee, ci=ci: e.matmul(pS[:, ci * NS:(ci + 1) * NS], wv[:, k, ee * 128:(ee + 1) * 128],
                                                                              xnT[:, k, c1:c1 + NS], start=(k == 0), stop=(k == 15)),
                                          [Rw, R_xn[k]], [RpS], inc=(k == 15))
                            after_group()
                    if ws:
                        act.op(lambda e: e.copy(pFM[:, cb * 4:(cb + 1) * 4, :], pS[:, 0:4 * NS].rearrange("p (a b) -> p a b", a=4)),
                               [RpS], [res(f"pfm{cb}")])

                def tm_block(cb, consume):
                    wvs = []
                    for half in range(2):
                        src = w_in[l, half * 1024:(half + 1) * 1024, cb * GW:(cb + 1) * GW].rearrange("(k p) c -> p k c", p=128)
                        wvs.append(load_w([128, 8, GW], src))
                    if not ws:
                        for half in range(2):
                            wv, Rw = wvs[half]
                            for c in range(4):
                                ps, Rps = pm[c], R_pm[c]
                                for kk in range(8):
                                    k = half * 8 + kk
                                    pe.op(lambda e, k=k, kk=kk, c=c, wv=wv, ps=ps: e.matmul(ps[:, :], xnT[:, k, c * 128:(c + 1) * 128], wv[:, kk, :],
                                                                                          start=(k == 0), stop=(k == 15)),
                                          [Rw, R_xn[k]], [Rps], inc=(kk == 7))
                                if half == 1:
                                    consume(c, ps, Rps)
                    else:
                        for c in range(4):
                            ps, Rps = nxt("m")
                            for k in range(16):
                                wv, Rw = wvs[k // 8]
                                pe.op(lambda e, k=k, c=c, wv=wv: e.matmul(ps[:, :], xnT[:, k, c * 128:(c + 1) * 128], wv[:, k % 8, :],
                                                                          start=(k == 0), stop=(k == 15)),
                                      [Rw, R_xn[k]], [Rps], inc=(k == 15))
                            consume(c, ps, Rps)
                    if ws:
                        pS, RpS = pm[3], R_pm[3]
                        for eb in range(4):
                            for k in range(16):
                                wv, Rw = wvs[k // 8]
                                pe.op(lambda e, k=k, eb=eb, wv=wv: e.matmul(pS[:, eb * NS:(eb + 1) * NS], wv[:, k % 8, eb * 128:(eb + 1) * 128],
                                                                          xnT[:, k, c1:c1 + NS], start=(k == 0), stop=(k == 15)),
                                      [Rw, R_xn[k]], [RpS], inc=(k == 15))
                        act.op(lambda e: e.copy(pFM[:, cb * 4:(cb + 1) * 4, :], pS[:, 0:4 * NS].rearrange("p (a b) -> p a b", a=4)),
                               [RpS], [res(f"pfm{cb}")])

                hooks = {"gen": None, "deferred": []}

                def after_group(flush=False):
                    keep = []
                    for (age, f) in hooks["deferred"]:
                        if age >= 2 or flush:
                            f()
                        else:
                            keep.append((age + 1, f))
                    hooks["deferred"] = keep
                    if hooks["gen"] is not None:
                        try:
                            next(hooks["gen"])
                        except StopIteration:
                            hooks["gen"] = None

                def c_q(c, ps, Rps):
                    rotary(qrot[:, c, :], ps, 128, cos_g[:, c, :], sin_g[:, c, :], Rps, res(f"qrot{c}"), res("cs"))

                def c_k(c, ps, Rps):
                    rotary(krot[:, c, :], ps, 128, cos_g[:, c, :], sin_g[:, c, :], Rps, res(f"krot{c}"), res("cs"))
                    dve.op(lambda e: e.tensor_tensor(kd[:, c, :].rearrange("p (h e) -> p h e", h=4),
                                                     krot[:, c, :].rearrange("p (h e) -> p h e", h=4),
                                                     kdec[:].unsqueeze(2).to_broadcast([128, 4, 128]), ALU.mult),
                           [res(f"krot{c}"), res("consts")], [res(f"kd{c}")])

                def c_v(c, ps, Rps):
                    act.op(lambda e: e.copy(vbf[:, c, :], ps[:, :]), [Rps], [res(f"vbf{c}")])

                def c_g(c, ps, Rps):
                    act.op(lambda e: e.activation(gs[:, c, :], ps[:, :], AF.Silu), [Rps], [res(f"gs{c}")])

                def c_vv(c, ps, Rps):
                    layernorm_heads(ps, 128, Rps, vnf, nheads=1, width=GW)
                    dve.op(lambda e: e.tensor_tensor(vnf[:, :], vnf[:, :], sgug[:, :], ALU.mult), [res("ltab")], [res("lnout")])
                    act.op(lambda e: e.copy(vnb[:, c, :], vnf[:, :]), [res("lnout")], [res(f"vnb{c}")])
                    if g == NG - 1 and c == 3:
                        sp.dma(ds_sg, sgu_p[l], vnf[:, :], [res("lnout")], [])

                tm_block(4, c_q)
                tm_block(5, c_k)
                tm_block(6, c_v)
                tm_block(7, c_g)
                tm_block(9, c_vv)

                def mixer_gen():
                    for c in range(4):
                        tsl = slice(c * 128, (c + 1) * 128)
                        pq, Rpq = nxt("b")
                        for h in range(4):
                            pe.op(lambda e, h=h: e.transpose(pq[:, h * 128:(h + 1) * 128], qrot[:, c, h * 128:(h + 1) * 128], identb[:]),
                                  [res(f"qrot{c}"), res("consts2")], [Rpq], inc=(h == 3))
                        dve.op(lambda e: e.tensor_copy(qT[:], pq[:, 0:512].rearrange("p (h t) -> p h t", h=4)), [Rpq], [res("qT")])
                        dve.op(lambda e: e.tensor_tensor(qdT[:], pq[:, 0:512].rearrange("p (h t) -> p h t", h=4), qdec[:], ALU.mult),
                               [Rpq, res("consts")], [res("qdT")])
                        pk, Rpk = nxt("b")
                        for h in range(4):
                            pe.op(lambda e, h=h: e.transpose(pk[:, h * 128:(h + 1) * 128], krot[:, c, h * 128:(h + 1) * 128], identb[:]),
                                  [res(f"krot{c}"), res("consts2")], [Rpk], inc=(h == 3))
                        act.op(lambda e: e.copy(kT[:], pk[:, 0:512].rearrange("p (h t) -> p h t", h=4)), [Rpk], [res("kT")])
                        yield
                        psc, Rpsc = nxt("a")
                        for h in range(4):
                            pe.op(lambda e, h=h: e.matmul(psc[:, h * 128:(h + 1) * 128], kT[:, h, :], qT[:, h, :], start=True, stop=True),
                                  [res("kT"), res("qT")], [Rpsc], inc=(h == 3))
                        dve.op(lambda e: e.tensor_tensor(scT[:], psc[:, :].rearrange("p (h t) -> p h t", h=4), maskT[:], ALU.mult),
                               [Rpsc, res("consts")], [res("scT")])
                        pkv, Rpkv = nxt("m")
                        for h in range(4):
                            hs = slice(h * 128, (h + 1) * 128)
                            pe.op(lambda e, h=h, hs=hs: e.matmul(pkv[:, hs], kd[:, c, hs], vbf[:, c, hs], start=True, stop=True),
                                  [res(f"kd{c}"), res(f"vbf{c}")], [Rpkv], inc=(h == 3))
                        yield
                        po, Rpo = nxt("a")
                        for h in range(4):
                            hs = slice(h * 128, (h + 1) * 128)
                            pe.op(lambda e, h=h, hs=hs: e.matmul(po[:, hs], scT[:, h, :], vbf[:, c, hs], start=True, stop=False),
                                  [res("scT"), res(f"vbf{c}")], [Rpo], inc=False)
                            pe.op(lambda e, h=h, hs=hs: e.matmul(po[:, hs], qdT[:, h, :], Sbf[:, l, h, :], start=False, stop=True),
                                  [res("qdT"), RSb], [Rpo], inc=(h == 3))
                        for h in range(4):
                            dve.op(lambda e, h=h: e.scalar_tensor_tensor(Sst[:, l, h, :], Sst[:, l, h, :], GAM[h] ** 128,
                                                                         pkv[:, h * 128:(h + 1) * 128], ALU.mult, ALU.add), [Rpkv], [RS])
                        act.op(lambda e: e.copy(Sbf[:, l, :, :], Sst[:, l, :, :]), [RS], [RSb])
                        layernorm_heads(po, 128, Rpo, onb)
                        dve.op(lambda e: e.tensor_tensor(onb[:, :], onb[:, :], retg[:, :], ALU.mult), [res("ltab")], [res("lnout")])
                        dve.op(lambda e: e.tensor_tensor(ycb[:, :], onb[:, :], gs[:, c, :], ALU.mult), [res("lnout"), res(f"gs{c}")], [res("ycb")])
                        yield
                        yield
                        pyc, Rpyc = nxt("b")
                        for h in range(4):
                            pe.op(lambda e, h=h: e.transpose(pyc[:, h * 128:(h + 1) * 128], ycb[:, h * 128:(h + 1) * 128], identb[:]),
                                  [res("ycb"), res("consts2")], [Rpyc], inc=(h == 3))
                        act.op(lambda e: e.copy(mixT[:, 8:12, tsl], pyc[:, 0:512].rearrange("p (h t) -> p h t", h=4)),
                               [Rpyc], [R_mix[8 + h] for h in range(4)])
                        pmx, Rpmx = nxt("a")
                        for h in range(4):
                            pe.op(lambda e, h=h: e.matmul(pmx[:, h * 128:(h + 1) * 128], vnb[:, c, h * 128:(h + 1) * 128], WsT[:, l, h, :],
                                                          start=True, stop=True), [res(f"vnb{c}"), res("consts3")], [Rpmx], inc=(h == 3))
                        dve.op(lambda e: e.tensor_tensor(tmp4[:], pmx[:, :].rearrange("p (h t) -> p h t", h=4), sgub[:, :, :], ALU.add),
                               [Rpmx, res("ltab")], [res("tmp4")])
                        dve.op(lambda e: e.tensor_tensor(mixT[:, 12:16, tsl], tmp4[:], u_sb[:, :, tsl], ALU.mult),
                               [res("tmp4"), res("u_sb")], [R_mix[12 + h] for h in range(4)])
                        yield

                hooks["gen"] = mixer_gen()

                def c_u(ci, ps, Rps):
                    act.op(lambda e: e.copy(u_sb[:, ci, :n], ps[:, :n]), [Rps], [res("u_sb")])

                fm_block(8, c_u)

                dve.op(lambda e: e.tensor_copy(aext[:, :, 0:15], carry_a[:, l, :, :]), [Rca], [res("aext")])

                def c_a(gi, ps, Rps):
                    act.op(lambda e: e.copy(aext[:, gi, 15:15 + n], ps[:, :n]), [Rps], [res("aext")])
                    cur = aext[:, gi, :]
                    L = 15 + n
                    sh = 1
                    for step in range(gi + 1):
                        o = ptmp[step % 2]
                        dve.op(lambda e, cur=cur, o=o, sh=sh: e.tensor_tensor(o[:, sh:L], cur[:, sh:L], cur[:, 0:L - sh], ALU.add),
                               [res("aext")] if step == 0 else [res(f"ptmp{(step - 1) % 2}")], [res(f"ptmp{step % 2}")])
                        cur = o
                        sh *= 2
                    Rcur = res(f"ptmp{gi % 2}")
                    dTg, RdT = dTs[gi % 3], res(f"dT{gi % 3}")
                    dve.op(lambda e, cur=cur: e.scalar_tensor_tensor(dTg[:, :n], cur[:, 15:15 + n], 1.0 / WINS[gi], aext[:, gi, 15:15 + n],
                                                                     ALU.mult, ALU.subtract), [Rcur, res("aext")], [RdT])
                    if g in (0, 2):
                        dve.op(lambda e, cur=cur: e.tensor_tensor(ptmp[(gi + 1) % 2][:, 0:16], cur[:, 15:31], invc0[:, (g // 2) * 4 + gi, :], ALU.mult),
                               [Rcur, res("consts")], [res(f"ptmp{(gi + 1) % 2}")])
                        dve.op(lambda e: e.tensor_tensor(dTg[:, 0:16], ptmp[(gi + 1) % 2][:, 0:16], aext[:, gi, 15:31], ALU.subtract),
                               [res(f"ptmp{(gi + 1) % 2}"), res("aext")], [RdT])

                    def pool_mm():
                        py, Rpy = nxt("m")
                        pe.op(lambda e: e.matmul(py[:, :n], poolw[:, l * 4 + gi, :], dTg[:, :n], start=True, stop=True),
                              [RdT, res("consts2")], [Rpy])
                        dve.op(lambda e: e.tensor_scalar(mixT[:, gi, :n], py[:, :n], pscale[:, l, gi:gi + 1], None, ALU.mult),
                               [Rpy, res("consts")], [R_mix[gi]])

                    hooks["deferred"].append((0, pool_mm))

                fm_block(0, c_a)
                dve.op(lambda e: e.tensor_copy(carry_a[:, l, :, :], aext[:, :, n:n + 15]), [res("aext")], [Rca])
                if g == NG - 1:
                    with nc.allow_non_contiguous_dma(reason="small state transpose"):
                        for gi in range(4):
                            sp.dma(ds_st, pool_p[l, :, gi * 128:(gi + 1) * 128].rearrange("t c -> c t"), carry_a[:, l, gi, :], [Rca], [],
                                   allow_slow_non_contiguous=True)

                dve.op(lambda e: e.tensor_copy(zext[:, :, 0:2], carry_z[:, l, :, :]), [Rcz], [res("zext")])

                def c_hh(ci, ps, Rps):
                    act.op(lambda e: e.copy(hh_sb[:, ci, :n], ps[:, :n]), [Rps], [res(f"hh{ci}")])

                fm_block(3, c_hh)

                def c_cg(ci, ps, Rps):
                    dve.op(lambda e: e.tensor_tensor(zext[:, ci, 2:2 + n], ps[:, :n], hh_sb[:, ci, :n], ALU.mult),
                           [Rps, res(f"hh{ci}")], [res("zext")])

                fm_block(2, c_cg)
                dve.op(lambda e: e.tensor_copy(carry_z[:, l, :, :], zext[:, :, n:n + 2]), [res("zext")], [Rcz])
                if g == NG - 1:
                    with nc.allow_non_contiguous_dma(reason="small state transpose"):
                        for gi in range(4):
                            sp.dma(ds_st, conv_p[l, :, gi * 128:(gi + 1) * 128].rearrange("t c -> c t"), carry_z[:, l, gi, :], [Rcz], [],
                                   allow_slow_non_contiguous=True)

                def c_bg(ci, ps, Rps):
                    dve.op(lambda e: e.tensor_scalar(cacc[:, :n], zext[:, ci, 0:n], convw[:, l, 0, ci:ci + 1], None, ALU.mult),
                           [res("zext"), res("consts")], [res("cacc")])
                    dve.op(lambda e: e.scalar_tensor_tensor(cacc[:, :n], zext[:, ci, 1:1 + n], convw[:, l, 1, ci:ci + 1], cacc[:, :n],
                                                            ALU.mult, ALU.add), [res("zext")], [res("cacc")])
                    dve.op(lambda e: e.scalar_tensor_tensor(cacc[:, :n], zext[:, ci, 2:2 + n], convw[:, l, 2, ci:ci + 1], cacc[:, :n],
                                                            ALU.mult, ALU.add), [res("zext")], [res("cacc")])
                    dve.op(lambda e: e.tensor_tensor(mixT[:, 4 + ci, :n], ps[:, :n], cacc[:, :n], ALU.mult),
                           [Rps, res("cacc")], [R_mix[4 + ci]])

                fm_block(1, c_bg)
                while hooks["gen"] is not None or hooks["deferred"]:
                    after_group(flush=True)
                if g == NG - 1:
                    sp.dma(ds_rp, ret_p[l].rearrange("h d e -> d h e"), Sst[:, l, :, :], [RS], [])
                mlimit[0] = 4
                if ws:
                    sample_mixers(l)
                if g < 2 and l == NL - 1:
                    continue
                wout_and_ffn(l, n, NS if ws else 0)
            if g < 2:
                return
            rmsnorm(n, 4, out_xn=False)
            if ws:
                rmsnorm(NS, 4, out_xn=False, c0=c1)
            store_y(n, [(y_p[(g - 2) * G + c * 128:(g - 2) * G + (c + 1) * 128, :], 128, c * 128) for c in range(4)])
            if ws:
                store_y(NS, [(y_s, NS, c1)])

        def sample_mixers(l):
            n = NS
            roff = (0, 1, 4, 11)
            R_bigp = ([res("aext"), res("zext"), res("u_sb")] + [res(f"hh{i}") for i in range(4)]
                      + [res(f"vnb{i}") for i in range(4)] + [res(f"gs{i}") for i in range(4)])
            dma_batch(sp, ds_sst, [
                (s_cprev, st_conv[l], [], R_act + R_bigp),
                (s_convw, convwb_d[:, l], [], R_act),
            ] + [(s_prev4[:, roff[gi]:roff[gi] + WINS[gi] - 1, :], st_pool[l, :, 16 - WINS[gi]:15, gi * 128:(gi + 1) * 128], [], R_act)
                 for gi in range(4)])
            dma_batch(sp, ds_cp, [
                (pool_s[l, :, 0:14, :], st_pool[l, :, 1:15, :], [], []),
                (conv_s[l, :, 0:1, :], st_conv[l, :, 1:2, :], [], []),
            ])

            def tm_block(cb, consume):
                ps, Rps = nxt("m")
                for eb in range(4):
                    pe.op(lambda e, eb=eb: e.transpose(ps[:NS, eb * 128:(eb + 1) * 128], pFM[:, cb * 4 + eb, :], ident[:]),
                          [res(f"pfm{cb}"), res("consts")], [Rps], inc=(eb == 3))
                consume(ps, Rps)

            def to_mix(src, Rsrc, col0):
                dve.op(lambda e: e.tensor_copy(s_mix[:, col0:col0 + GW], src), [Rsrc], [res("s_mix")])

            def c_a(ps, Rps):
                act.op(lambda e: e.copy(s_a[:, :], ps[:n, :]), [Rps], [res("s_a")])
                sp.dma(ds_sa, pool_s[l, :, 14, :], s_a[:, :], [res("s_a")], [])
                for gi, w in enumerate(WINS):
                    cs = slice(gi * 128, (gi + 1) * 128)
                    if w > 2:
                        dve.op(lambda e, cs=cs, w=w, gi=gi: e.tensor_reduce(
                            s_d[:, cs], s_prev4[:, roff[gi]:roff[gi] + w - 1, :].rearrange("p t c -> p c t"), AX.X, ALU.add),
                            [R_act[0]], [res("s_d")])
                    else:
                        dve.op(lambda e, cs=cs: e.tensor_copy(s_d[:, cs], s_prev4[:, 0, :]), [R_act[0]], [res("s_d")])
                    dve.op(lambda e, cs=cs: e.tensor_tensor(s_d[:, cs], s_d[:, cs], s_a[:, cs], ALU.add), [res("s_a")], [res("s_d")])
                    dve.op(lambda e, cs=cs, w=w: e.scalar_tensor_tensor(s_d[:, cs], s_d[:, cs], 1.0 / w, s_a[:, cs], ALU.mult, ALU.subtract),
                           [res("s_a")], [res("s_d")])
                pt, Rpt = nxt("a")
                for gi in range(4):
                    pe.op(lambda e, gi=gi: e.transpose(pt[:, gi * NS:(gi + 1) * NS], s_d[:, gi * 128:(gi + 1) * 128], ident[:NS, :NS]),
                          [res("s_d"), res("consts")], [Rpt], inc=(gi == 3))
                act.op(lambda e: e.copy(dT[:, 0:4 * NS], pt[:, 0:4 * NS]), [Rpt], [res("dT0")])
                py, Rpy = nxt("a")
                for gi in range(4):
                    pe.op(lambda e, gi=gi: e.matmul(py[:, gi * NS:(gi + 1) * NS], poolw[:, l * 4 + gi, :], dT[:, gi * NS:(gi + 1) * NS],
                                                    start=True, stop=True), [res("dT0"), res("consts2")], [Rpy], inc=(gi == 3))
                for gi in range(4):
                    dve.op(lambda e, gi=gi: e.tensor_scalar(mixT[:, gi, G:G + NS], py[:, gi * NS:(gi + 1) * NS], pscale[:, l, gi:gi + 1], None, ALU.mult),
                           [Rpy, res("consts")], [R_mix[gi]])

            tm_block(0, c_a)

            def c_hh(ps, Rps):
                act.op(lambda e: e.copy(s_hh[:, :], ps[:n, :]), [Rps], [res("s_hh")])

            tm_block(3, c_hh)

            def c_cg(ps, Rps):
                dve.op(lambda e: e.tensor_tensor(s_hh[:, :], ps[:n, :], s_hh[:, :], ALU.mult), [Rps], [res("s_hh")])
                sp.dma(ds_sz, conv_s[l, :, 1, :], s_hh[:, :], [res("s_hh")], [])

            tm_block(2, c_cg)

            def c_bg(ps, Rps):
                dve.op(lambda e: e.tensor_tensor(s_t[:, :], s_cprev[:, 0, :], s_convw[:, 0, :], ALU.mult),
                       [R_act[0]], [res("s_t")])
                dve.op(lambda e: e.tensor_tensor(s_d[:, :], s_cprev[:, 1, :], s_convw[:, 1, :], ALU.mult),
                       [R_act[0]], [res("s_d")])
                dve.op(lambda e: e.tensor_tensor(s_t[:, :], s_t[:, :], s_d[:, :], ALU.add), [res("s_d")], [res("s_t")])
                dve.op(lambda e: e.tensor_tensor(s_d[:, :], s_hh[:, :], s_convw[:, 2, :], ALU.mult), [res("s_hh"), R_act[0]], [res("s_d")])
                dve.op(lambda e: e.tensor_tensor(s_t[:, :], s_t[:, :], s_d[:, :], ALU.add), [res("s_d")], [res("s_t")])
                dve.op(lambda e: e.tensor_tensor(s_mix[:, 0:GW], ps[:n, :], s_t[:, :], ALU.mult), [Rps, res("s_t")], [res("s_mix")])

            tm_block(1, c_bg)

            def c_u(ps, Rps):
                act.op(lambda e: e.copy(s_u[:, :], ps[:n, :]), [Rps], [res("s_u")])

            tm_block(8, c_u)

            def c_q(ps, Rps):
                rotary(s_q[:, :], ps, NS, cos_s[:, :], sin_s[:, :], Rps, res("s_q"), res("consts"))

            def c_k(ps, Rps):
                rotary(s_k[:, :], ps, NS, cos_s[:, :], sin_s[:, :], Rps, res("s_k"), res("consts"))
                dve.op(lambda e: e.tensor_scalar(s_k[:, :], s_k[:, :], 128.0 ** -0.5, None, ALU.mult), [], [res("s_k")])

            def c_v(ps, Rps):
                act.op(lambda e: e.copy(s_v[:, :], ps[:n, :]), [Rps], [res("s_v")])

            def c_g(ps, Rps):
                act.op(lambda e: e.activation(s_g[:, :], ps[:n, :], AF.Silu), [Rps], [res("s_g")])

            def c_vv(ps, Rps):
                layernorm_heads(ps, NS, Rps, vnf, nheads=1, width=GW)
                dve.op(lambda e: e.tensor_tensor(vnf[:n, :], vnf[:n, :], sgug[:NS, :], ALU.mult), [res("ltab")], [res("lnout")])
                sp.dma(ds_sv, sgu_s[l], vnf[:n, :], [res("lnout")], [])
                for h in range(4):
                    hs = slice(h * 128, (h + 1) * 128)
                    dve.op(lambda e, h=h, hs=hs: e.tensor_scalar(s_t[:, hs], vnf[:n, hs], sgu00[:, l, 0, h:h + 1], sgu00[:, l, 1, h:h + 1],
                                                                 ALU.mult, ALU.add), [res("lnout"), res("consts")], [res("s_t")])
                dve.op(lambda e: e.tensor_tensor(s_mix[:, 2 * GW:3 * GW], s_t[:, :], s_u[:, :], ALU.mult),
                       [res("s_t"), res("s_u")], [res("s_mix")])

            tm_block(4, c_q)
            tm_block(5, c_k)
            tm_block(6, c_v)
            tm_block(7, c_g)
            tm_block(9, c_vv)

            dve.op(lambda e: e.tensor_tensor(s_t[:, :], s_q[:, :], s_k[:, :], ALU.mult), [res("s_q"), res("s_k")], [res("s_t")])
            dve.op(lambda e: e.tensor_reduce(s_sc[:, :], s_t[:, :].rearrange("p (h e) -> p h e", h=4), AX.X, ALU.add),
                   [res("s_t")], [res("s_sc")])
            pt, Rpt = nxt("a")
            for h in range(4):
                pe.op(lambda e, h=h: e.transpose(pt[:, h * NS:(h + 1) * NS], s_q[:, h * 128:(h + 1) * 128], ident[:NS, :NS]),
                      [res("s_q"), res("consts")], [Rpt], inc=(h == 3))
            act.op(lambda e: e.copy(s_qT[:], pt[:, 0:4 * NS].rearrange("p (h b) -> p h b", h=4)), [Rpt], [res("s_qT")])
            poS, RpoS = nxt("a")
            NQ = NS // 4
            RSq = [res("S_q0"), res("S_q1")]

            def s_load(qb):
                bi = qb % 2
                extra = (R_act + R_bigp) if qb < 2 else []
                sp.dma(ds_sSq[bi], S_qs[bi], st_ret[l, qb * 4:qb * 4 + 4].rearrange("b h d e -> d b h e"), [], [RSq[bi]] + extra)

            s_load(0)
            for qb in range(NQ):
                b0 = qb * 4
                bi = qb % 2
                S_half, RSh = S_qs[bi], RSq[bi]
                if qb + 1 < NQ:
                    s_load(qb + 1)
                for bb in range(4):
                    b = b0 + bb
                    for h in range(4):
                        pe.op(lambda e, b=b, bb=bb, h=h, S_half=S_half: e.matmul(poS[:, h * NS + b:h * NS + b + 1], S_half[:, bb, h, :],
                                                                                s_qT[:, h, b:b + 1], start=True, stop=True),
                              [res("s_qT"), RSh], [RpoS], inc=(bb == 3 and h == 3))
                for bb in range(4):
                    b = b0 + bb
                    kdg = s_kdiag[0]
                    Rk = res("s_kdiag0")
                    dve.op(lambda e, b=b, kdg=kdg: e.tensor_scalar(kdg[:, :], s_k[:, :], eye16[:, b:b + 1], None, ALU.mult),
                           [res("s_k"), res("consts")], [Rk])
                    pkv, Rpkv = nxt("m")
                    for h in range(4):
                        hs = slice(h * 128, (h + 1) * 128)
                        pe.op(lambda e, hs=hs, kdg=kdg: e.matmul(pkv[:, hs], kdg[:, hs], s_v[:, hs], start=True, stop=True),
                              [Rk, res("s_v")], [Rpkv], inc=(h == 3))
                    for h in range(4):
                        dve.op(lambda e, bb=bb, h=h, S_half=S_half: e.scalar_tensor_tensor(S_half[:, bb, h, :], S_half[:, bb, h, :], GAM[h],
                                                                                          pkv[:, h * 128:(h + 1) * 128], ALU.mult, ALU.add),
                               [Rpkv], [RSh])
                sp.dma(ds_sSq[bi], ret_s[l, b0:b0 + 4].rearrange("b h d e -> d b h e"), S_half,
                       [RSh] + (R_act if qb >= NQ - 2 else []), [])
            act.op(lambda e: e.copy(s_oT[:], poS[:, 0:4 * NS].rearrange("p (h b) -> p h b", h=4)), [RpoS], [res("s_oT")])
            po, Rpo = nxt("a")
            for h in range(4):
                pe.op(lambda e, h=h: e.transpose(po[:NS, h * 128:(h + 1) * 128], s_oT[:, h, :], ident[:]),
                      [res("s_oT"), res("consts")], [Rpo], inc=(h == 3))
            for h in range(4):
                hs = slice(h * 128, (h + 1) * 128)
                dve.op(lambda e, h=h, hs=hs: e.tensor_scalar(s_t[:, hs], s_v[:, hs], s_sc[:, h:h + 1], None, ALU.mult),
                       [res("s_v"), res("s_sc")], [res("s_t")])
                dve.op(lambda e, h=h, hs=hs: e.scalar_tensor_tensor(onb[:NS, hs], po[:NS, hs], GAM[h], s_t[:, hs], ALU.mult, ALU.add),
                       [Rpo, res("s_t")], [res("lnout")])
            layernorm_heads(onb, NS, res("lnout"), onb)
            dve.op(lambda e: e.tensor_tensor(onb[:NS, :], onb[:NS, :], retg[:NS, :], ALU.mult), [res("ltab")], [res("lnout")])
            dve.op(lambda e: e.tensor_tensor(s_mix[:, GW:2 * GW], onb[:NS, :], s_g[:, :], ALU.mult),
                   [res("lnout"), res("s_g")], [res("s_mix")])
            for m4 in range(1, 4):
                pt, Rpt = nxt("a")
                for kk in range(4):
                    m = m4 * 4 + kk
                    pe.op(lambda e, m=m, kk=kk: e.transpose(pt[:, kk * NS:(kk + 1) * NS], s_mix[:, (m - 4) * 128:(m - 3) * 128], ident[:NS, :NS]),
                          [res("s_mix"), res("consts")], [Rpt], inc=(kk == 3))
                act.op(lambda e, m4=m4: e.copy(mixT[:, m4 * 4:m4 * 4 + 4, G:G + NS], pt[:, 0:4 * NS].rearrange("p (a b) -> p a b", a=4)),
                       [Rpt], [R_mix[m4 * 4 + kk] for kk in range(4)])

        ds_xb = [mkds(f"ds_x{i}") for i in range(3)]
        ds_lt = mkds("ds_lt")
        ds_cs = mkds("ds_cs")
        ds_st = mkds("ds_st")
        ds_sg = mkds("ds_sg")
        ds_rp = mkds("ds_rp")
        ds_sst = mkds("ds_sst")
        ds_cp = mkds("ds_cp")
        ds_sa = mkds("ds_sa")
        ds_sz = mkds("ds_sz")
        ds_sv = mkds("ds_sv")
        ds_sSq = [mkds("ds_sS0"), mkds("ds_sS1")]

        for g in range(NG):
            prompt_group(g, g == NG - 1)

        for d in dsems:
            if d.cnt:
                nc.sync.wait_ge(d.sem, d.cnt)
    return nc


def _tables():
    f32 = np.float32
    half = 64
    inv = (np.float32(10000.0) ** (-np.arange(half, dtype=f32) / f32(half))).astype(f32)
    pos = np.arange(SEQ, dtype=f32)
    ang = pos[:, None] * inv[None, :]
    cos_t = np.cos(ang).astype(f32).reshape(16, 128, 64).transpose(1, 0, 2)
    sin_t = np.sin(ang).astype(f32).reshape(16, 128, 64).transpose(1, 0, 2)
    angs = (np.full((NS,), PAST, dtype=f32)[:, None] * inv[None, :]).astype(f32)
    cos_s = np.cos(angs).astype(f32)
    sin_s = np.sin(angs).astype(f32)
    lg = np.log1p(-(2.0 ** (-5.0 - np.arange(4, dtype=np.float64))))
    idx = np.arange(128, dtype=np.float64)
    diff = idx[None, :] - idx[:, None]
    s = 128.0 ** -0.5
    maskT = np.where(diff[:, None, :] >= 0, np.exp(np.maximum(diff, 0.0)[:, None, :] * lg[None, :, None]), 0.0) * s
    qdec = np.broadcast_to(np.exp((idx + 1.0)[None, None, :] * lg[None, :, None]), (128, 4, 128))
    kdec = np.exp((127.0 - idx)[:, None] * lg[None, :]) * s
    tri = (idx[:, None] <= idx[None, :]).astype(f32)
    invc0 = np.zeros((128, 2, 4, 16), f32)
    for gi, w in enumerate(WINS):
        invc0[:, 0, gi, :] = 1.0 / np.minimum(np.arange(16) + 1, w)
        invc0[:, 1, gi, :] = 1.0 / w
    return dict(cos_t=np.ascontiguousarray(cos_t), sin_t=np.ascontiguousarray(sin_t), cos_s=cos_s, sin_s=sin_s,
                maskT=maskT.astype(f32), qdec=np.ascontiguousarray(qdec).astype(f32), kdec=kdec.astype(f32), tri=tri,
                ident=np.eye(128, dtype=f32), invc0=invc0, eye16=np.eye(NS, dtype=f32))


_NC = None


def kernel(x_prompt, x_sample, state_pool, state_conv, state_ret, norm1_g, w_in, pool_w, pool_scale, conv_w, ret_norm_g,
           sgu_norm_g, sgu_w, sgu_b, w_out, norm2_g, w_gate_up, w_down, final_norm_g):
    global _NC
    f32 = np.float32
    A = lambda a: np.ascontiguousarray(np.asarray(a, dtype=f32))
    x_prompt, x_sample, state_pool, state_conv, state_ret = map(A, (x_prompt, x_sample, state_pool, state_conv, state_ret))
    w_in, w_out, w_gate_up, w_down, pool_w = map(A, (w_in, w_out, w_gate_up, w_down, pool_w))
    norm1_g, norm2_g, final_norm_g, pool_scale, conv_w = map(A, (norm1_g, norm2_g, final_norm_g, pool_scale, conv_w))
    ret_norm_g, sgu_norm_g, sgu_w, sgu_b = map(A, (ret_norm_g, sgu_norm_g, sgu_w, sgu_b))
    if _NC is None:
        _NC = build_program()
    nc = _NC
    gl = np.stack([norm1_g[0], norm2_g[0], norm1_g[1], norm2_g[1], final_norm_g])
    gcols = A(gl.reshape(5, 16, 128).transpose(2, 0, 1))
    pscale = A(pool_scale.reshape(NL, 4, 128).transpose(2, 0, 1))
    convw = A(conv_w.reshape(NL, 3, 4, 128).transpose(3, 0, 1, 2))
    convwb = A(np.broadcast_to(conv_w[None], (NS, NL, 3, GW)))
    retg = A(np.broadcast_to(ret_norm_g[None], (128, NL, GW)))
    sgug = A(np.broadcast_to(sgu_norm_g[None], (128, NL, GW)))
    sguwT = A(sgu_w.transpose(3, 0, 1, 2))
    sgub = A(np.broadcast_to(sgu_b[None], (128, NL, 4, 128)))
    sgu00 = A(np.broadcast_to(np.stack([sgu_w[:, :, 0, 0], sgu_b[:, :, 0]], axis=1)[None], (NS, NL, 2, 4)))
    tabs = _tables()
    shared = dict(w_in=w_in, w_out=w_out, w_gu=w_gate_up, w_dn=w_down, pool_w=pool_w, gcols=gcols, pscale=pscale, convw=convw,
                  convwb=convwb, retg=retg, sgug=sgug, sguwT=sguwT, sgub=sgub, sgu00=sgu00, **tabs)
    cosA = A(np.concatenate([tabs["cos_t"][:, 0:8], tabs["cos_t"][:, 0:8]], axis=1))
    sinA = A(np.concatenate([tabs["sin_t"][:, 0:8], tabs["sin_t"][:, 0:8]], axis=1))
    invcA = A(tabs["invc0"][:, [0, 0]].reshape(128, 8, 16))
    invcB = A(tabs["invc0"][:, [0, 1]].reshape(128, 8, 16))
    zhalf = np.zeros((SEQ // 2, D), f32)
    in_maps = []
    for c in range(8):
        i, role = c // 2, c % 2
        s0 = c * NS
        m = dict(shared)
        m.update(xs=A(x_sample[s0:s0 + NS, 0, :]), st_pool=A(state_pool[:, s0:s0 + NS]),
                 st_conv=A(state_conv[:, s0:s0 + NS]), st_ret=A(state_ret[:, s0:s0 + NS]))
        if role == 0:
            m.update(xp=A(np.concatenate([zhalf, x_prompt[i, 0:SEQ // 2]], axis=0)), cos_t=cosA, sin_t=sinA, invc0=invcA)
        else:
            m.update(xp=x_prompt[i], invc0=invcB)
        in_maps.append(m)
    res = run_bass_kernel_spmd(nc, in_maps, core_ids=list(range(8)))
    r = res.results
    y_prompt = np.stack([np.concatenate([r[2 * b]["y_p"], r[2 * b + 1]["y_p"]], axis=0) for b in range(4)])
    y_sample = np.concatenate([r[c]["y_s"] for c in range(8)])[:, None, :]
    pool_prompt = np.stack([r[2 * b + 1]["pool_p"] for b in range(4)], axis=1)
    pool_sample = np.concatenate([r[c]["pool_s"] for c in range(8)], axis=1)
    conv_prompt = np.stack([r[2 * b + 1]["conv_p"] for b in range(4)], axis=1)
    conv_sample = np.concatenate([r[c]["conv_s"] for c in range(8)], axis=1)
    ret_prompt = np.stack([r[2 * b + 1]["ret_p"] for b in range(4)], axis=1)
    ret_sample = np.concatenate([r[c]["ret_s"] for c in range(8)], axis=1)
    sgu_prompt = np.stack([r[2 * b + 1]["sgu_p"] for b in range(4)], axis=1)
    sgu_sample = np.concatenate([r[c]["sgu_s"] for c in range(8)], axis=1)[:, :, None, :]
    outs = (y_prompt, y_sample, pool_prompt, pool_sample, conv_prompt, conv_sample, ret_prompt, ret_sample, sgu_prompt, sgu_sample)
    return tuple(np.ascontiguousarray(o, dtype=f32) for o in outs)
```
